# Optimizing a Trainium2 kernel written in Bass

```python
import jax, jax.numpy as jnp
from jax import lax
import numpy as np

D_MODEL = 2048
BATCH = 32
SEQ = 256
DEPTH = 4
DEC_BATCH = 4
DEC_SEQ = 4096
PAST_LEN = 512

GRID_W = 64
Q_BLOCK = 128
ROPE_THETA = 10000.0
NORM_EPS = 1e-6
N_MOD = 6
ATTN_HEADS = 4
ATTN_KV_HEADS = 2
HEAD_DIM = 128
LRU_WIDTH = 1024
LRU_BLOCKS = 8
LRU_BLOCK_W = LRU_WIDTH // LRU_BLOCKS
LRU_CONV_W = 4
LRU_C = 8.0
LRU_A_MIN = 0.9
LRU_A_MAX = 0.999
MLA_HEADS = 4
MLA_NOPE = 128
MLA_ROPE = 64
MLA_V = 128
MLA_KV_RANK = 512
ATTN_Q_COLS = ATTN_HEADS * HEAD_DIM
ATTN_KV_COLS = ATTN_KV_HEADS * HEAD_DIM
MLA_Q_COLS = MLA_HEADS * (MLA_NOPE + MLA_ROPE)
IN_SIZES = (ATTN_Q_COLS, ATTN_KV_COLS, ATTN_KV_COLS, LRU_WIDTH, LRU_WIDTH, MLA_Q_COLS, MLA_KV_RANK, MLA_ROPE)
D_IN = sum(IN_SIZES)
D_MIX = ATTN_HEADS * HEAD_DIM + LRU_WIDTH + MLA_HEADS * MLA_V
D_FF = 5632
FFN_CONV_W = 3

kernel_name = 'hybrid_gqa_rglru_mla_prefix_diffusion_step'


def rmsnorm(x, g):
    xf = x.astype(jnp.float32)
    xf = xf * lax.rsqrt(jnp.mean(xf * xf, axis=-1, keepdims=True) + NORM_EPS)
    return xf.astype(x.dtype) * g


def _rotate(x, pos):
    quarter = x.shape[-1] // 2
    inv_freq = ROPE_THETA ** (-jnp.arange(quarter, dtype=jnp.float32) / quarter)
    ang = pos.astype(jnp.float32)[:, None] * inv_freq[None, :]
    cos = jnp.cos(ang)[None, :, None, :].astype(x.dtype)
    sin = jnp.sin(ang)[None, :, None, :].astype(x.dtype)
    x1, x2 = x[..., :quarter], x[..., quarter:]
    return jnp.concatenate([x1 * cos - x2 * sin, x2 * cos + x1 * sin], axis=-1)


def axial_rope(x):
    n_tok = x.shape[1]
    rows = n_tok // GRID_W
    row = jnp.repeat(jnp.arange(rows, dtype=jnp.int32), GRID_W)
    col = jnp.tile(jnp.arange(GRID_W, dtype=jnp.int32), rows)
    half = x.shape[-1] // 2
    return jnp.concatenate([_rotate(x[..., :half], row), _rotate(x[..., half:], col)], axis=-1)


def attention(q, k, v):
    bsz, n_q, n_h, d_q = q.shape
    n_kv = k.shape[2]
    grp = n_h // n_kv
    d_v = v.shape[-1]
    scale = d_q ** -0.5
    n_blk = n_q // Q_BLOCK
    qb = q.reshape(bsz, n_blk, Q_BLOCK, n_kv, grp, d_q).transpose(1, 0, 2, 3, 4, 5)

    def one_block(q_blk):
        s = jnp.einsum('bqhgd,bkhd->bhgqk', q_blk, k, preferred_element_type=jnp.float32) * scale
        p = jax.nn.softmax(s, axis=-1).astype(v.dtype)
        return jnp.einsum('bhgqk,bkhd->bqhgd', p, v)

    o = lax.map(one_block, qb)
    return o.transpose(1, 0, 2, 3, 4, 5).reshape(bsz, n_q, n_h, d_v)


def dwconv(x, w, b, left):
    ksz = w.shape[0]
    n = x.shape[1]
    xp = jnp.pad(x, ((0, 0), (left, ksz - 1 - left), (0, 0)))
    y = xp[:, 0:n] * w[0]
    for j in range(1, ksz):
        y = y + xp[:, j:j + n] * w[j]
    return y + b


def _lin_combine(left, right):
    a_l, b_l = left
    a_r, b_r = right
    return a_l * a_r, a_r * b_l + b_r


def rglru(x, w_a, b_a, w_x, b_x, lam, h0, reverse):
    bsz, n, width = x.shape
    xb = x.reshape(bsz, n, LRU_BLOCKS, LRU_BLOCK_W)
    gate_r = jnp.einsum('bsnk,nkj->bsnj', xb, w_a).reshape(bsz, n, width) + b_a
    gate_i = jnp.einsum('bsnk,nkj->bsnj', xb, w_x).reshape(bsz, n, width) + b_x
    r = jax.nn.sigmoid(gate_r.astype(jnp.float32))
    i = jax.nn.sigmoid(gate_i.astype(jnp.float32))
    log_a = LRU_C * r * jax.nn.log_sigmoid(lam.astype(jnp.float32))
    a = jnp.exp(log_a)
    u = jnp.sqrt(-jnp.expm1(2.0 * log_a)) * (i * x.astype(jnp.float32))
    if reverse:
        a = jnp.flip(a, axis=1)
        u = jnp.flip(u, axis=1)
    u = u.at[:, 0].add(a[:, 0] * h0.astype(jnp.float32))
    _, h = lax.associative_scan(_lin_combine, (a, u), axis=1)
    if reverse:
        h = jnp.flip(h, axis=1)
    return h


def mixer(h, p, ctx):
    bsz, n, _ = h.shape
    offsets = np.cumsum(IN_SIZES)[:-1].tolist()
    proj = h @ p['w_in']
    q_a, k_a, v_a, x_r, g_r, q_m, ckv, k_r = jnp.split(proj, offsets, axis=-1)
    q_a = rmsnorm(q_a.reshape(bsz, n, ATTN_HEADS, HEAD_DIM), p['g_q'])
    k_a = rmsnorm(k_a.reshape(bsz, n, ATTN_KV_HEADS, HEAD_DIM), p['g_k'])
    v_a = v_a.reshape(bsz, n, ATTN_KV_HEADS, HEAD_DIM)
    q_m = q_m.reshape(bsz, n, MLA_HEADS, MLA_NOPE + MLA_ROPE)
    ckv = rmsnorm(ckv, p['g_kv'])
    x_r = dwconv(x_r, p['lru_conv_w'], p['lru_conv_b'], 1)
    if ctx is None:
        k_all, v_all, ckv_all, kr_all = k_a, v_a, ckv, k_r
        h0f = jnp.zeros((bsz, LRU_WIDTH), jnp.float32)
        h0b = jnp.zeros((bsz, LRU_WIDTH), jnp.float32)
    else:
        q_a = axial_rope(q_a)
        k_a = axial_rope(k_a)
        q_m = jnp.concatenate([q_m[..., :MLA_NOPE], axial_rope(q_m[..., MLA_NOPE:])], axis=-1)
        k_r = axial_rope(k_r[:, :, None, :])[:, :, 0, :]
        ctx_k, ctx_v, ctx_ckv, ctx_kr, ctx_state = ctx
        k_all = jnp.concatenate([ctx_k, k_a], axis=1)
        v_all = jnp.concatenate([ctx_v, v_a], axis=1)
        ckv_all = jnp.concatenate([ctx_ckv, ckv], axis=1)
        kr_all = jnp.concatenate([ctx_kr, k_r], axis=1)
        h0f = ctx_state[:, 0]
        h0b = ctx_state[:, 1]
    o_a = attention(q_a, k_all, v_all)
    hf = rglru(x_r, p['lru_w_a'][0], p['lru_b_a'][0], p['lru_w_x'][0], p['lru_b_x'][0], p['lru_lambda'][0], h0f, False)
    hb = rglru(x_r, p['lru_w_a'][1], p['lru_b_a'][1], p['lru_w_x'][1], p['lru_b_x'][1], p['lru_lambda'][1], h0b, True)
    y_r = (hf + hb).astype(h.dtype) * jax.nn.gelu(g_r)
    k_nope = jnp.einsum('bsr,rhd->bshd', ckv_all, p['w_uk'])
    v_m = jnp.einsum('bsr,rhd->bshd', ckv_all, p['w_uv'])
    k_m = jnp.concatenate([k_nope, jnp.broadcast_to(kr_all[:, :, None, :], k_nope.shape[:3] + (MLA_ROPE,))], axis=-1)
    o_m = attention(q_m, k_m, v_m)
    out = jnp.concatenate([o_a.reshape(bsz, n, -1), y_r, o_m.reshape(bsz, n, -1)], axis=-1) @ p['w_out']
    if ctx is None:
        final_state = jnp.stack([hf[:, -1], hb[:, 0]], axis=1).astype(h.dtype)
        return out, (k_a, v_a, ckv, k_r, final_state)
    return out, None


def conv_ffn(h, p):
    u = dwconv(h @ p['ffn_w_up'], p['ffn_conv_w'], p['ffn_conv_b'], 1)
    g, v = jnp.split(u, 2, axis=-1)
    return (jax.nn.silu(g) * v) @ p['ffn_w_down']


def layer(x, cond, p, ctx):
    mod = jax.nn.silu(cond) @ p['w_mod'] + p['b_mod']
    if mod.ndim == 2:
        mod = mod[:, None, :]
    sh_m, sc_m, gt_m, sh_f, sc_f, gt_f = jnp.split(mod, N_MOD, axis=-1)
    h = rmsnorm(x, p['g_pre_mix']) * (1.0 + sc_m) + sh_m
    m, st = mixer(h, p, ctx)
    x = x + gt_m * rmsnorm(m, p['g_post_mix'])
    h = rmsnorm(x, p['g_pre_ffn']) * (1.0 + sc_f) + sh_f
    x = x + gt_f * rmsnorm(conv_ffn(h, p), p['g_post_ffn'])
    return x, st


def setup_inputs(seed: int = 0) -> dict:
    key = jax.random.key(seed)
    keys = iter(jax.random.split(key, 48))
    f32 = jnp.float32

    def nrm(shape, scale):
        return scale * jax.random.normal(next(keys), shape, f32)

    def gain(shape):
        return 1.0 + nrm(shape, 0.05)

    a_init = jax.random.uniform(next(keys), (DEPTH, 2, LRU_WIDTH), f32, LRU_A_MIN, LRU_A_MAX)
    s_init = a_init ** (1.0 / LRU_C)
    lru_lambda = jnp.log(s_init) - jnp.log1p(-s_init)
    return {
        'x_prompt': nrm((BATCH, SEQ, D_MODEL), 1.0),
        'x_sample': nrm((DEC_BATCH, DEC_SEQ, D_MODEL), 1.0),
        'cache_attn_k': nrm((DEC_BATCH, DEPTH, PAST_LEN, ATTN_KV_HEADS, HEAD_DIM), 1.0),
        'cache_attn_v': nrm((DEC_BATCH, DEPTH, PAST_LEN, ATTN_KV_HEADS, HEAD_DIM), 1.0),
        'cache_mla_ckv': nrm((DEC_BATCH, DEPTH, PAST_LEN, MLA_KV_RANK), 1.0),
        'cache_mla_krope': nrm((DEC_BATCH, DEPTH, PAST_LEN, MLA_ROPE), 1.0),
        'state_lru': nrm((DEC_BATCH, DEPTH, 2, LRU_WIDTH), 0.5),
        'c': nrm((DEC_BATCH, D_MODEL), 1.0),
        'c_ctx': nrm((D_MODEL,), 1.0),
        'w_mod': nrm((DEPTH, D_MODEL, N_MOD * D_MODEL), 0.5 * D_MODEL ** -0.5),
        'b_mod': nrm((DEPTH, N_MOD * D_MODEL), 0.02),
        'g_pre_mix': gain((DEPTH, D_MODEL)),
        'g_post_mix': gain((DEPTH, D_MODEL)),
        'g_pre_ffn': gain((DEPTH, D_MODEL)),
        'g_post_ffn': gain((DEPTH, D_MODEL)),
        'w_in': nrm((DEPTH, D_MODEL, D_IN), D_MODEL ** -0.5),
        'g_q': gain((DEPTH, HEAD_DIM)),
        'g_k': gain((DEPTH, HEAD_DIM)),
        'lru_conv_w': nrm((DEPTH, LRU_CONV_W, LRU_WIDTH), LRU_CONV_W ** -0.5),
        'lru_conv_b': nrm((DEPTH, LRU_WIDTH), 0.02),
        'lru_w_a': nrm((DEPTH, 2, LRU_BLOCKS, LRU_BLOCK_W, LRU_BLOCK_W), LRU_BLOCK_W ** -0.5),
        'lru_b_a': nrm((DEPTH, 2, LRU_WIDTH), 0.02),
        'lru_w_x': nrm((DEPTH, 2, LRU_BLOCKS, LRU_BLOCK_W, LRU_BLOCK_W), LRU_BLOCK_W ** -0.5),
        'lru_b_x': nrm((DEPTH, 2, LRU_WIDTH), 0.02),
        'lru_lambda': lru_lambda,
        'g_kv': gain((DEPTH, MLA_KV_RANK)),
        'w_uk': nrm((DEPTH, MLA_KV_RANK, MLA_HEADS, MLA_NOPE), MLA_KV_RANK ** -0.5),
        'w_uv': nrm((DEPTH, MLA_KV_RANK, MLA_HEADS, MLA_V), MLA_KV_RANK ** -0.5),
        'w_out': nrm((DEPTH, D_MIX, D_MODEL), D_MIX ** -0.5),
        'ffn_w_up': nrm((DEPTH, D_MODEL, 2 * D_FF), D_MODEL ** -0.5),
        'ffn_conv_w': nrm((DEPTH, FFN_CONV_W, 2 * D_FF), FFN_CONV_W ** -0.5),
        'ffn_conv_b': nrm((DEPTH, 2 * D_FF), 0.02),
        'ffn_w_down': nrm((DEPTH, D_FF, D_MODEL), D_FF ** -0.5),
    }


def reference(x_prompt, x_sample, cache_attn_k, cache_attn_v, cache_mla_ckv, cache_mla_krope, state_lru,
              c, c_ctx, w_mod, b_mod, g_pre_mix, g_post_mix, g_pre_ffn, g_post_ffn, w_in, g_q, g_k,
              lru_conv_w, lru_conv_b, lru_w_a, lru_b_a, lru_w_x, lru_b_x, lru_lambda, g_kv, w_uk, w_uv,
              w_out, ffn_w_up, ffn_conv_w, ffn_conv_b, ffn_w_down):
    def layer_params(l):
        return {
            'w_mod': w_mod[l], 'b_mod': b_mod[l],
            'g_pre_mix': g_pre_mix[l], 'g_post_mix': g_post_mix[l],
            'g_pre_ffn': g_pre_ffn[l], 'g_post_ffn': g_post_ffn[l],
            'w_in': w_in[l], 'g_q': g_q[l], 'g_k': g_k[l],
            'lru_conv_w': lru_conv_w[l], 'lru_conv_b': lru_conv_b[l],
            'lru_w_a': lru_w_a[l], 'lru_b_a': lru_b_a[l],
            'lru_w_x': lru_w_x[l], 'lru_b_x': lru_b_x[l], 'lru_lambda': lru_lambda[l],
            'g_kv': g_kv[l], 'w_uk': w_uk[l], 'w_uv': w_uv[l], 'w_out': w_out[l],
            'ffn_w_up': ffn_w_up[l], 'ffn_conv_w': ffn_conv_w[l], 'ffn_conv_b': ffn_conv_b[l],
            'ffn_w_down': ffn_w_down[l],
        }

    xp = x_prompt
    ks, vs, ckvs, krs, sts = [], [], [], [], []
    for l in range(DEPTH):
        xp, (k_l, v_l, ckv_l, kr_l, st_l) = layer(xp, c_ctx, layer_params(l), None)
        ks.append(k_l)
        vs.append(v_l)
        ckvs.append(ckv_l)
        krs.append(kr_l)
        sts.append(st_l)

    xs = x_sample
    for l in range(DEPTH):
        ctx = (cache_attn_k[:, l], cache_attn_v[:, l], cache_mla_ckv[:, l], cache_mla_krope[:, l], state_lru[:, l])
        xs, _ = layer(xs, c, layer_params(l), ctx)

    new_attn_k = jnp.stack(ks, axis=1)
    new_attn_v = jnp.stack(vs, axis=1)
    new_mla_ckv = jnp.stack(ckvs, axis=1)
    new_mla_krope = jnp.stack(krs, axis=1)
    new_state_lru = jnp.stack(sts, axis=1)
    return (xp, xs, new_attn_k, new_attn_v, new_mla_ckv, new_mla_krope, new_state_lru)
```

```python
import numpy as np
import concourse.bass as bass
import concourse.mybir as mybir
from concourse.bass_utils import run_bass_kernel_spmd
from contextlib import ExitStack

F32 = mybir.dt.float32
BF16 = mybir.dt.bfloat16
AF = mybir.ActivationFunctionType
ALU = mybir.AluOpType

L = 4
D = 2048
KC = 16
DIN = 4416
DFF = 5632
NFC = 44
EPS = 1e-6
NVL = 622
PAST = 512
SP_T = 1024
SS_T = 4096
VO = dict(bmod=0, gpm=96, gpom=112, gpf=128, gpof=144, gq=160, gk=161, lcw=162, lcb=194, lba=202,
          lbx=218, llam=234, gkv=250, fcw=254, fcb=518)

KDMA = 6


class Buf:
    __slots__ = ("w", "r", "prev")

    def __init__(self):
        self.w = {}
        self.r = {}
        self.prev = {}


def _merge(dst, src):
    for k, v in src.items():
        if dst.get(k, 0) < v:
            dst[k] = v


class Sched:
    ENG = ("pe", "act", "dve", "pool", "sp")

    def __init__(self, nc):
        self.nc = nc
        self.q = {e: [] for e in self.ENG}
        self.cnt = {e: 0 for e in self.ENG}
        self.waited = {e: {} for e in self.ENG}
        self.dma_n = {e: 0 for e in self.ENG}
        self.semkeys = set()
        self.all_events = {}
        self.bar = {e: {} for e in self.ENG}

    def barrier(self):
        snap = dict(self.all_events)
        for e in self.ENG:
            _merge(self.bar[e], snap)

    def _deps(self, eng, reads, writes, pwrites):
        deps = {}
        if self.bar[eng]:
            _merge(deps, self.bar[eng])
            self.bar[eng] = {}
        for b in reads:
            _merge(deps, b.w)
        for b in writes:
            _merge(deps, b.w)
            _merge(deps, b.r)
            _merge(deps, b.prev)
        for b in pwrites:
            if b.r:
                newprev = {}
                _merge(newprev, b.w)
                _merge(newprev, b.r)
                b.prev = newprev
                b.w = {}
                b.r = {}
            _merge(deps, b.prev)
        return deps

    def _commit(self, ev, reads, writes, pwrites):
        k, v = ev
        for b in reads:
            if b.r.get(k, 0) < v:
                b.r[k] = v
        for b in writes:
            b.w = {k: v}
            b.r = {}
            b.prev = {}
        for b in pwrites:
            if b.w.get(k, 0) < v:
                b.w[k] = v
        if self.all_events.get(k, 0) < v:
            self.all_events[k] = v

    def _waits(self, eng, deps, skip_self=False):
        out = []
        wd = self.waited[eng]
        for k, v in deps.items():
            if skip_self and k == ("eng", eng):
                continue
            if wd.get(k, 0) < v:
                wd[k] = v
                out.append((k, v))
        return out

    def op(self, eng, fn, reads=(), writes=(), pwrites=()):
        deps = self._deps(eng, reads, writes, pwrites)
        waits = self._waits(eng, deps, skip_self=(eng == "pe"))
        self.cnt[eng] += 1
        k = ("eng", eng)
        self.semkeys.add(k)
        ev = (k, self.cnt[eng])
        self.q[eng].append((waits, fn, k, 1))
        self._commit(ev, reads, writes, pwrites)

    def dma(self, queue, out, in_, reads=(), writes=(), pwrites=(), slow=False):
        deps = self._deps(queue, reads, writes, pwrites)
        n = self.dma_n[queue]
        self.dma_n[queue] += 1
        slot, rnd = n % KDMA, n // KDMA
        k = ("dma", queue, slot)
        self.semkeys.add(k)
        if rnd > 0:
            _merge(deps, {k: 16 * rnd})
        waits = self._waits(queue, deps)
        ev = (k, 16 * (rnd + 1))
        fn = (lambda e, out=out, in_=in_: e.dma_start(out=out, in_=in_, allow_slow_non_contiguous=True)) if slow else (lambda e, out=out, in_=in_: e.dma_start(out=out, in_=in_))
        self.q[queue].append((waits, fn, k, 16))
        self._commit(ev, reads, writes, pwrites)

    def emit(self):
        nc = self.nc
        with ExitStack() as es:
            sems = {}
            for k in sorted(self.semkeys, key=str):
                sems[k] = es.enter_context(nc.semaphore("s_" + "_".join(str(x) for x in k)))
            block = es.enter_context(nc.Block())
            finw = self._waits("sp", dict(self.all_events))

            def mk(e):
                def body(eng):
                    for waits, fn, k, inc in self.q[e]:
                        for (wk, wv) in waits:
                            eng.wait_ge(sems[wk], wv)
                        fn(eng).then_inc(sems[k], inc)
                    if e == "sp":
                        for (wk, wv) in finw:
                            eng.wait_ge(sems[wk], wv)
                return body

            block.tensor(mk("pe"))
            block.scalar(mk("act"))
            block.vector(mk("dve"))
            block.gpsimd(mk("pool"))
            block.sync(mk("sp"))


class Tile:
    __slots__ = ("t", "b")

    def __init__(self, t):
        self.t = t
        self.b = Buf()


def build_program():
    nc = bass.Bass("TRN2", target_bir_lowering=False)
    S = Sched(nc)

    def din(name, shape, dt=F32):
        return nc.dram_tensor(name, list(shape), dt, kind="ExternalInput").ap()

    def dout(name, shape, dt=F32):
        return nc.dram_tensor(name, list(shape), dt, kind="ExternalOutput").ap()

    def dscr(name, shape, dt):
        return nc.dram_tensor(name, list(shape), dt).ap()

    xp_in = din("xp", [D, SP_T])
    xs_in = din("xs", [D, SS_T])
    ck_in = din("ck", [L, 2, 128, PAST])
    cv_in = din("cv", [L, PAST, 256])
    cckv_in = din("cckv", [L, 512, PAST])
    ckr_in = din("ckr", [L, 64, PAST])
    st_in = din("st", [128, L * 2 * 8])
    cond_in = din("cond", [128, KC * 2])
    vecs_in = din("vecs", [128, L * NVL])
    rope128_in = din("rope128", [128, 2, SS_T])
    rope64_in = din("rope64", [64, 2, SS_T])
    rmat_in = din("rmat", [128, 256])
    w_mod = din("w_mod", [L, D, 6 * D])
    w_in = din("w_in", [L, D, DIN])
    lru_w_a = din("lru_w_a", [L, 2, 8, 128, 128])
    lru_w_x = din("lru_w_x", [L, 2, 8, 128, 128])
    w_uk = din("w_uk", [L, 512, 512])
    w_uv = din("w_uv", [L, 512, 512])
    w_out = din("w_out", [L, D, D])
    w_up = din("ffn_w_up", [L, D, 2 * DFF])
    w_down = din("ffn_w_down", [L, DFF, D])
    yp_out = dout("yp", [D, SP_T])
    ys_out = dout("ys", [D, SS_T])
    nk_out = dout("nk", [L, 2, 128, SP_T])
    nv_out = dout("nv", [L, SP_T, 256])
    nckv_out = dout("nckv", [L, 512, SP_T])
    nkr_out = dout("nkr", [L, 64, SP_T])
    nst_out = dout("nst", [128, L * 2 * 8 * 4])

    SB_TOTAL = 229312
    arena = {"off": 16640, "n": 0}

    def alloc(shape, dt):
        nbytes = int(np.prod(shape[1:])) * (4 if dt == F32 else 2)
        nbytes = (nbytes + 63) // 64 * 64
        off = arena["off"]
        assert off + nbytes <= SB_TOTAL, ("SBUF overflow", off, nbytes)
        arena["off"] = off + nbytes
        arena["n"] += 1
        return Tile(nc.alloc_sbuf_tensor_at("t%d" % arena["n"], list(shape), dt, offset=off))

    def mark():
        return arena["off"]

    def release(m):
        S.barrier()
        arena["off"] = m

    psum = [Tile(nc.alloc_psum_tensor("ps%d" % i, [128, 512], F32)) for i in range(8)]

    def act(out, in_, func, reads, writes=(), pwrites=(), bias=0.0, scale=1.0):
        S.op("act", lambda e: e.activation(out=out, in_=in_, func=func, bias=bias, scale=scale),
             reads=reads, writes=writes, pwrites=pwrites)

    def tt(eng, out, in0, in1, op, reads, writes=(), pwrites=()):
        S.op(eng, lambda e: e.tensor_tensor(out=out, in0=in0, in1=in1, op=op), reads=reads, writes=writes, pwrites=pwrites)

    def ts(eng, out, in0, s1, op0, reads, writes=(), pwrites=(), s2=None, op1=None):
        if op1 is None:
            S.op(eng, lambda e: e.tensor_scalar(out=out, in0=in0, scalar1=s1, scalar2=None, op0=op0),
                 reads=reads, writes=writes, pwrites=pwrites)
        else:
            S.op(eng, lambda e: e.tensor_scalar(out=out, in0=in0, scalar1=s1, scalar2=s2, op0=op0, op1=op1),
                 reads=reads, writes=writes, pwrites=pwrites)

    def stt(eng, out, in0, scalar, in1, op0, op1, reads, writes=(), pwrites=()):
        S.op(eng, lambda e: e.scalar_tensor_tensor(out=out, in0=in0, scalar=scalar, in1=in1, op0=op0, op1=op1),
             reads=reads, writes=writes, pwrites=pwrites)

    def cp(eng, out, in_, reads, writes=(), pwrites=()):
        S.op(eng, lambda e: e.tensor_copy(out=out, in_=in_), reads=reads, writes=writes, pwrites=pwrites)

    def mm(out, lhsT, rhs, start, stop, reads, writes=(), pwrites=()):
        S.op("pe", lambda e: e.matmul(out, lhsT=lhsT, rhs=rhs, start=start, stop=stop),
             reads=reads, writes=writes, pwrites=pwrites)

    def mmgroup(ps, parts, reads):
        out = parts[0][2]
        n = len(parts)
        for i, (l_, r_, o_) in enumerate(parts):
            if i == 0:
                mm(o_, l_, r_, True, n == 1, reads, writes=[ps.b])
            else:
                mm(o_, l_, r_, False, i == n - 1, reads, pwrites=[ps.b])

    def rsqrt_from(out_t, ssq_ap, ssq_reads, n, writes_b):
        act(out_t, ssq_ap, AF.Sqrt, ssq_reads, writes=[writes_b], bias=EPS, scale=1.0 / n)
        S.op("dve", lambda e: e.reciprocal(out=out_t, in_=out_t), reads=[writes_b], writes=[writes_b])

    vecs = alloc([128, L * NVL], F32)
    modT = alloc([128, L * 96 * 2], F32)
    dv = alloc([128, L * 2 * 6 * 16], F32)
    clam = alloc([128, L * 16], F32)
    clam2 = alloc([128, L * 16], F32)
    st_sb = alloc([128, L * 16], F32)
    ones_bf = alloc([128, 128], BF16)
    rmat = alloc([128, 256], F32)
    nst_sb = alloc([128, L * 2 * 8 * 4], F32)
    epsb = alloc([128, 1], F32)

    def V(l, name, i=0, n=1):
        o = l * NVL + VO[name] + i
        return vecs.t[:, o:o + n]

    def DV(l, ci, which, kc):
        o = ((l * 2 + ci) * 6 + which) * 16 + kc
        return dv.t[:, o:o + 1]

    S.dma("sp", vecs.t[:, :], vecs_in, writes=[vecs.b])
    S.dma("sp", rmat.t[:, :], rmat_in, writes=[rmat.b])
    S.dma("sp", st_sb.t[:, :], st_in, writes=[st_sb.b])
    S.op("pool", lambda e: e.memset(ones_bf.t[:, :], 1.0), writes=[ones_bf.b])
    S.op("pool", lambda e: e.memset(nst_sb.t[:, :], 0.0), writes=[nst_sb.b])
    S.barrier()

    class PanelStream:
        def __init__(self, nk, ncols_max, kstage):
            self.nk, self.ncm, self.kst = nk, ncols_max, kstage
            self.stg = [alloc([128, kstage, ncols_max], F32) for _ in range(3)]
            self.pan = [alloc([128, nk, ncols_max], BF16) for _ in range(2)]
            self.si = 0
            self.pi = 0
            self.ci = 0

        def stream(self, specs):
            nxt = self.load(*specs[0])
            for i in range(len(specs)):
                cur = nxt
                if i + 1 < len(specs):
                    nxt = self.load(*specs[i + 1])
                yield i, cur

        def load(self, src2d, ncols):
            pan = self.pan[self.pi % 2]
            self.pi += 1
            k0 = 0
            first = True
            while k0 < self.nk:
                kn = min(self.kst, self.nk - k0)
                stg = self.stg[self.si % 3]
                self.si += 1
                S.dma("sp", stg.t[:, 0:kn, 0:ncols],
                      src2d[k0 * 128:(k0 + kn) * 128, :].rearrange("(kc p) n -> p kc n", p=128),
                      writes=[stg.b])
                kw = {"writes": [pan.b]} if first else {"pwrites": [pan.b]}
                first = False
                if self.ci % 2 == 0:
                    cp("dve", pan.t[:, k0:k0 + kn, 0:ncols], stg.t[:, 0:kn, 0:ncols], [stg.b], **kw)
                else:
                    act(pan.t[:, k0:k0 + kn, 0:ncols], stg.t[:, 0:kn, 0:ncols], AF.Copy, [stg.b], **kw)
                self.ci += 1
                k0 += kn
            return pan

    m0 = mark()
    cond_sb = alloc([128, KC * 2], F32)
    cond_bf = alloc([128, KC, 2], BF16)
    S.dma("sp", cond_sb.t[:, :], cond_in, writes=[cond_sb.b])
    act(cond_bf.t[:, :, :], cond_sb.t[:, :].rearrange("p (k c) -> p k c", c=2), AF.Silu, [cond_sb.b], writes=[cond_bf.b])
    ps_w = PanelStream(KC, 512, 4)
    p0_specs = [(w_mod[l, :, pnl * 512:(pnl + 1) * 512], 512) for l in range(L) for pnl in range(24)]
    p0_iter = ps_w.stream(p0_specs)
    for l in range(L):
        pst = psum[l % 2]
        for pnl in range(24):
            _, pan = next(p0_iter)
            for c4 in range(4):
                j = pnl * 4 + c4
                parts = [(pan.t[:, kc, c4 * 128:(c4 + 1) * 128], cond_bf.t[:, kc, :], pst.t[:, 2 * j:2 * j + 2]) for kc in range(KC)]
                if j == 0:
                    mmgroup(pst, parts, [pan.b, cond_bf.b])
                else:
                    for i, (l_, r_, o_) in enumerate(parts):
                        mm(o_, l_, r_, i == 0, i == KC - 1, [pan.b, cond_bf.b], pwrites=[pst.b])
        tt("dve", modT.t[:, l * 192:(l + 1) * 192].rearrange("p (j c) -> p j c", c=2),
           pst.t[:, 0:192].rearrange("p (j c) -> p j c", c=2),
           V(l, "bmod", 0, 96).unsqueeze(2).to_broadcast([128, 96, 2]), ALU.add,
           [pst.b, vecs.b], pwrites=[modT.b])
    for l in range(L):
        for ci in range(2):
            def modcol(k6):
                return modT.t[:, l * 192:(l + 1) * 192].rearrange("p (j c) -> p j c", c=2)[:, k6 * 16:(k6 + 1) * 16, ci]

            def dvc(which):
                o = ((l * 2 + ci) * 6 + which) * 16
                return dv.t[:, o:o + 16]
            stt("dve", dvc(0), modcol(1), 1.0, V(l, "gpm", 0, 16), ALU.add, ALU.mult, [modT.b, vecs.b], pwrites=[dv.b])
            cp("dve", dvc(1), modcol(0), [modT.b], pwrites=[dv.b])
            tt("dve", dvc(2), modcol(2), V(l, "gpom", 0, 16), ALU.mult, [modT.b, vecs.b], pwrites=[dv.b])
            stt("dve", dvc(3), modcol(4), 1.0, V(l, "gpf", 0, 16), ALU.add, ALU.mult, [modT.b, vecs.b], pwrites=[dv.b])
            cp("dve", dvc(4), modcol(3), [modT.b], pwrites=[dv.b])
            tt("dve", dvc(5), modcol(5), V(l, "gpof", 0, 16), ALU.mult, [modT.b, vecs.b], pwrites=[dv.b])
    for l in range(L):
        e_ = clam2.t[:, l * 16:(l + 1) * 16]
        c_ = clam.t[:, l * 16:(l + 1) * 16]
        act(e_, V(l, "llam", 0, 16), AF.Exp, [vecs.b], pwrites=[clam2.b], scale=-1.0)
        ts("dve", c_, e_, -0.25, ALU.mult, [clam2.b], pwrites=[clam.b], s2=1.0 / 3.0, op1=ALU.add)
        tt("dve", c_, c_, e_, ALU.mult, [clam.b, clam2.b], writes=[clam.b])
        ts("dve", c_, c_, -0.5, ALU.add, [clam.b], writes=[clam.b])
        tt("dve", c_, c_, e_, ALU.mult, [clam.b, clam2.b], writes=[clam.b])
        ts("dve", c_, c_, 1.0, ALU.add, [clam.b], writes=[clam.b])
        tt("dve", c_, c_, e_, ALU.mult, [clam.b, clam2.b], writes=[clam.b])
        ts("dve", c_, c_, -8.0, ALU.mult, [clam.b], writes=[clam.b])
        ts("dve", e_, c_, 2.0, ALU.mult, [clam.b], writes=[clam2.b])
    release(m0)

    class Group:
        pass

    def mkgroup(name, ci, T, nseq, Sq, cache, x_in, y_out):
        g = Group()
        g.name, g.ci, g.T, g.nseq, g.S, g.cache = name, ci, T, nseq, Sq, cache
        g.Tk = T + (cache if nseq == 1 else 0)
        g.x_in, g.y_out = x_in, y_out
        g.xT = dscr(name + "_xT", [D, T], F32)
        g.H = dscr(name + "_H", [D, T], BF16)
        g.QA = dscr(name + "_QA", [4, 128, T], BF16)
        g.KA = dscr(name + "_KA", [2, 128, g.Tk], BF16)
        g.VA = dscr(name + "_VA", [g.Tk, 256], BF16)
        g.QN = dscr(name + "_QN", [4, 128, T], BF16)
        g.QR = dscr(name + "_QR", [4, 64, T], BF16)
        g.KN = dscr(name + "_KN", [4, 128, g.Tk], BF16)
        g.KR = dscr(name + "_KR", [64, g.Tk], BF16)
        g.VM = dscr(name + "_VM", [g.Tk, 512], BF16)
        g.CKV = dscr(name + "_CKV", [512, T], F32)
        g.XR = dscr(name + "_XR", [1024, T], F32)
        g.GR = dscr(name + "_GR", [1024, T], F32)
        g.MIX = dscr(name + "_MIX", [D, T], BF16)
        g.M = dscr(name + "_M", [D, T], F32)
        g.ACT = dscr(name + "_ACT", [DFF, T], BF16)
        for nm in ("xT", "H", "QA", "KA", "VA", "QN", "QR", "KN", "KR", "VM", "CKV", "XR", "GR", "MIX", "M", "ACT"):
            setattr(g, "b_" + nm, Buf())
        g.blocks = [(s, min(2048, T - s)) for s in range(0, T, 2048)]
        return g

    GP = mkgroup("p", 0, SP_T, 4, 256, 0, xp_in, yp_out)
    GS = mkgroup("s", 1, SS_T, 1, SS_T, PAST, xs_in, ys_out)

    rr = {"acc": 0, "aux": 0}

    def acc_ps():
        rr["acc"] += 1
        return psum[rr["acc"] % 4]

    def aux_ps():
        rr["aux"] += 1
        return psum[4 + rr["aux"] % 4]

    def prenorm_tile(g, l, which_a, xt, c0, tmp, sq, rstd, hstage):
        act(sq.t[:, :, :], xt.t[:, :, :], AF.Square, [xt.b], writes=[sq.b])
        pz = aux_ps()
        mmgroup(pz, [(ones_bf.t[:, :], sq.t[:, kc, :], pz.t[:, :]) for kc in range(KC)], [ones_bf.b, sq.b])
        rsqrt_from(rstd.t[:, :], pz.t[:, :], [pz.b], D, rstd.b)
        for kc in range(KC):
            tm = tmp[kc % len(tmp)]
            stt("dve", tm.t[:, :], xt.t[:, kc, :], DV(l, g.ci, which_a, kc), rstd.t[:, :], ALU.mult, ALU.mult,
                [xt.b, rstd.b, dv.b], writes=[tm.b])
            if kc == 0:
                act(hstage.t[:, kc, :], tm.t[:, :], AF.Identity, [tm.b, dv.b], writes=[hstage.b],
                    bias=DV(l, g.ci, which_a + 1, kc))
            else:
                act(hstage.t[:, kc, :], tm.t[:, :], AF.Identity, [tm.b, dv.b], pwrites=[hstage.b],
                    bias=DV(l, g.ci, which_a + 1, kc))
        S.dma("act", g.H[:, c0:c0 + 512].rearrange("(kc p) t -> p kc t", p=128), hstage.t[:, :, :],
              reads=[hstage.b], pwrites=[g.b_H])

    def phase_P1(g, l):
        m = mark()
        xts = [alloc([128, KC, 512], F32) for _ in range(2)]
        tmp = [alloc([128, 512], F32) for _ in range(3)]
        sq = alloc([128, KC, 512], BF16)
        rstd = alloc([128, 512], F32)
        hst = [alloc([128, KC, 512], BF16) for _ in range(2)]
        for ti in range(g.T // 512):
            xt = xts[ti % 2]
            S.dma("sp", xt.t[:, :, :], g.x_in[:, ti * 512:(ti + 1) * 512].rearrange("(kc p) t -> p kc t", p=128), writes=[xt.b])
            S.dma("act", g.xT[:, ti * 512:(ti + 1) * 512].rearrange("(kc p) t -> p kc t", p=128), xt.t[:, :, :],
                  reads=[xt.b], pwrites=[g.b_xT])
            prenorm_tile(g, l, 0, xt, ti * 512, tmp, sq, rstd, hst[ti % 2])
        release(m)

    P2_PANELS = [
        (0, 512, [(0, 128, "qa", 0), (128, 128, "qa", 1), (256, 128, "qa", 2), (384, 128, "qa", 3)]),
        (512, 512, [(0, 128, "ka", 0), (128, 128, "ka", 1), (256, 256, "va", 0)]),
        (1024, 512, [(i * 128, 128, "xr", i) for i in range(4)]),
        (1536, 512, [(i * 128, 128, "xr", 4 + i) for i in range(4)]),
        (2048, 512, [(i * 128, 128, "gr", i) for i in range(4)]),
        (2560, 512, [(i * 128, 128, "gr", 4 + i) for i in range(4)]),
        (3072, 512, [(0, 128, "qn", 0), (128, 64, "qr", 0), (192, 128, "qn", 1), (320, 64, "qr", 1), (384, 128, "qn", 2)]),
        (3584, 512, [(0, 64, "qr", 2), (64, 128, "qn", 3), (192, 64, "qr", 3), (256, 128, "ckv", 0), (384, 128, "ckv", 1)]),
        (4096, 320, [(0, 128, "ckv", 2), (128, 128, "ckv", 3), (256, 64, "kr", 0)]),
    ]

    def phase_P2(g, l, blk):
        b0, Tb = blk
        m = mark()
        hT = alloc([128, KC, Tb], BF16)
        ps_w = PanelStream(KC, 512, 4)
        hTb = [Buf() for _ in range(Tb // 512)]
        ev32 = [alloc([128, 512], F32) for _ in range(3)]
        evbf = [alloc([128, 512], BF16) for _ in range(3)]
        sqb = [alloc([128, 512], BF16) for _ in range(2)]
        rstd = [alloc([128, 512], F32) for _ in range(2)]
        rp = [alloc([128, 2, 512], F32) for _ in range(2)]
        t1 = [alloc([128, 512], F32) for _ in range(2)]
        cnt = {"e": 0, "s": 0, "r": 0}
        sample = g.nseq == 1
        koff = g.cache if sample else 0

        def rope(src32, width, c0, outbf):
            r_ = rp[cnt["r"] % 2]
            t_ = t1[cnt["r"] % 2]
            cnt["r"] += 1
            tab = rope128_in if width == 128 else rope64_in
            S.dma("sp", r_.t[0:width, :, :], tab[:, :, c0:c0 + 512], writes=[r_.b])
            pr = aux_ps()
            rm = rmat.t[:, 0:128] if width == 128 else rmat.t[0:64, 128:192]
            mmgroup(pr, [(rm, src32.t[0:width, :], pr.t[0:width, :])], [rmat.b, src32.b])
            tt("dve", t_.t[0:width, :], pr.t[0:width, :], r_.t[0:width, 1, :], ALU.mult, [pr.b, r_.b], writes=[t_.b])
            tt("pool", src32.t[0:width, :], src32.t[0:width, :], r_.t[0:width, 0, :], ALU.mult, [src32.b, r_.b], writes=[src32.b])
            tt("dve", outbf.t[0:width, :], src32.t[0:width, :], t_.t[0:width, :], ALU.add, [src32.b, t_.b], writes=[outbf.b])

        pend = {"f": None}

        def evac(ps, e32, eb, kind, idx, c0):
            if kind in ("xr", "gr", "ckv"):
                act(e32.t[:, :], ps.t[:, :], AF.Copy, [ps.b], writes=[e32.b])
                dst, bb = {"xr": (g.XR, g.b_XR), "gr": (g.GR, g.b_GR), "ckv": (g.CKV, g.b_CKV)}[kind]
                S.dma("act", dst[idx * 128:(idx + 1) * 128, c0:c0 + 512], e32.t[:, :], reads=[e32.b], pwrites=[bb])
            elif kind in ("qa", "ka"):
                sq_ = sqb[cnt["s"] % 2]
                rs_ = rstd[cnt["s"] % 2]
                cnt["s"] += 1
                act(sq_.t[:, :], ps.t[:, :], AF.Square, [ps.b], writes=[sq_.b])
                pz = aux_ps()
                mmgroup(pz, [(ones_bf.t[:, :], sq_.t[:, :], pz.t[:, :])], [ones_bf.b, sq_.b])
                rsqrt_from(rs_.t[:, :], pz.t[:, :], [pz.b], 128, rs_.b)
                gname = "gq" if kind == "qa" else "gk"
                stt("dve", e32.t[:, :], ps.t[:, :], V(l, gname), rs_.t[:, :], ALU.mult, ALU.mult,
                    [ps.b, rs_.b, vecs.b], writes=[e32.b])
                if kind == "ka" and not sample:
                    S.dma("act", nk_out[l, idx, :, c0:c0 + 512], e32.t[:, :], reads=[e32.b])
                if sample:
                    rope(e32, 128, c0, eb)
                else:
                    cp("pool", eb.t[:, :], e32.t[:, :], [e32.b], writes=[eb.b])
                if kind == "qa":
                    S.dma("act", g.QA[idx, :, c0:c0 + 512], eb.t[:, :], reads=[eb.b], pwrites=[g.b_QA])
                else:
                    S.dma("act", g.KA[idx, :, koff + c0:koff + c0 + 512], eb.t[:, :], reads=[eb.b], pwrites=[g.b_KA])
            elif kind == "qn":
                act(eb.t[:, :], ps.t[:, :], AF.Copy, [ps.b], writes=[eb.b])
                S.dma("act", g.QN[idx, :, c0:c0 + 512], eb.t[:, :], reads=[eb.b], pwrites=[g.b_QN])
            elif kind in ("qr", "kr"):
                if sample:
                    act(e32.t[0:64, :], ps.t[0:64, :], AF.Copy, [ps.b], writes=[e32.b])
                    rope(e32, 64, c0, eb)
                else:
                    if kind == "kr":
                        act(e32.t[0:64, :], ps.t[0:64, :], AF.Copy, [ps.b], writes=[e32.b])
                        S.dma("act", nkr_out[l, :, c0:c0 + 512], e32.t[0:64, :], reads=[e32.b])
                        cp("dve", eb.t[0:64, :], e32.t[0:64, :], [e32.b], writes=[eb.b])
                    else:
                        act(eb.t[0:64, :], ps.t[0:64, :], AF.Copy, [ps.b], writes=[eb.b])
                if kind == "qr":
                    S.dma("act", g.QR[idx, :, c0:c0 + 512], eb.t[0:64, :], reads=[eb.b], pwrites=[g.b_QR])
                else:
                    S.dma("act", g.KR[:, koff + c0:koff + c0 + 512], eb.t[0:64, :], reads=[eb.b], pwrites=[g.b_KR])


        for pi_, pan in ps_w.stream([(w_in[l, :, c0_:c0_ + nc_], nc_) for (c0_, nc_, _) in P2_PANELS]):
            (col0, ncols, chunks) = P2_PANELS[pi_]
            if pi_ == 0:
                for ti in range(Tb // 512):
                    S.dma("sp", hT.t[:, :, ti * 512:(ti + 1) * 512],
                          g.H[:, b0 + ti * 512:b0 + (ti + 1) * 512].rearrange("(kc p) t -> p kc t", p=128),
                          reads=[g.b_H], writes=[hTb[ti]])
            for ti in range(Tb // 512):
                c0 = b0 + ti * 512
                for (off, width, kind, idx) in chunks:
                    if kind == "va":
                        if pend["f"] is not None:
                            pend["f"]()
                            pend["f"] = None
                        for j in range(4):
                            ps = acc_ps()
                            mmgroup(ps, [(hT.t[:, kc, ti * 512 + j * 128: ti * 512 + (j + 1) * 128], pan.t[:, kc, off:off + 256], ps.t[:, 0:256])
                                         for kc in range(KC)], [hTb[ti], pan.b])
                            eb = evbf[cnt["e"] % 3]
                            e32 = ev32[cnt["e"] % 3]
                            cnt["e"] += 1
                            r0 = c0 + j * 128
                            if not sample:
                                act(e32.t[:, 0:256], ps.t[:, 0:256], AF.Copy, [ps.b], writes=[e32.b])
                                S.dma("act", nv_out[l, r0:r0 + 128, :], e32.t[:, 0:256], reads=[e32.b])
                                cp("dve", eb.t[:, 0:256], e32.t[:, 0:256], [e32.b], writes=[eb.b])
                            else:
                                act(eb.t[:, 0:256], ps.t[:, 0:256], AF.Copy, [ps.b], writes=[eb.b])
                            S.dma("act", g.VA[koff + r0:koff + r0 + 128, :], eb.t[:, 0:256], reads=[eb.b], pwrites=[g.b_VA])
                        continue
                    ps = acc_ps()
                    mmgroup(ps, [(pan.t[:, kc, off:off + width], hT.t[:, kc, ti * 512:(ti + 1) * 512], ps.t[0:width, :])
                                 for kc in range(KC)], [hTb[ti], pan.b])
                    e32 = ev32[cnt["e"] % 3]
                    eb = evbf[cnt["e"] % 3]
                    cnt["e"] += 1
                    fn_ = (lambda ps=ps, e32=e32, eb=eb, kind=kind, idx=idx, c0=c0: evac(ps, e32, eb, kind, idx, c0))
                    if pend["f"] is not None:
                        pend["f"]()
                    pend["f"] = fn_
        if pend["f"] is not None:
            pend["f"]()
            pend["f"] = None
        release(m)


    def phase_P2C(g, l):
        m = mark()
        a32 = alloc([128, 2, 512], F32)
        abf = alloc([128, 2, 512], BF16)
        S.dma("sp", a32.t[:, :, :], ck_in[l].rearrange("h d t -> d h t"), writes=[a32.b])
        cp("dve", abf.t[:, :, :], a32.t[:, :, :], [a32.b], writes=[abf.b])
        S.dma("act", g.KA[:, :, 0:PAST].rearrange("h d t -> d h t"), abf.t[:, :, :], reads=[abf.b], pwrites=[g.b_KA])
        v32 = alloc([128, 4, 256], F32)
        vbf = alloc([128, 4, 256], BF16)
        S.dma("sp", v32.t[:, :, :], cv_in[l].rearrange("(kb p) d -> p kb d", p=128), writes=[v32.b])
        cp("dve", vbf.t[:, :, :], v32.t[:, :, :], [v32.b], writes=[vbf.b])
        S.dma("act", g.VA[0:PAST, :].rearrange("(kb p) d -> p kb d", p=128), vbf.t[:, :, :], reads=[vbf.b], pwrites=[g.b_VA])
        r32 = alloc([64, 512], F32)
        rbf = alloc([64, 512], BF16)
        S.dma("sp", r32.t[:, :], ckr_in[l], writes=[r32.b])
        cp("dve", rbf.t[:, :], r32.t[:, :], [r32.b], writes=[rbf.b])
        S.dma("act", g.KR[:, 0:PAST], rbf.t[:, :], reads=[rbf.b], pwrites=[g.b_KR])
        release(m)

    def phase_P2M(g, l):
        m = mark()
        sample = g.nseq == 1
        koff = g.cache if sample else 0
        w32 = alloc([128, 4, 512], F32)
        wuk = alloc([128, 4, 512], BF16)
        wuv = alloc([128, 4, 512], BF16)
        S.dma("sp", w32.t[:, :, :], w_uk[l].rearrange("(rc p) n -> p rc n", p=128), writes=[w32.b])
        cp("pool", wuk.t[:, :, :], w32.t[:, :, :], [w32.b], writes=[wuk.b])
        S.dma("sp", w32.t[:, :, :], w_uv[l].rearrange("(rc p) n -> p rc n", p=128), writes=[w32.b])
        cp("pool", wuv.t[:, :, :], w32.t[:, :, :], [w32.b], writes=[wuv.b])
        c32 = [alloc([128, 4, 512], F32) for _ in range(2)]
        cbf = [alloc([128, 4, 512], BF16) for _ in range(2)]
        sq = alloc([128, 4, 512], BF16)
        rstd = alloc([128, 512], F32)
        evb = [alloc([128, 512], BF16) for _ in range(3)]
        ne = 0
        tiles = []
        if sample:
            tiles.append(("cache", 0))
        tiles += [("new", ti) for ti in range(g.T // 512)]
        for n, (kind, ti) in enumerate(tiles):
            c_ = c32[n % 2]
            b_ = cbf[n % 2]
            if kind == "cache":
                S.dma("sp", c_.t[:, :, :], cckv_in[l].rearrange("(rc p) t -> p rc t", p=128), writes=[c_.b])
                cp("dve", b_.t[:, :, :], c_.t[:, :, :], [c_.b], writes=[b_.b])
                kc0 = 0
            else:
                c0 = ti * 512
                kc0 = koff + c0
                S.dma("sp", c_.t[:, :, :], g.CKV[:, c0:c0 + 512].rearrange("(rc p) t -> p rc t", p=128), reads=[g.b_CKV], writes=[c_.b])
                act(sq.t[:, :, :], c_.t[:, :, :], AF.Square, [c_.b], writes=[sq.b])
                pz = aux_ps()
                mmgroup(pz, [(ones_bf.t[:, :], sq.t[:, rc, :], pz.t[:, :]) for rc in range(4)], [ones_bf.b, sq.b])
                rsqrt_from(rstd.t[:, :], pz.t[:, :], [pz.b], 512, rstd.b)
                for rc in range(4):
                    stt("dve", c_.t[:, rc, :], c_.t[:, rc, :], V(l, "gkv", rc), rstd.t[:, :], ALU.mult, ALU.mult,
                        [c_.b, rstd.b, vecs.b], writes=[c_.b])
                if not sample:
                    S.dma("act", nckv_out[l, :, c0:c0 + 512].rearrange("(rc p) t -> p rc t", p=128), c_.t[:, :, :], reads=[c_.b])
                cp("pool", b_.t[:, :, :], c_.t[:, :, :], [c_.b], writes=[b_.b])
            for h in range(4):
                ps = acc_ps()
                mmgroup(ps, [(wuk.t[:, rc, h * 128:(h + 1) * 128], b_.t[:, rc, :], ps.t[:, :]) for rc in range(4)], [wuk.b, b_.b])
                e_ = evb[ne % 3]
                ne += 1
                act(e_.t[:, :], ps.t[:, :], AF.Copy, [ps.b], writes=[e_.b])
                S.dma("act", g.KN[h, :, kc0:kc0 + 512], e_.t[:, :], reads=[e_.b], pwrites=[g.b_KN])
            for j in range(4):
                ps = acc_ps()
                mmgroup(ps, [(b_.t[:, rc, j * 128:(j + 1) * 128], wuv.t[:, rc, :], ps.t[:, :]) for rc in range(4)], [wuv.b, b_.b])
                e_ = evb[ne % 3]
                ne += 1
                act(e_.t[:, :], ps.t[:, :], AF.Copy, [ps.b], writes=[e_.b])
                S.dma("act", g.VM[kc0 + j * 128:kc0 + (j + 1) * 128, :], e_.t[:, :], reads=[e_.b], pwrites=[g.b_VM])
        release(m)

    def phase_P2L(g, l):
        m = mark()
        sample = g.nseq == 1
        T, ns, Sq = g.T, g.nseq, g.S
        W = Sq + 3
        g32 = alloc([128, 32, 128], F32)
        gw = alloc([128, 32, 128], BF16)
        S.dma("sp", g32.t[:, 0:16, :], lru_w_a[l].rearrange("d n k j -> k (d n) j"), writes=[g32.b])
        S.dma("sp", g32.t[:, 16:32, :], lru_w_x[l].rearrange("d n k j -> k (d n) j"), pwrites=[g32.b])
        cp("pool", gw.t[:, :, :], g32.t[:, :, :], [g32.b], writes=[gw.b])
        xpad = alloc([128, ns, W], F32)
        xc = alloc([128, ns, Sq], F32)
        xcb = alloc([128, ns, Sq], BF16)
        A = alloc([128, ns, Sq], F32)
        U = alloc([128, ns, Sq], F32)
        Hf = alloc([128, ns, Sq], F32)
        Hb = alloc([128, ns, Sq], F32)
        G = alloc([128, ns, Sq], F32)
        yb = alloc([128, ns, Sq], BF16)
        gt = [alloc([128, 512], F32) for _ in range(2)]
        S.op("pool", lambda e: e.memset(xpad.t[:, :, :], 0.0), writes=[xpad.b])
        S.barrier()
        flat = lambda t_: t_.t[:, :, :].rearrange("p s t -> p (s t)")
        for ch in range(8):
            S.dma("sp", xpad.t[:, :, 1:1 + Sq], g.XR[ch * 128:(ch + 1) * 128, :].rearrange("p (s t) -> p s t", s=ns),
                  reads=[g.b_XR, xpad.b], writes=[xpad.b])
            S.dma("sp", flat(G), g.GR[ch * 128:(ch + 1) * 128, :], reads=[g.b_GR], writes=[G.b])
            ts("dve", xc.t[:, :, :], xpad.t[:, :, 0:Sq], V(l, "lcw", 0 * 8 + ch), ALU.mult, [xpad.b, vecs.b], writes=[xc.b],
               s2=V(l, "lcb", ch), op1=ALU.add)
            for j in (1, 2, 3):
                stt("dve", xc.t[:, :, :], xpad.t[:, :, j:j + Sq], V(l, "lcw", j * 8 + ch), xc.t[:, :, :], ALU.mult, ALU.add,
                    [xpad.b, xc.b, vecs.b], writes=[xc.b])
            act(xcb.t[:, :, :], xc.t[:, :, :], AF.Copy, [xc.b], writes=[xcb.b])
            act(flat(G), flat(G), AF.Gelu, [G.b], writes=[G.b])
            for d in range(2):
                H = Hf if d == 0 else Hb
                lo = l * 16 + d * 8 + ch
                for ti in range(T // 512):
                    sl = slice(ti * 512, (ti + 1) * 512)
                    pa = acc_ps()
                    mmgroup(pa, [(gw.t[:, d * 8 + ch, :], flat(xcb)[:, sl], pa.t[:, :])], [gw.b, xcb.b])
                    pi = acc_ps()
                    mmgroup(pi, [(gw.t[:, 16 + d * 8 + ch, :], flat(xcb)[:, sl], pi.t[:, :])], [gw.b, xcb.b])
                    act(flat(A)[:, sl], pa.t[:, :], AF.Sigmoid, [pa.b, vecs.b], pwrites=[A.b], bias=V(l, "lba", d * 8 + ch))
                    act(flat(U)[:, sl], pi.t[:, :], AF.Sigmoid, [pi.b, vecs.b], pwrites=[U.b], bias=V(l, "lbx", d * 8 + ch))
                act(flat(H), flat(A), AF.Exp, [A.b, clam2.b], writes=[H.b], scale=clam2.t[:, lo:lo + 1])
                act(flat(A), flat(A), AF.Exp, [A.b, clam.b], writes=[A.b], scale=clam.t[:, lo:lo + 1])
                ts("dve", flat(H), flat(H), -1.0, ALU.mult, [H.b], writes=[H.b], s2=1.0, op1=ALU.add)
                act(flat(H), flat(H), AF.Sqrt, [H.b], writes=[H.b])
                tt("dve", flat(U), flat(U), flat(xc), ALU.mult, [U.b, xc.b], writes=[U.b])
                tt("dve", flat(U), flat(U), flat(H), ALU.mult, [U.b, H.b], writes=[U.b])
                for s_ in range(ns):
                    if sample:
                        so = l * 16 + d * 8 + ch
                        init = st_sb.t[:, so:so + 1]
                        rd = [A.b, U.b, st_sb.b]
                    else:
                        init = 0.0
                        rd = [A.b, U.b]
                    if d == 0:
                        S.op("dve", lambda e, s_=s_, init=init, H=H: e.tensor_tensor_scan(
                            out=H.t[:, s_, :], data0=A.t[:, s_, :], data1=U.t[:, s_, :], initial=init, op0=ALU.mult, op1=ALU.add),
                            reads=rd, **({"writes": [H.b]} if s_ == 0 else {"pwrites": [H.b]}))
                    else:
                        rv = lambda t_, s_=s_: bass.AP(t_.t, s_ * Sq + Sq - 1, [[ns * Sq, 128], [-1, Sq]])
                        S.op("dve", lambda e, rv=rv, init=init, H=H: e.tensor_tensor_scan(
                            out=rv(H), data0=rv(A), data1=rv(U), initial=init, op0=ALU.mult, op1=ALU.add),
                            reads=rd, **({"writes": [H.b]} if s_ == 0 else {"pwrites": [H.b]}))
                if not sample:
                    o = ((l * 2 + d) * 8 + ch) * 4
                    col = Sq - 1 if d == 0 else 0
                    cp("pool", nst_sb.t[:, o:o + 4], H.t[:, :, col], [H.b], pwrites=[nst_sb.b])
            tt("dve", flat(Hf), flat(Hf), flat(Hb), ALU.add, [Hf.b, Hb.b], writes=[Hf.b])
            tt("dve", flat(yb), flat(Hf), flat(G), ALU.mult, [Hf.b, G.b], writes=[yb.b])
            S.dma("act", g.MIX[512 + ch * 128:512 + (ch + 1) * 128, :], flat(yb), reads=[yb.b], pwrites=[g.b_MIX])
        release(m)

    def phase_P3(g, l):
        m = mark()
        sample = g.nseq == 1
        Tk = g.Tk if sample else g.S
        nkb = Tk // 128
        QW = 512 if sample else 256
        nqg = g.S // QW
        kT = [alloc([128, Tk], BF16) for _ in range(2)]
        krT = alloc([64, g.Tk], BF16)
        vv = [alloc([128, nkb, 128], BF16) for _ in range(2)]
        qT = [alloc([128, QW], BF16) for _ in range(2)]
        qrT = [alloc([64, QW], BF16) for _ in range(2)]
        pT = [alloc([128, QW], BF16) for _ in range(3)]
        rz = alloc([128, QW], F32)
        ob = [alloc([128, QW], BF16) for _ in range(2)]
        S.dma("sp", krT.t[:, :], g.KR[:, :], reads=[g.b_KR], writes=[krT.b])
        n = 0
        nq = 0
        npt = 0
        for s_ in range(g.nseq):
            k0 = 0 if sample else s_ * g.S
            for h in range(8):
                mla = h >= 4
                hh = h - 4 if mla else h
                kt_, v_ = kT[n % 2], vv[n % 2]
                n += 1
                if mla:
                    S.dma("sp", kt_.t[:, :], g.KN[hh, :, k0:k0 + Tk], reads=[g.b_KN], writes=[kt_.b])
                    S.dma("sp", v_.t[:, :, :], g.VM[k0:k0 + Tk, hh * 128:(hh + 1) * 128].rearrange("(kb p) d -> p kb d", p=128),
                          reads=[g.b_VM], writes=[v_.b])
                    scale = 192 ** -0.5
                else:
                    S.dma("sp", kt_.t[:, :], g.KA[hh // 2, :, k0:k0 + Tk], reads=[g.b_KA], writes=[kt_.b])
                    S.dma("sp", v_.t[:, :, :], g.VA[k0:k0 + Tk, (hh // 2) * 128:(hh // 2 + 1) * 128].rearrange("(kb p) d -> p kb d", p=128),
                          reads=[g.b_VA], writes=[v_.b])
                    scale = 128 ** -0.5
                for qg in range(nqg):
                    q0 = s_ * g.S + qg * QW
                    q_, qr_ = qT[nq % 2], qrT[nq % 2]
                    o_ = ob[nq % 2]
                    nq += 1
                    if mla:
                        S.dma("sp", q_.t[:, :], g.QN[hh, :, q0:q0 + QW], reads=[g.b_QN], writes=[q_.b])
                        S.dma("sp", qr_.t[:, :], g.QR[hh, :, q0:q0 + QW], reads=[g.b_QR], writes=[qr_.b])
                    else:
                        S.dma("sp", q_.t[:, :], g.QA[hh, :, q0:q0 + QW], reads=[g.b_QA], writes=[q_.b])
                    pO = psum[2 + (nq % 2) * 1]
                    pZ = psum[4 + (nq % 2) * 1]

                    def s_mm(kb):
                        ps = psum[kb % 2]
                        ksl = slice(kb * 128, (kb + 1) * 128)
                        if mla:
                            kr0 = k0 + kb * 128
                            mmgroup(ps, [(kt_.t[:, ksl], q_.t[:, :], ps.t[:, 0:QW]),
                                         (krT.t[:, kr0:kr0 + 128], qr_.t[:, :], ps.t[:, 0:QW])], [kt_.b, q_.b, krT.b, qr_.b])
                        else:
                            mmgroup(ps, [(kt_.t[:, ksl], q_.t[:, :], ps.t[:, 0:QW])], [kt_.b, q_.b])
                        return ps
                    pend = s_mm(0)
                    for kb in range(nkb):
                        ps = pend
                        if kb + 1 < nkb:
                            pend = s_mm(kb + 1)
                        p_ = pT[npt % 3]
                        npt += 1
                        act(p_.t[:, :], ps.t[:, 0:QW], AF.Exp, [ps.b], writes=[p_.b], scale=scale)
                        if kb == 0:
                            mm(pO.t[:, 0:QW], v_.t[:, kb, :], p_.t[:, :], True, nkb == 1, [v_.b, p_.b], writes=[pO.b])
                            mm(pZ.t[:, 0:QW], ones_bf.t[:, :], p_.t[:, :], True, nkb == 1, [ones_bf.b, p_.b], writes=[pZ.b])
                        else:
                            mm(pO.t[:, 0:QW], v_.t[:, kb, :], p_.t[:, :], False, kb == nkb - 1, [v_.b, p_.b], pwrites=[pO.b])
                            mm(pZ.t[:, 0:QW], ones_bf.t[:, :], p_.t[:, :], False, kb == nkb - 1, [ones_bf.b, p_.b], pwrites=[pZ.b])
                    S.op("dve", lambda e, pZ=pZ: e.reciprocal(out=rz.t[:, :], in_=pZ.t[:, 0:QW]), reads=[pZ.b], writes=[rz.b])
                    tt("dve", o_.t[:, :], pO.t[:, 0:QW], rz.t[:, :], ALU.mult, [pO.b, rz.b], writes=[o_.b])
                    row = (1536 + hh * 128) if mla else hh * 128
                    S.dma("act", g.MIX[row:row + 128, q0:q0 + QW], o_.t[:, :], reads=[o_.b], pwrites=[g.b_MIX])
        release(m)

    def proj_to_M(g, l, blk, src_scr, b_src, nk, wsrc, kstage, pcols):
        b0, Tb = blk
        m = mark()
        aT = alloc([128, nk, Tb], BF16)
        aTb = [Buf() for _ in range(Tb // 512)]
        ps_w = PanelStream(nk, pcols, kstage)
        ev = [alloc([128, 512], F32) for _ in range(3)]
        ne = 0
        for pnl, pan in ps_w.stream([(wsrc[:, q_ * pcols:(q_ + 1) * pcols], pcols) for q_ in range(D // pcols)]):
            if pnl == 0:
                for ti in range(Tb // 512):
                    for k0_ in range(0, nk, 16):
                        kn_ = min(16, nk - k0_)
                        S.dma("sp", aT.t[:, k0_:k0_ + kn_, ti * 512:(ti + 1) * 512],
                              src_scr[k0_ * 128:(k0_ + kn_) * 128, b0 + ti * 512:b0 + (ti + 1) * 512].rearrange("(kc p) t -> p kc t", p=128),
                              reads=[b_src], **({"writes": [aTb[ti]]} if k0_ == 0 else {"pwrites": [aTb[ti]]}))
            for ti in range(Tb // 512):
                c0 = b0 + ti * 512
                for c in range(pcols // 128):
                    mc = pnl * (pcols // 128) + c
                    ps = acc_ps()
                    mmgroup(ps, [(pan.t[:, kc, c * 128:(c + 1) * 128], aT.t[:, kc, ti * 512:(ti + 1) * 512], ps.t[:, :])
                                 for kc in range(nk)], [aTb[ti], pan.b])
                    e_ = ev[ne % 3]
                    ne += 1
                    act(e_.t[:, :], ps.t[:, :], AF.Copy, [ps.b], writes=[e_.b])
                    S.dma("act", g.M[mc * 128:(mc + 1) * 128, c0:c0 + 512], e_.t[:, :], reads=[e_.b], pwrites=[g.b_M])
        release(m)

    def phase_update(g, l, which_g, next_norm):
        m = mark()
        xts = [alloc([128, KC, 512], F32) for _ in range(2)]
        mts = [alloc([128, KC, 512], F32) for _ in range(2)]
        tmp = [alloc([128, 512], F32) for _ in range(3)]
        sq = alloc([128, KC, 512], BF16)
        rstd = alloc([128, 512], F32)
        rstd2 = alloc([128, 512], F32)
        hst = [alloc([128, KC, 512], BF16) for _ in range(1)]
        for ti in range(g.T // 512):
            xt, mt = xts[ti % 2], mts[ti % 2]
            sl = slice(ti * 512, (ti + 1) * 512)
            S.dma("sp", xt.t[:, :, :], g.xT[:, sl].rearrange("(kc p) t -> p kc t", p=128), reads=[g.b_xT], writes=[xt.b])
            S.dma("sp", mt.t[:, :, :], g.M[:, sl].rearrange("(kc p) t -> p kc t", p=128), reads=[g.b_M], writes=[mt.b])
            act(sq.t[:, :, :], mt.t[:, :, :], AF.Square, [mt.b], writes=[sq.b])
            pz = aux_ps()
            mmgroup(pz, [(ones_bf.t[:, :], sq.t[:, kc, :], pz.t[:, :]) for kc in range(KC)], [ones_bf.b, sq.b])
            rsqrt_from(rstd.t[:, :], pz.t[:, :], [pz.b], D, rstd.b)
            for kc in range(KC):
                tm = tmp[kc % 3]
                stt("dve", tm.t[:, :], mt.t[:, kc, :], DV(l, g.ci, which_g, kc), rstd.t[:, :], ALU.mult, ALU.mult,
                    [mt.b, rstd.b, dv.b], writes=[tm.b])
                tt("pool" if kc % 4 == 3 else "dve", xt.t[:, kc, :], xt.t[:, kc, :], tm.t[:, :], ALU.add, [xt.b, tm.b], writes=[xt.b])
            if next_norm is None:
                S.dma("act", g.y_out[:, sl].rearrange("(kc p) t -> p kc t", p=128), xt.t[:, :, :], reads=[xt.b])
            else:
                S.dma("act", g.xT[:, sl].rearrange("(kc p) t -> p kc t", p=128), xt.t[:, :, :], reads=[xt.b], pwrites=[g.b_xT])
                prenorm_tile(g, next_norm[0], next_norm[1], xt, ti * 512, tmp, sq, rstd2, hst[0])
        release(m)

    def phase_P6(g, l, blk):
        b0, Tb = blk
        m = mark()
        sample = g.nseq == 1
        ns = 1 if sample else g.nseq
        Sq = Tb if sample else g.S
        W = Sq + 2
        hT = alloc([128, KC, Tb], BF16)
        halo = alloc([128, KC, 2], BF16)
        hTb = [Buf() for _ in range(Tb // 512)]
        S.op("pool", lambda e: e.memset(halo.t[:, :, :], 0.0), writes=[halo.b])
        has_halo = sample
        if sample:
            Hv = g.H.rearrange("(kc p) t -> p kc t", p=128)
            if b0 > 0:
                S.dma("sp", halo.t[:, :, 0:1], Hv[:, :, b0 - 1:b0], reads=[g.b_H, halo.b], writes=[halo.b], slow=True)
            if b0 + Tb < g.T:
                S.dma("sp", halo.t[:, :, 1:2], Hv[:, :, b0 + Tb:b0 + Tb + 1], reads=[g.b_H, halo.b], writes=[halo.b], slow=True)
        ps_w = PanelStream(KC, 512, 4)
        upad = [alloc([128, ns, W], F32) for _ in range(2)]
        cc = [alloc([128, ns, Sq], F32) for _ in range(2)]
        Gs = alloc([128, 4, Tb], BF16)
        ab = [alloc([128, Tb], BF16) for _ in range(2)]
        for u_ in upad:
            S.op("pool", lambda e, u_=u_: e.memset(u_.t[:, :, :], 0.0), writes=[u_.b])
        S.barrier()
        nu = 0
        def colbase_of(pp_):
            return (DFF if pp_ % 2 == 1 else 0) + (pp_ // 2) * 512
        for pp, pan in ps_w.stream([(w_up[l, :, colbase_of(q_):colbase_of(q_) + 512], 512) for q_ in range(22)]):
            if pp == 0:
                for ti in range(Tb // 512):
                    S.dma("sp", hT.t[:, :, ti * 512:(ti + 1) * 512],
                          g.H[:, b0 + ti * 512:b0 + (ti + 1) * 512].rearrange("(kc p) t -> p kc t", p=128),
                          reads=[g.b_H], writes=[hTb[ti]])
            isv = pp % 2 == 1
            cp4 = pp // 2
            for c in range(4):
                chn = cp4 * 4 + c + (NFC if isv else 0)
                u_ = upad[nu % 2]
                c_ = cc[nu % 2]
                a_ = ab[nu % 2]
                nu += 1
                for ti in range(Tb // 512):
                    ps = acc_ps()
                    mmgroup(ps, [(pan.t[:, kc, c * 128:(c + 1) * 128], hT.t[:, kc, ti * 512:(ti + 1) * 512], ps.t[:, :])
                                 for kc in range(KC)], [hTb[ti], pan.b])
                    if sample:
                        dstv = u_.t[:, 0, 1 + ti * 512:1 + (ti + 1) * 512]
                        srcv = ps.t[:, :]
                    else:
                        dstv = u_.t[:, 2 * ti:2 * ti + 2, 1:1 + Sq]
                        srcv = ps.t[:, :].rearrange("p (s t) -> p s t", s=2)
                    act(dstv, srcv, AF.Copy, [ps.b], pwrites=[u_.b])
                if has_halo:
                    ph = aux_ps()
                    mmgroup(ph, [(pan.t[:, kc, c * 128:(c + 1) * 128], halo.t[:, kc, :], ph.t[:, 0:2]) for kc in range(KC)],
                            [halo.b, pan.b])
                    act(u_.t[:, 0, 0:1], ph.t[:, 0:1], AF.Copy, [ph.b], pwrites=[u_.b])
                    act(u_.t[:, 0, W - 1:W], ph.t[:, 1:2], AF.Copy, [ph.b], pwrites=[u_.b])
                ts("dve", c_.t[:, :, :], u_.t[:, :, 0:Sq], V(l, "fcw", 0 * 88 + chn), ALU.mult, [u_.b, vecs.b], writes=[c_.b])
                stt("dve", c_.t[:, :, :], u_.t[:, :, 1:1 + Sq], V(l, "fcw", 1 * 88 + chn), c_.t[:, :, :], ALU.mult, ALU.add,
                    [u_.b, c_.b, vecs.b], writes=[c_.b])
                stt("dve", c_.t[:, :, :], u_.t[:, :, 2:2 + Sq], V(l, "fcw", 2 * 88 + chn), c_.t[:, :, :], ALU.mult, ALU.add,
                    [u_.b, c_.b, vecs.b], writes=[c_.b])
                cflat = c_.t[:, :, :].rearrange("p s t -> p (s t)")
                if not isv:
                    act(Gs.t[:, c, :], cflat, AF.Silu, [c_.b, vecs.b], bias=V(l, "fcb", chn),
                        **({"writes": [Gs.b]} if c == 0 else {"pwrites": [Gs.b]}))
                else:
                    act(cflat, cflat, AF.Identity, [c_.b, vecs.b], writes=[c_.b], bias=V(l, "fcb", chn))
                    tt("pool", a_.t[:, :], cflat, Gs.t[:, c, :], ALU.mult, [c_.b, Gs.b], writes=[a_.b])
                    r0 = (cp4 * 4 + c) * 128
                    S.dma("act", g.ACT[r0:r0 + 128, b0:b0 + Tb], a_.t[:, :], reads=[a_.b], pwrites=[g.b_ACT])
        release(m)

    import os
    PH = {"n": 0, "max": int(os.environ.get("MK_STOP", "100000"))}

    def ph(fn, *a):
        if PH["n"] < PH["max"]:
            fn(*a)
        PH["n"] += 1

    def run_group(g):
        ph(phase_P1, g, 0)
        for l in range(L):
            for blk in g.blocks:
                ph(phase_P2, g, l, blk)
            if g.nseq == 1:
                ph(phase_P2C, g, l)
            ph(phase_P2M, g, l)
            ph(phase_P2L, g, l)
            ph(phase_P3, g, l)
            for blk in g.blocks:
                ph(proj_to_M, g, l, blk, g.MIX, g.b_MIX, KC, w_out[l], 4, 512)
            ph(phase_update, g, l, 2, (l, 3))
            for blk in g.blocks:
                ph(phase_P6, g, l, blk)
            for s0 in range(0, g.T, 1024):
                ph(proj_to_M, g, l, (s0, 1024), g.ACT, g.b_ACT, NFC, w_down[l], 11, 256)
            ph(phase_update, g, l, 5, (l + 1, 0) if l + 1 < L else None)

    which = os.environ.get("MK_GROUPS", "ps")
    if "p" in which:
        run_group(GP)
        S.dma("act", nst_out, nst_sb.t[:, :], reads=[nst_sb.b])
    if "s" in which:
        run_group(GS)
    S.emit()
    return nc


def _fm(v):
    v = np.asarray(v, np.float32)
    lead = int(np.prod(v.shape[:-1])) if v.ndim > 1 else 1
    return np.ascontiguousarray(v.reshape(lead, -1, 128).transpose(2, 0, 1).reshape(128, -1))


def _rope_tables(n_tok, width):
    half = width // 2
    quarter = half // 2
    inv = (10000.0 ** (-np.arange(quarter, dtype=np.float32) / quarter)).astype(np.float32)
    t = np.arange(n_tok)
    row = (t // 64).astype(np.float32)
    col = (t % 64).astype(np.float32)
    tab = np.zeros((width, 2, n_tok), np.float32)
    for part, pos in ((0, row), (1, col)):
        ang = pos[None, :] * inv[:, None]
        c, s = np.cos(ang).astype(np.float32), np.sin(ang).astype(np.float32)
        base = part * half
        tab[base:base + quarter, 0] = c
        tab[base + quarter:base + half, 0] = c
        tab[base:base + quarter, 1] = -s
        tab[base + quarter:base + half, 1] = s
    return tab


_NC_CACHE = {}


def kernel(x_prompt, x_sample, cache_attn_k, cache_attn_v, cache_mla_ckv, cache_mla_krope, state_lru,
           c, c_ctx, w_mod, b_mod, g_pre_mix, g_post_mix, g_pre_ffn, g_post_ffn, w_in, g_q, g_k,
           lru_conv_w, lru_conv_b, lru_w_a, lru_b_a, lru_w_x, lru_b_x, lru_lambda, g_kv, w_uk, w_uv,
           w_out, ffn_w_up, ffn_conv_w, ffn_conv_b, ffn_w_down):
    f = lambda a: np.ascontiguousarray(np.asarray(a, dtype=np.float32))
    if "nc" not in _NC_CACHE:
        _NC_CACHE["nc"] = build_program()
    nc = _NC_CACHE["nc"]
    vecs = np.zeros((128, L * NVL), np.float32)
    for l in range(L):
        o = l * NVL
        def put(name, arr):
            a = _fm(arr)
            vecs[:, o + VO[name]:o + VO[name] + a.shape[1]] = a
        put("bmod", b_mod[l]); put("gpm", g_pre_mix[l]); put("gpom", g_post_mix[l]); put("gpf", g_pre_ffn[l])
        put("gpof", g_post_ffn[l]); put("gq", g_q[l]); put("gk", g_k[l]); put("lcw", lru_conv_w[l]); put("lcb", lru_conv_b[l])
        put("lba", lru_b_a[l]); put("lbx", lru_b_x[l]); put("llam", lru_lambda[l]); put("gkv", g_kv[l])
        put("fcw", ffn_conv_w[l]); put("fcb", ffn_conv_b[l])
    rmat = np.zeros((128, 256), np.float32)
    for d in range(128):
        rmat[d ^ 32, d] = 1.0
    for d in range(64):
        rmat[d ^ 16, 128 + d] = 1.0
    rope128 = _rope_tables(SS_T, 128)
    rope64 = _rope_tables(SS_T, 64)
    shared = dict(vecs=vecs, rmat=rmat, rope128=rope128, rope64=rope64,
                  w_mod=f(w_mod), w_in=f(w_in), lru_w_a=f(lru_w_a), lru_w_x=f(lru_w_x),
                  w_uk=f(w_uk).reshape(L, 512, 512), w_uv=f(w_uv).reshape(L, 512, 512), w_out=f(w_out),
                  ffn_w_up=f(ffn_w_up), ffn_w_down=f(ffn_w_down))
    x_prompt = np.asarray(x_prompt, np.float32)
    x_sample = np.asarray(x_sample, np.float32)
    in_maps = []
    for i in range(8):
        b = i // 2
        m = dict(shared)
        m["xp"] = np.ascontiguousarray(x_prompt[4 * i:4 * i + 4].reshape(SP_T, D).T)
        m["xs"] = np.ascontiguousarray(x_sample[b].T)
        m["ck"] = np.ascontiguousarray(np.asarray(cache_attn_k[b], np.float32).transpose(0, 2, 3, 1))
        m["cv"] = np.ascontiguousarray(np.asarray(cache_attn_v[b], np.float32).reshape(L, PAST, 256))
        m["cckv"] = np.ascontiguousarray(np.asarray(cache_mla_ckv[b], np.float32).transpose(0, 2, 1))
        m["ckr"] = np.ascontiguousarray(np.asarray(cache_mla_krope[b], np.float32).transpose(0, 2, 1))
        m["st"] = _fm(np.asarray(state_lru[b], np.float32).reshape(L * 2, 1024))
        cond = np.stack([np.asarray(c_ctx, np.float32), np.asarray(c[b], np.float32)], axis=-1)
        m["cond"] = np.ascontiguousarray(cond.reshape(KC, 128, 2).transpose(1, 0, 2).reshape(128, KC * 2))
        in_maps.append(m)
    import os
    ncores = int(os.environ.get("MK_NCORES", "8"))
    res = run_bass_kernel_spmd(nc, in_maps[:ncores], core_ids=list(range(ncores)))
    R = list(res.results)
    while len(R) < 8:
        R.append({k: np.zeros_like(v) for k, v in R[0].items()})
    y_prompt = np.empty((32, 256, D), np.float32)
    y_sample = np.empty((4, SS_T, D), np.float32)
    nk = np.empty((32, L, 256, 2, 128), np.float32)
    nv = np.empty((32, L, 256, 2, 128), np.float32)
    nckv = np.empty((32, L, 256, 512), np.float32)
    nkr = np.empty((32, L, 256, 64), np.float32)
    nst = np.empty((32, L, 2, 1024), np.float32)
    for i in range(8):
        r = R[i]
        y_prompt[4 * i:4 * i + 4] = r["yp"].T.reshape(4, 256, D)
        if i % 2 == 0:
            y_sample[i // 2] = r["ys"].T
        nk[4 * i:4 * i + 4] = r["nk"].reshape(L, 2, 128, 4, 256).transpose(3, 0, 4, 1, 2)
        nv[4 * i:4 * i + 4] = r["nv"].reshape(L, 4, 256, 2, 128).transpose(1, 0, 2, 3, 4)
        nckv[4 * i:4 * i + 4] = r["nckv"].reshape(L, 512, 4, 256).transpose(2, 0, 3, 1)
        nkr[4 * i:4 * i + 4] = r["nkr"].reshape(L, 64, 4, 256).transpose(2, 0, 3, 1)
        nst[4 * i:4 * i + 4] = r["nst"].reshape(128, L, 2, 8, 4).transpose(4, 1, 2, 3, 0).reshape(4, L, 2, 1024)
    return (y_prompt, y_sample, nk, nv, nckv, nkr, nst)
```

```python
import numpy as np
import concourse.bass as bass
import concourse.mybir as mybir
from concourse.bass_utils import run_bass_kernel_spmd
from contextlib import ExitStack

F32 = mybir.dt.float32
BF16 = mybir.dt.bfloat16
AF = mybir.ActivationFunctionType
ALU = mybir.AluOpType

L = 4
D = 2048
KC = 16
DIN = 4416
DFF = 5632
NFC = 44
EPS = 1e-6
NVL = 622
PAST = 512
SP_T = 1024
SS_T = 4096
VO = dict(bmod=0, gpm=96, gpom=112, gpf=128, gpof=144, gq=160, gk=161, lcw=162, lcb=194, lba=202,
          lbx=218, llam=234, gkv=250, fcw=254, fcb=518)

KDMA = 6


class Buf:
    __slots__ = ("w", "r", "prev")

    def __init__(self):
        self.w = {}
        self.r = {}
        self.prev = {}


def _merge(dst, src):
    for k, v in src.items():
        if dst.get(k, 0) < v:
            dst[k] = v


class Sched:
    ENG = ("pe", "act", "dve", "pool", "sp")

    def __init__(self, nc):
        self.nc = nc
        self.q = {e: [] for e in self.ENG}
        self.cnt = {e: 0 for e in self.ENG}
        self.waited = {e: {} for e in self.ENG}
        self.dma_n = {e: 0 for e in self.ENG}
        self.semkeys = set()
        self.all_events = {}
        self.bar = {e: {} for e in self.ENG}

    def barrier(self):
        snap = dict(self.all_events)
        for e in self.ENG:
            _merge(self.bar[e], snap)

    def _deps(self, eng, reads, writes, pwrites):
        deps = {}
        if self.bar[eng]:
            _merge(deps, self.bar[eng])
            self.bar[eng] = {}
        for b in reads:
            _merge(deps, b.w)
        for b in writes:
            _merge(deps, b.w)
            _merge(deps, b.r)
            _merge(deps, b.prev)
        for b in pwrites:
            if b.r:
                newprev = {}
                _merge(newprev, b.w)
                _merge(newprev, b.r)
                b.prev = newprev
                b.w = {}
                b.r = {}
            _merge(deps, b.prev)
        return deps

    def _commit(self, ev, reads, writes, pwrites):
        k, v = ev
        for b in reads:
            if b.r.get(k, 0) < v:
                b.r[k] = v
        for b in writes:
            b.w = {k: v}
            b.r = {}
            b.prev = {}
        for b in pwrites:
            if b.w.get(k, 0) < v:
                b.w[k] = v
        if self.all_events.get(k, 0) < v:
            self.all_events[k] = v

    def _waits(self, eng, deps, skip_self=False):
        out = []
        wd = self.waited[eng]
        for k, v in deps.items():
            if skip_self and k == ("eng", eng):
                continue
            if wd.get(k, 0) < v:
                wd[k] = v
                out.append((k, v))
        return out

    def op(self, eng, fn, reads=(), writes=(), pwrites=()):
        deps = self._deps(eng, reads, writes, pwrites)
        waits = self._waits(eng, deps, skip_self=(eng == "pe"))
        self.cnt[eng] += 1
        k = ("eng", eng)
        self.semkeys.add(k)
        ev = (k, self.cnt[eng])
        self.q[eng].append((waits, fn, k, 1))
        self._commit(ev, reads, writes, pwrites)

    def dma(self, queue, out, in_, reads=(), writes=(), pwrites=(), slow=False):
        deps = self._deps(queue, reads, writes, pwrites)
        n = self.dma_n[queue]
        self.dma_n[queue] += 1
        slot, rnd = n % KDMA, n // KDMA
        k = ("dma", queue, slot)
        self.semkeys.add(k)
        if rnd > 0:
            _merge(deps, {k: 16 * rnd})
        waits = self._waits(queue, deps)
        ev = (k, 16 * (rnd + 1))
        fn = (lambda e, out=out, in_=in_: e.dma_start(out=out, in_=in_, allow_slow_non_contiguous=True)) if slow else (lambda e, out=out, in_=in_: e.dma_start(out=out, in_=in_))
        self.q[queue].append((waits, fn, k, 16))
        self._commit(ev, reads, writes, pwrites)

    def emit(self):
        nc = self.nc
        with ExitStack() as es:
            sems = {}
            for k in sorted(self.semkeys, key=str):
                sems[k] = es.enter_context(nc.semaphore("s_" + "_".join(str(x) for x in k)))
            block = es.enter_context(nc.Block())
            finw = self._waits("sp", dict(self.all_events))

            def mk(e):
                def body(eng):
                    for waits, fn, k, inc in self.q[e]:
                        for (wk, wv) in waits:
                            eng.wait_ge(sems[wk], wv)
                        fn(eng).then_inc(sems[k], inc)
                    if e == "sp":
                        for (wk, wv) in finw:
                            eng.wait_ge(sems[wk], wv)
                return body

            block.tensor(mk("pe"))
            block.scalar(mk("act"))
            block.vector(mk("dve"))
            block.gpsimd(mk("pool"))
            block.sync(mk("sp"))


class Tile:
    __slots__ = ("t", "b")

    def __init__(self, t):
        self.t = t
        self.b = Buf()


def build_program():
    nc = bass.Bass("TRN2", target_bir_lowering=False)
    S = Sched(nc)

    def din(name, shape, dt=F32):
        return nc.dram_tensor(name, list(shape), dt, kind="ExternalInput").ap()

    def dout(name, shape, dt=F32):
        return nc.dram_tensor(name, list(shape), dt, kind="ExternalOutput").ap()

    def dscr(name, shape, dt):
        return nc.dram_tensor(name, list(shape), dt).ap()

    xp_in = din("xp", [D, SP_T])
    xs_in = din("xs", [D, SS_T])
    ck_in = din("ck", [L, 2, 128, PAST])
    cv_in = din("cv", [L, PAST, 256])
    cckv_in = din("cckv", [L, 512, PAST])
    ckr_in = din("ckr", [L, 64, PAST])
    st_in = din("st", [128, L * 2 * 8])
    cond_in = din("cond", [128, KC * 2])
    vecs_in = din("vecs", [128, L * NVL])
    rope128_in = din("rope128", [128, 2, SS_T])
    rope64_in = din("rope64", [64, 2, SS_T])
    rmat_in = din("rmat", [128, 256])
    w_mod = din("w_mod", [L, D, 6 * D])
    w_in = din("w_in", [L, D, DIN])
    lru_w_a = din("lru_w_a", [L, 2, 8, 128, 128])
    lru_w_x = din("lru_w_x", [L, 2, 8, 128, 128])
    w_uk = din("w_uk", [L, 512, 512])
    w_uv = din("w_uv", [L, 512, 512])
    w_out = din("w_out", [L, D, D])
    w_up = din("ffn_w_up", [L, D, 2 * DFF])
    w_down = din("ffn_w_down", [L, DFF, D])
    yp_out = dout("yp", [D, SP_T])
    ys_out = dout("ys", [D, SS_T])
    nk_out = dout("nk", [L, 2, 128, SP_T])
    nv_out = dout("nv", [L, SP_T, 256])
    nckv_out = dout("nckv", [L, 512, SP_T])
    nkr_out = dout("nkr", [L, 64, SP_T])
    nst_out = dout("nst", [128, L * 2 * 8 * 4])

    SB_TOTAL = 229312
    arena = {"off": 16640, "n": 0}

    def alloc(shape, dt):
        nbytes = int(np.prod(shape[1:])) * (4 if dt == F32 else 2)
        nbytes = (nbytes + 63) // 64 * 64
        off = arena["off"]
        assert off + nbytes <= SB_TOTAL, ("SBUF overflow", off, nbytes)
        arena["off"] = off + nbytes
        arena["n"] += 1
        return Tile(nc.alloc_sbuf_tensor_at("t%d" % arena["n"], list(shape), dt, offset=off))

    def mark():
        return arena["off"]

    def release(m):
        S.barrier()
        arena["off"] = m

    psum = [Tile(nc.alloc_psum_tensor("ps%d" % i, [128, 512], F32)) for i in range(8)]

    def act(out, in_, func, reads, writes=(), pwrites=(), bias=0.0, scale=1.0):
        S.op("act", lambda e: e.activation(out=out, in_=in_, func=func, bias=bias, scale=scale),
             reads=reads, writes=writes, pwrites=pwrites)

    def tt(eng, out, in0, in1, op, reads, writes=(), pwrites=()):
        S.op(eng, lambda e: e.tensor_tensor(out=out, in0=in0, in1=in1, op=op), reads=reads, writes=writes, pwrites=pwrites)

    def ts(eng, out, in0, s1, op0, reads, writes=(), pwrites=(), s2=None, op1=None):
        if op1 is None:
            S.op(eng, lambda e: e.tensor_scalar(out=out, in0=in0, scalar1=s1, scalar2=None, op0=op0),
                 reads=reads, writes=writes, pwrites=pwrites)
        else:
            S.op(eng, lambda e: e.tensor_scalar(out=out, in0=in0, scalar1=s1, scalar2=s2, op0=op0, op1=op1),
                 reads=reads, writes=writes, pwrites=pwrites)

    def stt(eng, out, in0, scalar, in1, op0, op1, reads, writes=(), pwrites=()):
        S.op(eng, lambda e: e.scalar_tensor_tensor(out=out, in0=in0, scalar=scalar, in1=in1, op0=op0, op1=op1),
             reads=reads, writes=writes, pwrites=pwrites)

    def cp(eng, out, in_, reads, writes=(), pwrites=()):
        S.op(eng, lambda e: e.tensor_copy(out=out, in_=in_), reads=reads, writes=writes, pwrites=pwrites)

    def mm(out, lhsT, rhs, start, stop, reads, writes=(), pwrites=()):
        S.op("pe", lambda e: e.matmul(out, lhsT=lhsT, rhs=rhs, start=start, stop=stop),
             reads=reads, writes=writes, pwrites=pwrites)

    def mmgroup(ps, parts, reads):
        out = parts[0][2]
        n = len(parts)
        for i, (l_, r_, o_) in enumerate(parts):
            if i == 0:
                mm(o_, l_, r_, True, n == 1, reads, writes=[ps.b])
            else:
                mm(o_, l_, r_, False, i == n - 1, reads, pwrites=[ps.b])

    def rsqrt_from(out_t, ssq_ap, ssq_reads, n, writes_b):
        act(out_t, ssq_ap, AF.Sqrt, ssq_reads, writes=[writes_b], bias=EPS, scale=1.0 / n)
        S.op("dve", lambda e: e.reciprocal(out=out_t, in_=out_t), reads=[writes_b], writes=[writes_b])

    vecs = alloc([128, L * NVL], F32)
    modT = alloc([128, L * 96 * 2], F32)
    dv = alloc([128, L * 2 * 6 * 16], F32)
    clam = alloc([128, L * 16], F32)
    clam2 = alloc([128, L * 16], F32)
    st_sb = alloc([128, L * 16], F32)
    ones_bf = alloc([128, 128], BF16)
    rmat = alloc([128, 256], F32)
    nst_sb = alloc([128, L * 2 * 8 * 4], F32)
    epsb = alloc([128, 1], F32)

    def V(l, name, i=0, n=1):
        o = l * NVL + VO[name] + i
        return vecs.t[:, o:o + n]

    def DV(l, ci, which, kc):
        o = ((l * 2 + ci) * 6 + which) * 16 + kc
        return dv.t[:, o:o + 1]

    S.dma("sp", vecs.t[:, :], vecs_in, writes=[vecs.b])
    S.dma("sp", rmat.t[:, :], rmat_in, writes=[rmat.b])
    S.dma("sp", st_sb.t[:, :], st_in, writes=[st_sb.b])
    S.op("pool", lambda e: e.memset(ones_bf.t[:, :], 1.0), writes=[ones_bf.b])
    S.op("pool", lambda e: e.memset(nst_sb.t[:, :], 0.0), writes=[nst_sb.b])
    S.barrier()

    class PanelStream:
        def __init__(self, nk, ncols_max, kstage):
            self.nk, self.ncm, self.kst = nk, ncols_max, kstage
            self.stg = [alloc([128, kstage, ncols_max], F32) for _ in range(4)]
            self.pan = [alloc([128, nk, ncols_max], BF16) for _ in range(2)]
            self.si = 0
            self.pi = 0
            self.ci = 0

        def stream(self, specs):
            nxt = self.load(*specs[0])
            for i in range(len(specs)):
                cur = nxt
                if i + 1 < len(specs):
                    nxt = self.load(*specs[i + 1])
                yield i, cur

        def load(self, src2d, ncols):
            pan = self.pan[self.pi % 2]
            self.pi += 1
            k0 = 0
            first = True
            while k0 < self.nk:
                kn = min(self.kst, self.nk - k0)
                stg = self.stg[self.si % 4]
                self.si += 1
                S.dma("sp", stg.t[:, 0:kn, 0:ncols],
                      src2d[k0 * 128:(k0 + kn) * 128, :].rearrange("(kc p) n -> p kc n", p=128),
                      writes=[stg.b])
                kw = {"writes": [pan.b]} if first else {"pwrites": [pan.b]}
                first = False
                if self.ci % 2 == 0:
                    cp("dve", pan.t[:, k0:k0 + kn, 0:ncols], stg.t[:, 0:kn, 0:ncols], [stg.b], **kw)
                else:
                    act(pan.t[:, k0:k0 + kn, 0:ncols], stg.t[:, 0:kn, 0:ncols], AF.Copy, [stg.b], **kw)
                self.ci += 1
                k0 += kn
            return pan

    m0 = mark()
    cond_sb = alloc([128, KC * 2], F32)
    cond_bf = alloc([128, KC, 2], BF16)
    S.dma("sp", cond_sb.t[:, :], cond_in, writes=[cond_sb.b])
    act(cond_bf.t[:, :, :], cond_sb.t[:, :].rearrange("p (k c) -> p k c", c=2), AF.Silu, [cond_sb.b], writes=[cond_bf.b])
    ps_w = PanelStream(KC, 512, 4)
    p0_specs = [(w_mod[l, :, pnl * 512:(pnl + 1) * 512], 512) for l in range(L) for pnl in range(24)]
    p0_iter = ps_w.stream(p0_specs)
    for l in range(L):
        pst = psum[l % 2]
        for pnl in range(24):
            _, pan = next(p0_iter)
            for c4 in range(4):
                j = pnl * 4 + c4
                parts = [(pan.t[:, kc, c4 * 128:(c4 + 1) * 128], cond_bf.t[:, kc, :], pst.t[:, 2 * j:2 * j + 2]) for kc in range(KC)]
                if j == 0:
                    mmgroup(pst, parts, [pan.b, cond_bf.b])
                else:
                    for i, (l_, r_, o_) in enumerate(parts):
                        mm(o_, l_, r_, i == 0, i == KC - 1, [pan.b, cond_bf.b], pwrites=[pst.b])
        tt("dve", modT.t[:, l * 192:(l + 1) * 192].rearrange("p (j c) -> p j c", c=2),
           pst.t[:, 0:192].rearrange("p (j c) -> p j c", c=2),
           V(l, "bmod", 0, 96).unsqueeze(2).to_broadcast([128, 96, 2]), ALU.add,
           [pst.b, vecs.b], pwrites=[modT.b])
    for l in range(L):
        for ci in range(2):
            def modcol(k6):
                return modT.t[:, l * 192:(l + 1) * 192].rearrange("p (j c) -> p j c", c=2)[:, k6 * 16:(k6 + 1) * 16, ci]

            def dvc(which):
                o = ((l * 2 + ci) * 6 + which) * 16
                return dv.t[:, o:o + 16]
            stt("dve", dvc(0), modcol(1), 1.0, V(l, "gpm", 0, 16), ALU.add, ALU.mult, [modT.b, vecs.b], pwrites=[dv.b])
            cp("dve", dvc(1), modcol(0), [modT.b], pwrites=[dv.b])
            tt("dve", dvc(2), modcol(2), V(l, "gpom", 0, 16), ALU.mult, [modT.b, vecs.b], pwrites=[dv.b])
            stt("dve", dvc(3), modcol(4), 1.0, V(l, "gpf", 0, 16), ALU.add, ALU.mult, [modT.b, vecs.b], pwrites=[dv.b])
            cp("dve", dvc(4), modcol(3), [modT.b], pwrites=[dv.b])
            tt("dve", dvc(5), modcol(5), V(l, "gpof", 0, 16), ALU.mult, [modT.b, vecs.b], pwrites=[dv.b])
    for l in range(L):
        e_ = clam2.t[:, l * 16:(l + 1) * 16]
        c_ = clam.t[:, l * 16:(l + 1) * 16]
        act(e_, V(l, "llam", 0, 16), AF.Exp, [vecs.b], pwrites=[clam2.b], scale=-1.0)
        ts("dve", c_, e_, -0.25, ALU.mult, [clam2.b], pwrites=[clam.b], s2=1.0 / 3.0, op1=ALU.add)
        tt("dve", c_, c_, e_, ALU.mult, [clam.b, clam2.b], writes=[clam.b])
        ts("dve", c_, c_, -0.5, ALU.add, [clam.b], writes=[clam.b])
        tt("dve", c_, c_, e_, ALU.mult, [clam.b, clam2.b], writes=[clam.b])
        ts("dve", c_, c_, 1.0, ALU.add, [clam.b], writes=[clam.b])
        tt("dve", c_, c_, e_, ALU.mult, [clam.b, clam2.b], writes=[clam.b])
        ts("dve", c_, c_, -8.0, ALU.mult, [clam.b], writes=[clam.b])
        ts("dve", e_, c_, 2.0, ALU.mult, [clam.b], writes=[clam2.b])
    release(m0)

    class Group:
        pass

    def mkgroup(name, ci, T, nseq, Sq, cache, x_in, y_out):
        g = Group()
        g.name, g.ci, g.T, g.nseq, g.S, g.cache = name, ci, T, nseq, Sq, cache
        g.Tk = T + (cache if nseq == 1 else 0)
        g.x_in, g.y_out = x_in, y_out
        g.xT = dscr(name + "_xT", [D, T], F32)
        g.H = dscr(name + "_H", [D, T], BF16)
        g.QA = dscr(name + "_QA", [4, 128, T], BF16)
        g.KA = dscr(name + "_KA", [2, 128, g.Tk], BF16)
        g.VA = dscr(name + "_VA", [g.Tk, 256], BF16)
        g.QN = dscr(name + "_QN", [4, 128, T], BF16)
        g.QR = dscr(name + "_QR", [4, 64, T], BF16)
        g.KN = dscr(name + "_KN", [4, 128, g.Tk], BF16)
        g.KR = dscr(name + "_KR", [64, g.Tk], BF16)
        g.VM = dscr(name + "_VM", [g.Tk, 512], BF16)
        g.CKV = dscr(name + "_CKV", [512, T], F32)
        g.XR = dscr(name + "_XR", [1024, T], F32)
        g.GR = dscr(name + "_GR", [1024, T], F32)
        g.MIX = dscr(name + "_MIX", [D, T], BF16)
        g.M = dscr(name + "_M", [D, T], F32)
        g.ACT = dscr(name + "_ACT", [DFF, T], BF16)
        for nm in ("xT", "H", "QA", "KA", "VA", "QN", "QR", "KN", "KR", "VM", "CKV", "XR", "GR", "MIX", "M", "ACT"):
            setattr(g, "b_" + nm, Buf())
        g.blocks = [(s, min(2048, T - s)) for s in range(0, T, 2048)]
        return g

    GP = mkgroup("p", 0, SP_T, 4, 256, 0, xp_in, yp_out)
    GS = mkgroup("s", 1, SS_T, 1, SS_T, PAST, xs_in, ys_out)

    rr = {"acc": 0, "aux": 0}

    def acc_ps():
        rr["acc"] += 1
        return psum[rr["acc"] % 4]

    def aux_ps():
        rr["aux"] += 1
        return psum[4 + rr["aux"] % 4]

    def prenorm_tile(g, l, which_a, xt, c0, tmp, sq, rstd, hstage):
        act(sq.t[:, :, :], xt.t[:, :, :], AF.Square, [xt.b], writes=[sq.b])
        pz = aux_ps()
        mmgroup(pz, [(ones_bf.t[:, :], sq.t[:, kc, :], pz.t[:, :]) for kc in range(KC)], [ones_bf.b, sq.b])
        rsqrt_from(rstd.t[:, :], pz.t[:, :], [pz.b], D, rstd.b)
        for kc in range(KC):
            tm = tmp[kc % len(tmp)]
            stt("dve", tm.t[:, :], xt.t[:, kc, :], DV(l, g.ci, which_a, kc), rstd.t[:, :], ALU.mult, ALU.mult,
                [xt.b, rstd.b, dv.b], writes=[tm.b])
            if kc == 0:
                act(hstage.t[:, kc, :], tm.t[:, :], AF.Identity, [tm.b, dv.b], writes=[hstage.b],
                    bias=DV(l, g.ci, which_a + 1, kc))
            else:
                act(hstage.t[:, kc, :], tm.t[:, :], AF.Identity, [tm.b, dv.b], pwrites=[hstage.b],
                    bias=DV(l, g.ci, which_a + 1, kc))
        S.dma("act", g.H[:, c0:c0 + 512].rearrange("(kc p) t -> p kc t", p=128), hstage.t[:, :, :],
              reads=[hstage.b], pwrites=[g.b_H])

    def phase_P1(g, l):
        m = mark()
        xts = [alloc([128, KC, 512], F32) for _ in range(2)]
        tmp = [alloc([128, 512], F32) for _ in range(3)]
        sq = alloc([128, KC, 512], BF16)
        rstd = alloc([128, 512], F32)
        hst = [alloc([128, KC, 512], BF16) for _ in range(2)]
        for ti in range(g.T // 512):
            xt = xts[ti % 2]
            S.dma("sp", xt.t[:, :, :], g.x_in[:, ti * 512:(ti + 1) * 512].rearrange("(kc p) t -> p kc t", p=128), writes=[xt.b])
            S.dma("act", g.xT[:, ti * 512:(ti + 1) * 512].rearrange("(kc p) t -> p kc t", p=128), xt.t[:, :, :],
                  reads=[xt.b], pwrites=[g.b_xT])
            prenorm_tile(g, l, 0, xt, ti * 512, tmp, sq, rstd, hst[ti % 2])
        release(m)

    P2_PANELS = [
        (0, 512, [(0, 128, "qa", 0), (128, 128, "qa", 1), (256, 128, "qa", 2), (384, 128, "qa", 3)]),
        (512, 512, [(0, 128, "ka", 0), (128, 128, "ka", 1), (256, 256, "va", 0)]),
        (1024, 512, [(i * 128, 128, "xr", i) for i in range(4)]),
        (1536, 512, [(i * 128, 128, "xr", 4 + i) for i in range(4)]),
        (2048, 512, [(i * 128, 128, "gr", i) for i in range(4)]),
        (2560, 512, [(i * 128, 128, "gr", 4 + i) for i in range(4)]),
        (3072, 512, [(0, 128, "qn", 0), (128, 64, "qr", 0), (192, 128, "qn", 1), (320, 64, "qr", 1), (384, 128, "qn", 2)]),
        (3584, 512, [(0, 64, "qr", 2), (64, 128, "qn", 3), (192, 64, "qr", 3), (256, 128, "ckv", 0), (384, 128, "ckv", 1)]),
        (4096, 320, [(0, 128, "ckv", 2), (128, 128, "ckv", 3), (256, 64, "kr", 0)]),
    ]

    def phase_P2(g, l, blk):
        b0, Tb = blk
        m = mark()
        hT = alloc([128, KC, Tb], BF16)
        ps_w = PanelStream(KC, 512, 4)
        hTb = [Buf() for _ in range(Tb // 512)]
        ev32 = [alloc([128, 512], F32) for _ in range(3)]
        evbf = [alloc([128, 512], BF16) for _ in range(3)]
        sqb = [alloc([128, 512], BF16) for _ in range(2)]
        rstd = [alloc([128, 512], F32) for _ in range(2)]
        rp = [alloc([128, 2, 512], F32) for _ in range(2)]
        t1 = [alloc([128, 512], F32) for _ in range(2)]
        cnt = {"e": 0, "s": 0, "r": 0}
        sample = g.nseq == 1
        koff = g.cache if sample else 0

        def rope(src32, width, c0, outbf):
            r_ = rp[cnt["r"] % 2]
            t_ = t1[cnt["r"] % 2]
            cnt["r"] += 1
            tab = rope128_in if width == 128 else rope64_in
            S.dma("sp", r_.t[0:width, :, :], tab[:, :, c0:c0 + 512], writes=[r_.b])
            pr = aux_ps()
            rm = rmat.t[:, 0:128] if width == 128 else rmat.t[0:64, 128:192]
            mmgroup(pr, [(rm, src32.t[0:width, :], pr.t[0:width, :])], [rmat.b, src32.b])
            tt("dve", t_.t[0:width, :], pr.t[0:width, :], r_.t[0:width, 1, :], ALU.mult, [pr.b, r_.b], writes=[t_.b])
            tt("pool", src32.t[0:width, :], src32.t[0:width, :], r_.t[0:width, 0, :], ALU.mult, [src32.b, r_.b], writes=[src32.b])
            tt("dve", outbf.t[0:width, :], src32.t[0:width, :], t_.t[0:width, :], ALU.add, [src32.b, t_.b], writes=[outbf.b])

        pend = {"f": None}

        def evac(ps, e32, eb, kind, idx, c0):
            if kind in ("xr", "gr", "ckv"):
                act(e32.t[:, :], ps.t[:, :], AF.Copy, [ps.b], writes=[e32.b])
                dst, bb = {"xr": (g.XR, g.b_XR), "gr": (g.GR, g.b_GR), "ckv": (g.CKV, g.b_CKV)}[kind]
                S.dma("act", dst[idx * 128:(idx + 1) * 128, c0:c0 + 512], e32.t[:, :], reads=[e32.b], pwrites=[bb])
            elif kind in ("qa", "ka"):
                sq_ = sqb[cnt["s"] % 2]
                rs_ = rstd[cnt["s"] % 2]
                cnt["s"] += 1
                act(sq_.t[:, :], ps.t[:, :], AF.Square, [ps.b], writes=[sq_.b])
                pz = aux_ps()
                mmgroup(pz, [(ones_bf.t[:, :], sq_.t[:, :], pz.t[:, :])], [ones_bf.b, sq_.b])
                rsqrt_from(rs_.t[:, :], pz.t[:, :], [pz.b], 128, rs_.b)
                gname = "gq" if kind == "qa" else "gk"
                stt("dve", e32.t[:, :], ps.t[:, :], V(l, gname), rs_.t[:, :], ALU.mult, ALU.mult,
                    [ps.b, rs_.b, vecs.b], writes=[e32.b])
                if kind == "ka" and not sample:
                    S.dma("act", nk_out[l, idx, :, c0:c0 + 512], e32.t[:, :], reads=[e32.b])
                if sample:
                    rope(e32, 128, c0, eb)
                else:
                    cp("pool", eb.t[:, :], e32.t[:, :], [e32.b], writes=[eb.b])
                if kind == "qa":
                    S.dma("act", g.QA[idx, :, c0:c0 + 512], eb.t[:, :], reads=[eb.b], pwrites=[g.b_QA])
                else:
                    S.dma("act", g.KA[idx, :, koff + c0:koff + c0 + 512], eb.t[:, :], reads=[eb.b], pwrites=[g.b_KA])
            elif kind == "qn":
                act(eb.t[:, :], ps.t[:, :], AF.Copy, [ps.b], writes=[eb.b])
                S.dma("act", g.QN[idx, :, c0:c0 + 512], eb.t[:, :], reads=[eb.b], pwrites=[g.b_QN])
            elif kind in ("qr", "kr"):
                if sample:
                    act(e32.t[0:64, :], ps.t[0:64, :], AF.Copy, [ps.b], writes=[e32.b])
                    rope(e32, 64, c0, eb)
                else:
                    if kind == "kr":
                        act(e32.t[0:64, :], ps.t[0:64, :], AF.Copy, [ps.b], writes=[e32.b])
                        S.dma("act", nkr_out[l, :, c0:c0 + 512], e32.t[0:64, :], reads=[e32.b])
                        cp("dve", eb.t[0:64, :], e32.t[0:64, :], [e32.b], writes=[eb.b])
                    else:
                        act(eb.t[0:64, :], ps.t[0:64, :], AF.Copy, [ps.b], writes=[eb.b])
                if kind == "qr":
                    S.dma("act", g.QR[idx, :, c0:c0 + 512], eb.t[0:64, :], reads=[eb.b], pwrites=[g.b_QR])
                else:
                    S.dma("act", g.KR[:, koff + c0:koff + c0 + 512], eb.t[0:64, :], reads=[eb.b], pwrites=[g.b_KR])


        for pi_, pan in ps_w.stream([(w_in[l, :, c0_:c0_ + nc_], nc_) for (c0_, nc_, _) in P2_PANELS]):
            (col0, ncols, chunks) = P2_PANELS[pi_]
            if pi_ == 0:
                for ti in range(Tb // 512):
                    S.dma("sp", hT.t[:, :, ti * 512:(ti + 1) * 512],
                          g.H[:, b0 + ti * 512:b0 + (ti + 1) * 512].rearrange("(kc p) t -> p kc t", p=128),
                          reads=[g.b_H], writes=[hTb[ti]])
            for ti in range(Tb // 512):
                c0 = b0 + ti * 512
                for (off, width, kind, idx) in chunks:
                    if kind == "va":
                        if pend["f"] is not None:
                            pend["f"]()
                            pend["f"] = None
                        for j in range(4):
                            ps = acc_ps()
                            mmgroup(ps, [(hT.t[:, kc, ti * 512 + j * 128: ti * 512 + (j + 1) * 128], pan.t[:, kc, off:off + 256], ps.t[:, 0:256])
                                         for kc in range(KC)], [hTb[ti], pan.b])
                            eb = evbf[cnt["e"] % 3]
                            e32 = ev32[cnt["e"] % 3]
                            cnt["e"] += 1
                            r0 = c0 + j * 128
                            if not sample:
                                act(e32.t[:, 0:256], ps.t[:, 0:256], AF.Copy, [ps.b], writes=[e32.b])
                                S.dma("act", nv_out[l, r0:r0 + 128, :], e32.t[:, 0:256], reads=[e32.b])
                                cp("dve", eb.t[:, 0:256], e32.t[:, 0:256], [e32.b], writes=[eb.b])
                            else:
                                act(eb.t[:, 0:256], ps.t[:, 0:256], AF.Copy, [ps.b], writes=[eb.b])
                            S.dma("act", g.VA[koff + r0:koff + r0 + 128, :], eb.t[:, 0:256], reads=[eb.b], pwrites=[g.b_VA])
                        continue
                    ps = acc_ps()
                    mmgroup(ps, [(pan.t[:, kc, off:off + width], hT.t[:, kc, ti * 512:(ti + 1) * 512], ps.t[0:width, :])
                                 for kc in range(KC)], [hTb[ti], pan.b])
                    e32 = ev32[cnt["e"] % 3]
                    eb = evbf[cnt["e"] % 3]
                    cnt["e"] += 1
                    fn_ = (lambda ps=ps, e32=e32, eb=eb, kind=kind, idx=idx, c0=c0: evac(ps, e32, eb, kind, idx, c0))
                    if pend["f"] is not None:
                        pend["f"]()
                    pend["f"] = fn_
        if pend["f"] is not None:
            pend["f"]()
            pend["f"] = None
        release(m)


    def phase_P2C(g, l):
        m = mark()
        a32 = alloc([128, 2, 512], F32)
        abf = alloc([128, 2, 512], BF16)
        S.dma("sp", a32.t[:, :, :], ck_in[l].rearrange("h d t -> d h t"), writes=[a32.b])
        cp("dve", abf.t[:, :, :], a32.t[:, :, :], [a32.b], writes=[abf.b])
        S.dma("act", g.KA[:, :, 0:PAST].rearrange("h d t -> d h t"), abf.t[:, :, :], reads=[abf.b], pwrites=[g.b_KA])
        v32 = alloc([128, 4, 256], F32)
        vbf = alloc([128, 4, 256], BF16)
        S.dma("sp", v32.t[:, :, :], cv_in[l].rearrange("(kb p) d -> p kb d", p=128), writes=[v32.b])
        cp("dve", vbf.t[:, :, :], v32.t[:, :, :], [v32.b], writes=[vbf.b])
        S.dma("act", g.VA[0:PAST, :].rearrange("(kb p) d -> p kb d", p=128), vbf.t[:, :, :], reads=[vbf.b], pwrites=[g.b_VA])
        r32 = alloc([64, 512], F32)
        rbf = alloc([64, 512], BF16)
        S.dma("sp", r32.t[:, :], ckr_in[l], writes=[r32.b])
        cp("dve", rbf.t[:, :], r32.t[:, :], [r32.b], writes=[rbf.b])
        S.dma("act", g.KR[:, 0:PAST], rbf.t[:, :], reads=[rbf.b], pwrites=[g.b_KR])
        release(m)

    def phase_P2M(g, l):
        m = mark()
        sample = g.nseq == 1
        koff = g.cache if sample else 0
        w32 = alloc([128, 4, 512], F32)
        wuk = alloc([128, 4, 512], BF16)
        wuv = alloc([128, 4, 512], BF16)
        S.dma("sp", w32.t[:, :, :], w_uk[l].rearrange("(rc p) n -> p rc n", p=128), writes=[w32.b])
        cp("pool", wuk.t[:, :, :], w32.t[:, :, :], [w32.b], writes=[wuk.b])
        S.dma("sp", w32.t[:, :, :], w_uv[l].rearrange("(rc p) n -> p rc n", p=128), writes=[w32.b])
        cp("pool", wuv.t[:, :, :], w32.t[:, :, :], [w32.b], writes=[wuv.b])
        c32 = [alloc([128, 4, 512], F32) for _ in range(2)]
        cbf = [alloc([128, 4, 512], BF16) for _ in range(2)]
        sq = alloc([128, 4, 512], BF16)
        rstd = alloc([128, 512], F32)
        evb = [alloc([128, 512], BF16) for _ in range(3)]
        ne = 0
        tiles = []
        if sample:
            tiles.append(("cache", 0))
        tiles += [("new", ti) for ti in range(g.T // 512)]
        for n, (kind, ti) in enumerate(tiles):
            c_ = c32[n % 2]
            b_ = cbf[n % 2]
            if kind == "cache":
                S.dma("sp", c_.t[:, :, :], cckv_in[l].rearrange("(rc p) t -> p rc t", p=128), writes=[c_.b])
                cp("dve", b_.t[:, :, :], c_.t[:, :, :], [c_.b], writes=[b_.b])
                kc0 = 0
            else:
                c0 = ti * 512
                kc0 = koff + c0
                S.dma("sp", c_.t[:, :, :], g.CKV[:, c0:c0 + 512].rearrange("(rc p) t -> p rc t", p=128), reads=[g.b_CKV], writes=[c_.b])
                act(sq.t[:, :, :], c_.t[:, :, :], AF.Square, [c_.b], writes=[sq.b])
                pz = aux_ps()
                mmgroup(pz, [(ones_bf.t[:, :], sq.t[:, rc, :], pz.t[:, :]) for rc in range(4)], [ones_bf.b, sq.b])
                rsqrt_from(rstd.t[:, :], pz.t[:, :], [pz.b], 512, rstd.b)
                for rc in range(4):
                    stt("dve", c_.t[:, rc, :], c_.t[:, rc, :], V(l, "gkv", rc), rstd.t[:, :], ALU.mult, ALU.mult,
                        [c_.b, rstd.b, vecs.b], writes=[c_.b])
                if not sample:
                    S.dma("act", nckv_out[l, :, c0:c0 + 512].rearrange("(rc p) t -> p rc t", p=128), c_.t[:, :, :], reads=[c_.b])
                cp("pool", b_.t[:, :, :], c_.t[:, :, :], [c_.b], writes=[b_.b])
            for h in range(4):
                ps = acc_ps()
                mmgroup(ps, [(wuk.t[:, rc, h * 128:(h + 1) * 128], b_.t[:, rc, :], ps.t[:, :]) for rc in range(4)], [wuk.b, b_.b])
                e_ = evb[ne % 3]
                ne += 1
                act(e_.t[:, :], ps.t[:, :], AF.Copy, [ps.b], writes=[e_.b])
                S.dma("act", g.KN[h, :, kc0:kc0 + 512], e_.t[:, :], reads=[e_.b], pwrites=[g.b_KN])
            for j in range(4):
                ps = acc_ps()
                mmgroup(ps, [(b_.t[:, rc, j * 128:(j + 1) * 128], wuv.t[:, rc, :], ps.t[:, :]) for rc in range(4)], [wuv.b, b_.b])
                e_ = evb[ne % 3]
                ne += 1
                act(e_.t[:, :], ps.t[:, :], AF.Copy, [ps.b], writes=[e_.b])
                S.dma("act", g.VM[kc0 + j * 128:kc0 + (j + 1) * 128, :], e_.t[:, :], reads=[e_.b], pwrites=[g.b_VM])
        release(m)

    def phase_P2L(g, l):
        m = mark()
        sample = g.nseq == 1
        T, ns, Sq = g.T, g.nseq, g.S
        W = Sq + 3
        g32 = alloc([128, 32, 128], F32)
        gw = alloc([128, 32, 128], BF16)
        S.dma("sp", g32.t[:, 0:16, :], lru_w_a[l].rearrange("d n k j -> k (d n) j"), writes=[g32.b])
        S.dma("sp", g32.t[:, 16:32, :], lru_w_x[l].rearrange("d n k j -> k (d n) j"), pwrites=[g32.b])
        cp("pool", gw.t[:, :, :], g32.t[:, :, :], [g32.b], writes=[gw.b])
        xpad = alloc([128, ns, W], F32)
        xc = alloc([128, ns, Sq], F32)
        xcb = alloc([128, ns, Sq], BF16)
        A = alloc([128, ns, Sq], F32)
        U = alloc([128, ns, Sq], F32)
        Hf = alloc([128, ns, Sq], F32)
        Hb = alloc([128, ns, Sq], F32)
        G = alloc([128, ns, Sq], F32)
        yb = alloc([128, ns, Sq], BF16)
        gt = [alloc([128, 512], F32) for _ in range(2)]
        S.op("pool", lambda e: e.memset(xpad.t[:, :, :], 0.0), writes=[xpad.b])
        S.barrier()
        flat = lambda t_: t_.t[:, :, :].rearrange("p s t -> p (s t)")
        for ch in range(8):
            S.dma("sp", xpad.t[:, :, 1:1 + Sq], g.XR[ch * 128:(ch + 1) * 128, :].rearrange("p (s t) -> p s t", s=ns),
                  reads=[g.b_XR, xpad.b], writes=[xpad.b])
            S.dma("sp", flat(G), g.GR[ch * 128:(ch + 1) * 128, :], reads=[g.b_GR], writes=[G.b])
            ts("dve", xc.t[:, :, :], xpad.t[:, :, 0:Sq], V(l, "lcw", 0 * 8 + ch), ALU.mult, [xpad.b, vecs.b], writes=[xc.b],
               s2=V(l, "lcb", ch), op1=ALU.add)
            for j in (1, 2, 3):
                stt("dve", xc.t[:, :, :], xpad.t[:, :, j:j + Sq], V(l, "lcw", j * 8 + ch), xc.t[:, :, :], ALU.mult, ALU.add,
                    [xpad.b, xc.b, vecs.b], writes=[xc.b])
            act(xcb.t[:, :, :], xc.t[:, :, :], AF.Copy, [xc.b], writes=[xcb.b])
            act(flat(G), flat(G), AF.Gelu, [G.b], writes=[G.b])
            for d in range(2):
                H = Hf if d == 0 else Hb
                lo = l * 16 + d * 8 + ch
                for ti in range(T // 512):
                    sl = slice(ti * 512, (ti + 1) * 512)
                    pa = acc_ps()
                    mmgroup(pa, [(gw.t[:, d * 8 + ch, :], flat(xcb)[:, sl], pa.t[:, :])], [gw.b, xcb.b])
                    pi = acc_ps()
                    mmgroup(pi, [(gw.t[:, 16 + d * 8 + ch, :], flat(xcb)[:, sl], pi.t[:, :])], [gw.b, xcb.b])
                    act(flat(A)[:, sl], pa.t[:, :], AF.Sigmoid, [pa.b, vecs.b], pwrites=[A.b], bias=V(l, "lba", d * 8 + ch))
                    act(flat(U)[:, sl], pi.t[:, :], AF.Sigmoid, [pi.b, vecs.b], pwrites=[U.b], bias=V(l, "lbx", d * 8 + ch))
                act(flat(H), flat(A), AF.Exp, [A.b, clam2.b], writes=[H.b], scale=clam2.t[:, lo:lo + 1])
                act(flat(A), flat(A), AF.Exp, [A.b, clam.b], writes=[A.b], scale=clam.t[:, lo:lo + 1])
                act(flat(H), flat(H), AF.Identity, [H.b], writes=[H.b], bias=1.0, scale=-1.0)
                act(flat(H), flat(H), AF.Sqrt, [H.b], writes=[H.b])
                tt("dve", flat(U), flat(U), flat(xc), ALU.mult, [U.b, xc.b], writes=[U.b])
                tt("dve", flat(U), flat(U), flat(H), ALU.mult, [U.b, H.b], writes=[U.b])
                for s_ in range(ns):
                    if sample:
                        so = l * 16 + d * 8 + ch
                        init = st_sb.t[:, so:so + 1]
                        rd = [A.b, U.b, st_sb.b]
                    else:
                        init = 0.0
                        rd = [A.b, U.b]
                    if d == 0:
                        S.op("dve", lambda e, s_=s_, init=init, H=H: e.tensor_tensor_scan(
                            out=H.t[:, s_, :], data0=A.t[:, s_, :], data1=U.t[:, s_, :], initial=init, op0=ALU.mult, op1=ALU.add),
                            reads=rd, **({"writes": [H.b]} if s_ == 0 else {"pwrites": [H.b]}))
                    else:
                        rv = lambda t_, s_=s_: bass.AP(t_.t, s_ * Sq + Sq - 1, [[ns * Sq, 128], [-1, Sq]])
                        S.op("dve", lambda e, rv=rv, init=init, H=H: e.tensor_tensor_scan(
                            out=rv(H), data0=rv(A), data1=rv(U), initial=init, op0=ALU.mult, op1=ALU.add),
                            reads=rd, **({"writes": [H.b]} if s_ == 0 else {"pwrites": [H.b]}))
                if not sample:
                    o = ((l * 2 + d) * 8 + ch) * 4
                    col = Sq - 1 if d == 0 else 0
                    cp("pool", nst_sb.t[:, o:o + 4], H.t[:, :, col], [H.b], pwrites=[nst_sb.b])
            tt("dve", flat(Hf), flat(Hf), flat(Hb), ALU.add, [Hf.b, Hb.b], writes=[Hf.b])
            tt("dve", flat(yb), flat(Hf), flat(G), ALU.mult, [Hf.b, G.b], writes=[yb.b])
            S.dma("act", g.MIX[512 + ch * 128:512 + (ch + 1) * 128, :], flat(yb), reads=[yb.b], pwrites=[g.b_MIX])
        release(m)

    def phase_P3(g, l):
        m = mark()
        sample = g.nseq == 1
        Tk = g.Tk if sample else g.S
        nkb = Tk // 128
        QW = 512 if sample else 256
        nqg = g.S // QW
        kT = [alloc([128, Tk], BF16) for _ in range(2)]
        krT = alloc([64, g.Tk], BF16)
        vv = [alloc([128, nkb, 128], BF16) for _ in range(2)]
        qT = [alloc([128, QW], BF16) for _ in range(2)]
        qrT = [alloc([64, QW], BF16) for _ in range(2)]
        pT = [alloc([128, QW], BF16) for _ in range(3)]
        rz = alloc([128, QW], F32)
        ob = [alloc([128, QW], BF16) for _ in range(2)]
        S.dma("sp", krT.t[:, :], g.KR[:, :], reads=[g.b_KR], writes=[krT.b])
        n = 0
        nq = 0
        npt = 0
        for s_ in range(g.nseq):
            k0 = 0 if sample else s_ * g.S
            for h in range(8):
                mla = h >= 4
                hh = h - 4 if mla else h
                kt_, v_ = kT[n % 2], vv[n % 2]
                n += 1
                if mla:
                    S.dma("sp", kt_.t[:, :], g.KN[hh, :, k0:k0 + Tk], reads=[g.b_KN], writes=[kt_.b])
                    S.dma("sp", v_.t[:, :, :], g.VM[k0:k0 + Tk, hh * 128:(hh + 1) * 128].rearrange("(kb p) d -> p kb d", p=128),
                          reads=[g.b_VM], writes=[v_.b])
                    scale = 192 ** -0.5
                else:
                    S.dma("sp", kt_.t[:, :], g.KA[hh // 2, :, k0:k0 + Tk], reads=[g.b_KA], writes=[kt_.b])
                    S.dma("sp", v_.t[:, :, :], g.VA[k0:k0 + Tk, (hh // 2) * 128:(hh // 2 + 1) * 128].rearrange("(kb p) d -> p kb d", p=128),
                          reads=[g.b_VA], writes=[v_.b])
                    scale = 128 ** -0.5
                for qg in range(nqg):
                    q0 = s_ * g.S + qg * QW
                    q_, qr_ = qT[nq % 2], qrT[nq % 2]
                    o_ = ob[nq % 2]
                    nq += 1
                    if mla:
                        S.dma("sp", q_.t[:, :], g.QN[hh, :, q0:q0 + QW], reads=[g.b_QN], writes=[q_.b])
                        S.dma("sp", qr_.t[:, :], g.QR[hh, :, q0:q0 + QW], reads=[g.b_QR], writes=[qr_.b])
                    else:
                        S.dma("sp", q_.t[:, :], g.QA[hh, :, q0:q0 + QW], reads=[g.b_QA], writes=[q_.b])
                    pO = psum[2 + (nq % 2) * 1]
                    pZ = psum[4 + (nq % 2) * 1]

                    def s_mm(kb):
                        ps = psum[kb % 2]
                        ksl = slice(kb * 128, (kb + 1) * 128)
                        if mla:
                            kr0 = k0 + kb * 128
                            mmgroup(ps, [(kt_.t[:, ksl], q_.t[:, :], ps.t[:, 0:QW]),
                                         (krT.t[:, kr0:kr0 + 128], qr_.t[:, :], ps.t[:, 0:QW])], [kt_.b, q_.b, krT.b, qr_.b])
                        else:
                            mmgroup(ps, [(kt_.t[:, ksl], q_.t[:, :], ps.t[:, 0:QW])], [kt_.b, q_.b])
                        return ps
                    pend = s_mm(0)
                    for kb in range(nkb):
                        ps = pend
                        if kb + 1 < nkb:
                            pend = s_mm(kb + 1)
                        p_ = pT[npt % 3]
                        npt += 1
                        act(p_.t[:, :], ps.t[:, 0:QW], AF.Exp, [ps.b], writes=[p_.b], scale=scale)
                        if kb == 0:
                            mm(pO.t[:, 0:QW], v_.t[:, kb, :], p_.t[:, :], True, nkb == 1, [v_.b, p_.b], writes=[pO.b])
                            mm(pZ.t[:, 0:QW], ones_bf.t[:, :], p_.t[:, :], True, nkb == 1, [ones_bf.b, p_.b], writes=[pZ.b])
                        else:
                            mm(pO.t[:, 0:QW], v_.t[:, kb, :], p_.t[:, :], False, kb == nkb - 1, [v_.b, p_.b], pwrites=[pO.b])
                            mm(pZ.t[:, 0:QW], ones_bf.t[:, :], p_.t[:, :], False, kb == nkb - 1, [ones_bf.b, p_.b], pwrites=[pZ.b])
                    S.op("dve", lambda e, pZ=pZ: e.reciprocal(out=rz.t[:, :], in_=pZ.t[:, 0:QW]), reads=[pZ.b], writes=[rz.b])
                    tt("dve", o_.t[:, :], pO.t[:, 0:QW], rz.t[:, :], ALU.mult, [pO.b, rz.b], writes=[o_.b])
                    row = (1536 + hh * 128) if mla else hh * 128
                    S.dma("act", g.MIX[row:row + 128, q0:q0 + QW], o_.t[:, :], reads=[o_.b], pwrites=[g.b_MIX])
        release(m)

    def proj_to_M(g, l, blk, src_scr, b_src, nk, wsrc, kstage, pcols):
        b0, Tb = blk
        m = mark()
        aT = alloc([128, nk, Tb], BF16)
        aTb = [Buf() for _ in range(Tb // 512)]
        ps_w = PanelStream(nk, pcols, kstage)
        ev = [alloc([128, 512], F32) for _ in range(3)]
        ne = 0
        for pnl, pan in ps_w.stream([(wsrc[:, q_ * pcols:(q_ + 1) * pcols], pcols) for q_ in range(D // pcols)]):
            if pnl == 0:
                for ti in range(Tb // 512):
                    for k0_ in range(0, nk, 16):
                        kn_ = min(16, nk - k0_)
                        S.dma("sp", aT.t[:, k0_:k0_ + kn_, ti * 512:(ti + 1) * 512],
                              src_scr[k0_ * 128:(k0_ + kn_) * 128, b0 + ti * 512:b0 + (ti + 1) * 512].rearrange("(kc p) t -> p kc t", p=128),
                              reads=[b_src], **({"writes": [aTb[ti]]} if k0_ == 0 else {"pwrites": [aTb[ti]]}))
            for ti in range(Tb // 512):
                c0 = b0 + ti * 512
                for c in range(pcols // 128):
                    mc = pnl * (pcols // 128) + c
                    ps = acc_ps()
                    mmgroup(ps, [(pan.t[:, kc, c * 128:(c + 1) * 128], aT.t[:, kc, ti * 512:(ti + 1) * 512], ps.t[:, :])
                                 for kc in range(nk)], [aTb[ti], pan.b])
                    e_ = ev[ne % 3]
                    ne += 1
                    act(e_.t[:, :], ps.t[:, :], AF.Copy, [ps.b], writes=[e_.b])
                    S.dma("act", g.M[mc * 128:(mc + 1) * 128, c0:c0 + 512], e_.t[:, :], reads=[e_.b], pwrites=[g.b_M])
        release(m)

    def phase_update(g, l, which_g, next_norm):
        m = mark()
        xts = [alloc([128, KC, 512], F32) for _ in range(2)]
        mts = [alloc([128, KC, 512], F32) for _ in range(2)]
        tmp = [alloc([128, 512], F32) for _ in range(3)]
        sq = alloc([128, KC, 512], BF16)
        rstd = alloc([128, 512], F32)
        rstd2 = alloc([128, 512], F32)
        hst = [alloc([128, KC, 512], BF16) for _ in range(1)]
        for ti in range(g.T // 512):
            xt, mt = xts[ti % 2], mts[ti % 2]
            sl = slice(ti * 512, (ti + 1) * 512)
            S.dma("sp", xt.t[:, :, :], g.xT[:, sl].rearrange("(kc p) t -> p kc t", p=128), reads=[g.b_xT], writes=[xt.b])
            S.dma("sp", mt.t[:, :, :], g.M[:, sl].rearrange("(kc p) t -> p kc t", p=128), reads=[g.b_M], writes=[mt.b])
            act(sq.t[:, :, :], mt.t[:, :, :], AF.Square, [mt.b], writes=[sq.b])
            pz = aux_ps()
            mmgroup(pz, [(ones_bf.t[:, :], sq.t[:, kc, :], pz.t[:, :]) for kc in range(KC)], [ones_bf.b, sq.b])
            rsqrt_from(rstd.t[:, :], pz.t[:, :], [pz.b], D, rstd.b)
            for kc in range(KC):
                tm = tmp[kc % 3]
                stt("dve", tm.t[:, :], mt.t[:, kc, :], DV(l, g.ci, which_g, kc), rstd.t[:, :], ALU.mult, ALU.mult,
                    [mt.b, rstd.b, dv.b], writes=[tm.b])
                tt("pool" if kc % 4 == 3 else "dve", xt.t[:, kc, :], xt.t[:, kc, :], tm.t[:, :], ALU.add, [xt.b, tm.b], writes=[xt.b])
            if next_norm is None:
                S.dma("act", g.y_out[:, sl].rearrange("(kc p) t -> p kc t", p=128), xt.t[:, :, :], reads=[xt.b])
            else:
                S.dma("act", g.xT[:, sl].rearrange("(kc p) t -> p kc t", p=128), xt.t[:, :, :], reads=[xt.b], pwrites=[g.b_xT])
                prenorm_tile(g, next_norm[0], next_norm[1], xt, ti * 512, tmp, sq, rstd2, hst[0])
        release(m)

    def phase_P6(g, l, blk):
        b0, Tb = blk
        m = mark()
        sample = g.nseq == 1
        ns = 1 if sample else g.nseq
        Sq = Tb if sample else g.S
        W = Sq + 2
        hT = alloc([128, KC, Tb], BF16)
        halo = alloc([128, KC, 2], BF16)
        hTb = [Buf() for _ in range(Tb // 512)]
        S.op("pool", lambda e: e.memset(halo.t[:, :, :], 0.0), writes=[halo.b])
        has_halo = sample
        if sample:
            Hv = g.H.rearrange("(kc p) t -> p kc t", p=128)
            if b0 > 0:
                S.dma("sp", halo.t[:, :, 0:1], Hv[:, :, b0 - 1:b0], reads=[g.b_H, halo.b], writes=[halo.b], slow=True)
            if b0 + Tb < g.T:
                S.dma("sp", halo.t[:, :, 1:2], Hv[:, :, b0 + Tb:b0 + Tb + 1], reads=[g.b_H, halo.b], writes=[halo.b], slow=True)
        ps_w = PanelStream(KC, 512, 4)
        upad = [alloc([128, ns, W], F32) for _ in range(2)]
        cc = [alloc([128, ns, Sq], F32) for _ in range(2)]
        Gs = alloc([128, 4, Tb], BF16)
        ab = [alloc([128, Tb], BF16) for _ in range(2)]
        for u_ in upad:
            S.op("pool", lambda e, u_=u_: e.memset(u_.t[:, :, :], 0.0), writes=[u_.b])
        S.barrier()
        nu = 0
        def colbase_of(pp_):
            return (DFF if pp_ % 2 == 1 else 0) + (pp_ // 2) * 512
        for pp, pan in ps_w.stream([(w_up[l, :, colbase_of(q_):colbase_of(q_) + 512], 512) for q_ in range(22)]):
            if pp == 0:
                for ti in range(Tb // 512):
                    S.dma("sp", hT.t[:, :, ti * 512:(ti + 1) * 512],
                          g.H[:, b0 + ti * 512:b0 + (ti + 1) * 512].rearrange("(kc p) t -> p kc t", p=128),
                          reads=[g.b_H], writes=[hTb[ti]])
            isv = pp % 2 == 1
            cp4 = pp // 2
            for c in range(4):
                chn = cp4 * 4 + c + (NFC if isv else 0)
                u_ = upad[nu % 2]
                c_ = cc[nu % 2]
                a_ = ab[nu % 2]
                nu += 1
                for ti in range(Tb // 512):
                    ps = acc_ps()
                    mmgroup(ps, [(pan.t[:, kc, c * 128:(c + 1) * 128], hT.t[:, kc, ti * 512:(ti + 1) * 512], ps.t[:, :])
                                 for kc in range(KC)], [hTb[ti], pan.b])
                    if sample:
                        dstv = u_.t[:, 0, 1 + ti * 512:1 + (ti + 1) * 512]
                        srcv = ps.t[:, :]
                    else:
                        dstv = u_.t[:, 2 * ti:2 * ti + 2, 1:1 + Sq]
                        srcv = ps.t[:, :].rearrange("p (s t) -> p s t", s=2)
                    act(dstv, srcv, AF.Copy, [ps.b], pwrites=[u_.b])
                if has_halo:
                    ph = aux_ps()
                    mmgroup(ph, [(pan.t[:, kc, c * 128:(c + 1) * 128], halo.t[:, kc, :], ph.t[:, 0:2]) for kc in range(KC)],
                            [halo.b, pan.b])
                    act(u_.t[:, 0, 0:1], ph.t[:, 0:1], AF.Copy, [ph.b], pwrites=[u_.b])
                    act(u_.t[:, 0, W - 1:W], ph.t[:, 1:2], AF.Copy, [ph.b], pwrites=[u_.b])
                ts("dve", c_.t[:, :, :], u_.t[:, :, 0:Sq], V(l, "fcw", 0 * 88 + chn), ALU.mult, [u_.b, vecs.b], writes=[c_.b])
                stt("dve", c_.t[:, :, :], u_.t[:, :, 1:1 + Sq], V(l, "fcw", 1 * 88 + chn), c_.t[:, :, :], ALU.mult, ALU.add,
                    [u_.b, c_.b, vecs.b], writes=[c_.b])
                stt("dve", c_.t[:, :, :], u_.t[:, :, 2:2 + Sq], V(l, "fcw", 2 * 88 + chn), c_.t[:, :, :], ALU.mult, ALU.add,
                    [u_.b, c_.b, vecs.b], writes=[c_.b])
                cflat = c_.t[:, :, :].rearrange("p s t -> p (s t)")
                if not isv:
                    act(Gs.t[:, c, :], cflat, AF.Silu, [c_.b, vecs.b], bias=V(l, "fcb", chn),
                        **({"writes": [Gs.b]} if c == 0 else {"pwrites": [Gs.b]}))
                else:
                    act(cflat, cflat, AF.Identity, [c_.b, vecs.b], writes=[c_.b], bias=V(l, "fcb", chn))
                    tt("pool", a_.t[:, :], cflat, Gs.t[:, c, :], ALU.mult, [c_.b, Gs.b], writes=[a_.b])
                    r0 = (cp4 * 4 + c) * 128
                    S.dma("act", g.ACT[r0:r0 + 128, b0:b0 + Tb], a_.t[:, :], reads=[a_.b], pwrites=[g.b_ACT])
        release(m)

    import os
    PH = {"n": 0, "max": int(os.environ.get("MK_STOP", "100000"))}

    def ph(fn, *a):
        if PH["n"] < PH["max"]:
            fn(*a)
        PH["n"] += 1

    def run_group(g):
        ph(phase_P1, g, 0)
        for l in range(L):
            for blk in g.blocks:
                ph(phase_P2, g, l, blk)
            if g.nseq == 1:
                ph(phase_P2C, g, l)
            ph(phase_P2M, g, l)
            ph(phase_P2L, g, l)
            ph(phase_P3, g, l)
            for blk in g.blocks:
                ph(proj_to_M, g, l, blk, g.MIX, g.b_MIX, KC, w_out[l], 4, 512)
            ph(phase_update, g, l, 2, (l, 3))
            for blk in g.blocks:
                ph(phase_P6, g, l, blk)
            for s0 in range(0, g.T, 1024):
                ph(proj_to_M, g, l, (s0, 1024), g.ACT, g.b_ACT, NFC, w_down[l], 11, 256)
            ph(phase_update, g, l, 5, (l + 1, 0) if l + 1 < L else None)

    which = os.environ.get("MK_GROUPS", "ps")
    if "p" in which:
        run_group(GP)
        S.dma("act", nst_out, nst_sb.t[:, :], reads=[nst_sb.b])
    if "s" in which:
        run_group(GS)
    S.emit()
    return nc


def _fm(v):
    v = np.asarray(v, np.float32)
    lead = int(np.prod(v.shape[:-1])) if v.ndim > 1 else 1
    return np.ascontiguousarray(v.reshape(lead, -1, 128).transpose(2, 0, 1).reshape(128, -1))


def _rope_tables(n_tok, width):
    half = width // 2
    quarter = half // 2
    inv = (10000.0 ** (-np.arange(quarter, dtype=np.float32) / quarter)).astype(np.float32)
    t = np.arange(n_tok)
    row = (t // 64).astype(np.float32)
    col = (t % 64).astype(np.float32)
    tab = np.zeros((width, 2, n_tok), np.float32)
    for part, pos in ((0, row), (1, col)):
        ang = pos[None, :] * inv[:, None]
        c, s = np.cos(ang).astype(np.float32), np.sin(ang).astype(np.float32)
        base = part * half
        tab[base:base + quarter, 0] = c
        tab[base + quarter:base + half, 0] = c
        tab[base:base + quarter, 1] = -s
        tab[base + quarter:base + half, 1] = s
    return tab


_NC_CACHE = {}


def kernel(x_prompt, x_sample, cache_attn_k, cache_attn_v, cache_mla_ckv, cache_mla_krope, state_lru,
           c, c_ctx, w_mod, b_mod, g_pre_mix, g_post_mix, g_pre_ffn, g_post_ffn, w_in, g_q, g_k,
           lru_conv_w, lru_conv_b, lru_w_a, lru_b_a, lru_w_x, lru_b_x, lru_lambda, g_kv, w_uk, w_uv,
           w_out, ffn_w_up, ffn_conv_w, ffn_conv_b, ffn_w_down):
    f = lambda a: np.ascontiguousarray(np.asarray(a, dtype=np.float32))
    if "nc" not in _NC_CACHE:
        _NC_CACHE["nc"] = build_program()
    nc = _NC_CACHE["nc"]
    vecs = np.zeros((128, L * NVL), np.float32)
    for l in range(L):
        o = l * NVL
        def put(name, arr):
            a = _fm(arr)
            vecs[:, o + VO[name]:o + VO[name] + a.shape[1]] = a
        put("bmod", b_mod[l]); put("gpm", g_pre_mix[l]); put("gpom", g_post_mix[l]); put("gpf", g_pre_ffn[l])
        put("gpof", g_post_ffn[l]); put("gq", g_q[l]); put("gk", g_k[l]); put("lcw", lru_conv_w[l]); put("lcb", lru_conv_b[l])
        put("lba", lru_b_a[l]); put("lbx", lru_b_x[l]); put("llam", lru_lambda[l]); put("gkv", g_kv[l])
        put("fcw", ffn_conv_w[l]); put("fcb", ffn_conv_b[l])
    rmat = np.zeros((128, 256), np.float32)
    for d in range(128):
        rmat[d ^ 32, d] = 1.0
    for d in range(64):
        rmat[d ^ 16, 128 + d] = 1.0
    rope128 = _rope_tables(SS_T, 128)
    rope64 = _rope_tables(SS_T, 64)
    shared = dict(vecs=vecs, rmat=rmat, rope128=rope128, rope64=rope64,
                  w_mod=f(w_mod), w_in=f(w_in), lru_w_a=f(lru_w_a), lru_w_x=f(lru_w_x),
                  w_uk=f(w_uk).reshape(L, 512, 512), w_uv=f(w_uv).reshape(L, 512, 512), w_out=f(w_out),
                  ffn_w_up=f(ffn_w_up), ffn_w_down=f(ffn_w_down))
    x_prompt = np.asarray(x_prompt, np.float32)
    x_sample = np.asarray(x_sample, np.float32)
    in_maps = []
    for i in range(8):
        b = i // 2
        m = dict(shared)
        m["xp"] = np.ascontiguousarray(x_prompt[4 * i:4 * i + 4].reshape(SP_T, D).T)
        m["xs"] = np.ascontiguousarray(x_sample[b].T)
        m["ck"] = np.ascontiguousarray(np.asarray(cache_attn_k[b], np.float32).transpose(0, 2, 3, 1))
        m["cv"] = np.ascontiguousarray(np.asarray(cache_attn_v[b], np.float32).reshape(L, PAST, 256))
        m["cckv"] = np.ascontiguousarray(np.asarray(cache_mla_ckv[b], np.float32).transpose(0, 2, 1))
        m["ckr"] = np.ascontiguousarray(np.asarray(cache_mla_krope[b], np.float32).transpose(0, 2, 1))
        m["st"] = _fm(np.asarray(state_lru[b], np.float32).reshape(L * 2, 1024))
        cond = np.stack([np.asarray(c_ctx, np.float32), np.asarray(c[b], np.float32)], axis=-1)
        m["cond"] = np.ascontiguousarray(cond.reshape(KC, 128, 2).transpose(1, 0, 2).reshape(128, KC * 2))
        in_maps.append(m)
    import os
    ncores = int(os.environ.get("MK_NCORES", "8"))
    res = run_bass_kernel_spmd(nc, in_maps[:ncores], core_ids=list(range(ncores)))
    R = list(res.results)
    while len(R) < 8:
        R.append({k: np.zeros_like(v) for k, v in R[0].items()})
    y_prompt = np.empty((32, 256, D), np.float32)
    y_sample = np.empty((4, SS_T, D), np.float32)
    nk = np.empty((32, L, 256, 2, 128), np.float32)
    nv = np.empty((32, L, 256, 2, 128), np.float32)
    nckv = np.empty((32, L, 256, 512), np.float32)
    nkr = np.empty((32, L, 256, 64), np.float32)
    nst = np.empty((32, L, 2, 1024), np.float32)
    for i in range(8):
        r = R[i]
        y_prompt[4 * i:4 * i + 4] = r["yp"].T.reshape(4, 256, D)
        if i % 2 == 0:
            y_sample[i // 2] = r["ys"].T
        nk[4 * i:4 * i + 4] = r["nk"].reshape(L, 2, 128, 4, 256).transpose(3, 0, 4, 1, 2)
        nv[4 * i:4 * i + 4] = r["nv"].reshape(L, 4, 256, 2, 128).transpose(1, 0, 2, 3, 4)
        nckv[4 * i:4 * i + 4] = r["nckv"].reshape(L, 512, 4, 256).transpose(2, 0, 3, 1)
        nkr[4 * i:4 * i + 4] = r["nkr"].reshape(L, 64, 4, 256).transpose(2, 0, 3, 1)
        nst[4 * i:4 * i + 4] = r["nst"].reshape(128, L, 2, 8, 4).transpose(4, 1, 2, 3, 0).reshape(4, L, 2, 1024)
    return (y_prompt, y_sample, nk, nv, nckv, nkr, nst)
```

```python
import numpy as np
import concourse.bass as bass
import concourse.mybir as mybir
from concourse.bass_utils import run_bass_kernel_spmd
from contextlib import ExitStack

F32 = mybir.dt.float32
BF16 = mybir.dt.bfloat16
AF = mybir.ActivationFunctionType
ALU = mybir.AluOpType

L = 4
D = 2048
KC = 16
DIN = 4416
DFF = 5632
NFC = 44
EPS = 1e-6
NVL = 622
PAST = 512
SP_T = 1024
SS_T = 4096
VO = dict(bmod=0, gpm=96, gpom=112, gpf=128, gpof=144, gq=160, gk=161, lcw=162, lcb=194, lba=202,
          lbx=218, llam=234, gkv=250, fcw=254, fcb=518)

KDMA = 8


class Buf:
    __slots__ = ("w", "r", "prev")

    def __init__(self):
        self.w = {}
        self.r = {}
        self.prev = {}


def _merge(dst, src):
    for k, v in src.items():
        if dst.get(k, 0) < v:
            dst[k] = v


class Sched:
    ENG = ("pe", "act", "dve", "pool", "sp")

    def __init__(self, nc):
        self.nc = nc
        self.q = {e: [] for e in self.ENG}
        self.cnt = {e: 0 for e in self.ENG}
        self.waited = {e: {} for e in self.ENG}
        self.dma_n = {e: 0 for e in self.ENG}
        self.semkeys = set()
        self.all_events = {}
        self.bar = {e: {} for e in self.ENG}

    def barrier(self):
        snap = dict(self.all_events)
        for e in self.ENG:
            _merge(self.bar[e], snap)

    def _deps(self, eng, reads, writes, pwrites):
        deps = {}
        if self.bar[eng]:
            _merge(deps, self.bar[eng])
            self.bar[eng] = {}
        for b in reads:
            _merge(deps, b.w)
        for b in writes:
            _merge(deps, b.w)
            _merge(deps, b.r)
            _merge(deps, b.prev)
        for b in pwrites:
            if b.r:
                newprev = {}
                _merge(newprev, b.w)
                _merge(newprev, b.r)
                b.prev = newprev
                b.w = {}
                b.r = {}
            _merge(deps, b.prev)
        return deps

    def _commit(self, ev, reads, writes, pwrites):
        k, v = ev
        for b in reads:
            if b.r.get(k, 0) < v:
                b.r[k] = v
        for b in writes:
            b.w = {k: v}
            b.r = {}
            b.prev = {}
        for b in pwrites:
            if b.w.get(k, 0) < v:
                b.w[k] = v
        if self.all_events.get(k, 0) < v:
            self.all_events[k] = v

    def _waits(self, eng, deps, skip_self=False):
        out = []
        wd = self.waited[eng]
        for k, v in deps.items():
            if skip_self and k == ("eng", eng):
                continue
            if wd.get(k, 0) < v:
                wd[k] = v
                out.append((k, v))
        return out

    def op(self, eng, fn, reads=(), writes=(), pwrites=()):
        deps = self._deps(eng, reads, writes, pwrites)
        waits = self._waits(eng, deps, skip_self=(eng == "pe"))
        self.cnt[eng] += 1
        k = ("eng", eng)
        self.semkeys.add(k)
        ev = (k, self.cnt[eng])
        self.q[eng].append((waits, fn, k, 1))
        self._commit(ev, reads, writes, pwrites)

    def dma(self, queue, out, in_, reads=(), writes=(), pwrites=(), slow=False):
        deps = self._deps(queue, reads, writes, pwrites)
        n = self.dma_n[queue]
        self.dma_n[queue] += 1
        slot, rnd = n % KDMA, n // KDMA
        k = ("dma", queue, slot)
        self.semkeys.add(k)
        if rnd > 0:
            _merge(deps, {k: 16 * rnd})
        waits = self._waits(queue, deps)
        ev = (k, 16 * (rnd + 1))
        fn = (lambda e, out=out, in_=in_: e.dma_start(out=out, in_=in_, allow_slow_non_contiguous=True)) if slow else (lambda e, out=out, in_=in_: e.dma_start(out=out, in_=in_))
        self.q[queue].append((waits, fn, k, 16))
        self._commit(ev, reads, writes, pwrites)

    def emit(self):
        nc = self.nc
        with ExitStack() as es:
            sems = {}
            for k in sorted(self.semkeys, key=str):
                sems[k] = es.enter_context(nc.semaphore("s_" + "_".join(str(x) for x in k)))
            block = es.enter_context(nc.Block())
            finw = self._waits("sp", dict(self.all_events))

            def mk(e):
                def body(eng):
                    for waits, fn, k, inc in self.q[e]:
                        for (wk, wv) in waits:
                            eng.wait_ge(sems[wk], wv)
                        fn(eng).then_inc(sems[k], inc)
                    if e == "sp":
                        for (wk, wv) in finw:
                            eng.wait_ge(sems[wk], wv)
                return body

            block.tensor(mk("pe"))
            block.scalar(mk("act"))
            block.vector(mk("dve"))
            block.gpsimd(mk("pool"))
            block.sync(mk("sp"))


class Tile:
    __slots__ = ("t", "b")

    def __init__(self, t):
        self.t = t
        self.b = Buf()


def build_program():
    nc = bass.Bass("TRN2", target_bir_lowering=False)
    S = Sched(nc)

    def din(name, shape, dt=F32):
        return nc.dram_tensor(name, list(shape), dt, kind="ExternalInput").ap()

    def dout(name, shape, dt=F32):
        return nc.dram_tensor(name, list(shape), dt, kind="ExternalOutput").ap()

    def dscr(name, shape, dt):
        return nc.dram_tensor(name, list(shape), dt).ap()

    xp_in = din("xp", [D, SP_T])
    xs_in = din("xs", [D, SS_T])
    ck_in = din("ck", [L, 2, 128, PAST])
    cv_in = din("cv", [L, PAST, 256])
    cckv_in = din("cckv", [L, 512, PAST])
    ckr_in = din("ckr", [L, 64, PAST])
    st_in = din("st", [128, L * 2 * 8])
    cond_in = din("cond", [128, KC * 2])
    vecs_in = din("vecs", [128, L * NVL])
    rope128_in = din("rope128", [128, 2, SS_T])
    rope64_in = din("rope64", [64, 2, SS_T])
    rmat_in = din("rmat", [128, 256])
    w_mod = din("w_mod", [L, D, 6 * D])
    w_in = din("w_in", [L, D, DIN])
    lru_w_a = din("lru_w_a", [L, 2, 8, 128, 128])
    lru_w_x = din("lru_w_x", [L, 2, 8, 128, 128])
    w_uk = din("w_uk", [L, 512, 512])
    w_uv = din("w_uv", [L, 512, 512])
    w_out = din("w_out", [L, D, D])
    w_up = din("ffn_w_up", [L, D, 2 * DFF])
    w_down = din("ffn_w_down", [L, DFF, D])
    yp_out = dout("yp", [D, SP_T])
    ys_out = dout("ys", [D, SS_T])
    nk_out = dout("nk", [L, 2, 128, SP_T])
    nv_out = dout("nv", [L, SP_T, 256])
    nckv_out = dout("nckv", [L, 512, SP_T])
    nkr_out = dout("nkr", [L, 64, SP_T])
    nst_out = dout("nst", [128, L * 2 * 8 * 4])

    SB_TOTAL = 229312
    arena = {"off": 16640, "n": 0}

    def alloc(shape, dt):
        nbytes = int(np.prod(shape[1:])) * (4 if dt == F32 else 2)
        nbytes = (nbytes + 63) // 64 * 64
        off = arena["off"]
        assert off + nbytes <= SB_TOTAL, ("SBUF overflow", off, nbytes)
        arena["off"] = off + nbytes
        arena["n"] += 1
        return Tile(nc.alloc_sbuf_tensor_at("t%d" % arena["n"], list(shape), dt, offset=off))

    def mark():
        return arena["off"]

    def release(m):
        S.barrier()
        arena["off"] = m

    psum = [Tile(nc.alloc_psum_tensor("ps%d" % i, [128, 512], F32)) for i in range(8)]

    def act(out, in_, func, reads, writes=(), pwrites=(), bias=0.0, scale=1.0):
        S.op("act", lambda e: e.activation(out=out, in_=in_, func=func, bias=bias, scale=scale),
             reads=reads, writes=writes, pwrites=pwrites)

    def tt(eng, out, in0, in1, op, reads, writes=(), pwrites=()):
        S.op(eng, lambda e: e.tensor_tensor(out=out, in0=in0, in1=in1, op=op), reads=reads, writes=writes, pwrites=pwrites)

    def ts(eng, out, in0, s1, op0, reads, writes=(), pwrites=(), s2=None, op1=None):
        if op1 is None:
            S.op(eng, lambda e: e.tensor_scalar(out=out, in0=in0, scalar1=s1, scalar2=None, op0=op0),
                 reads=reads, writes=writes, pwrites=pwrites)
        else:
            S.op(eng, lambda e: e.tensor_scalar(out=out, in0=in0, scalar1=s1, scalar2=s2, op0=op0, op1=op1),
                 reads=reads, writes=writes, pwrites=pwrites)

    def stt(eng, out, in0, scalar, in1, op0, op1, reads, writes=(), pwrites=()):
        S.op(eng, lambda e: e.scalar_tensor_tensor(out=out, in0=in0, scalar=scalar, in1=in1, op0=op0, op1=op1),
             reads=reads, writes=writes, pwrites=pwrites)

    def cp(eng, out, in_, reads, writes=(), pwrites=()):
        S.op(eng, lambda e: e.tensor_copy(out=out, in_=in_), reads=reads, writes=writes, pwrites=pwrites)

    def mm(out, lhsT, rhs, start, stop, reads, writes=(), pwrites=()):
        S.op("pe", lambda e: e.matmul(out, lhsT=lhsT, rhs=rhs, start=start, stop=stop),
             reads=reads, writes=writes, pwrites=pwrites)

    def mmgroup(ps, parts, reads):
        out = parts[0][2]
        n = len(parts)
        for i, (l_, r_, o_) in enumerate(parts):
            if i == 0:
                mm(o_, l_, r_, True, n == 1, reads, writes=[ps.b])
            else:
                mm(o_, l_, r_, False, i == n - 1, reads, pwrites=[ps.b])

    def rsqrt_from(out_t, ssq_ap, ssq_reads, n, writes_b):
        act(out_t, ssq_ap, AF.Sqrt, ssq_reads, writes=[writes_b], bias=EPS, scale=1.0 / n)
        S.op("dve", lambda e: e.reciprocal(out=out_t, in_=out_t), reads=[writes_b], writes=[writes_b])

    vecs = alloc([128, L * NVL], F32)
    modT = alloc([128, L * 96 * 2], F32)
    dv = alloc([128, L * 2 * 6 * 16], F32)
    clam = alloc([128, L * 16], F32)
    clam2 = alloc([128, L * 16], F32)
    st_sb = alloc([128, L * 16], F32)
    ones_bf = alloc([128, 128], BF16)
    rmat = alloc([128, 256], F32)
    nst_sb = alloc([128, L * 2 * 8 * 4], F32)
    epsb = alloc([128, 1], F32)

    def V(l, name, i=0, n=1):
        o = l * NVL + VO[name] + i
        return vecs.t[:, o:o + n]

    def DV(l, ci, which, kc):
        o = ((l * 2 + ci) * 6 + which) * 16 + kc
        return dv.t[:, o:o + 1]

    S.dma("sp", vecs.t[:, :], vecs_in, writes=[vecs.b])
    S.dma("sp", rmat.t[:, :], rmat_in, writes=[rmat.b])
    S.dma("sp", st_sb.t[:, :], st_in, writes=[st_sb.b])
    S.op("pool", lambda e: e.memset(ones_bf.t[:, :], 1.0), writes=[ones_bf.b])
    S.op("pool", lambda e: e.memset(nst_sb.t[:, :], 0.0), writes=[nst_sb.b])
    S.barrier()

    class PanelStream:
        def __init__(self, nk, ncols_max, kstage):
            self.nk, self.ncm, self.kst = nk, ncols_max, kstage
            self.stg = [alloc([128, kstage, ncols_max], F32) for _ in range(4)]
            self.pan = [alloc([128, nk, ncols_max], BF16) for _ in range(2)]
            self.si = 0
            self.pi = 0
            self.ci = 0

        def stream(self, specs):
            nxt = self.load(*specs[0])
            for i in range(len(specs)):
                cur = nxt
                if i + 1 < len(specs):
                    nxt = self.load(*specs[i + 1])
                yield i, cur

        def load(self, src2d, ncols):
            pan = self.pan[self.pi % 2]
            self.pi += 1
            k0 = 0
            first = True
            while k0 < self.nk:
                kn = min(self.kst, self.nk - k0)
                stg = self.stg[self.si % 4]
                self.si += 1
                S.dma("sp", stg.t[:, 0:kn, 0:ncols],
                      src2d[k0 * 128:(k0 + kn) * 128, :].rearrange("(kc p) n -> p kc n", p=128),
                      writes=[stg.b])
                kw = {"writes": [pan.b]} if first else {"pwrites": [pan.b]}
                first = False
                if self.ci % 2 == 0:
                    cp("dve", pan.t[:, k0:k0 + kn, 0:ncols], stg.t[:, 0:kn, 0:ncols], [stg.b], **kw)
                else:
                    act(pan.t[:, k0:k0 + kn, 0:ncols], stg.t[:, 0:kn, 0:ncols], AF.Copy, [stg.b], **kw)
                self.ci += 1
                k0 += kn
            return pan

    m0 = mark()
    cond_sb = alloc([128, KC * 2], F32)
    cond_bf = alloc([128, KC, 2], BF16)
    S.dma("sp", cond_sb.t[:, :], cond_in, writes=[cond_sb.b])
    act(cond_bf.t[:, :, :], cond_sb.t[:, :].rearrange("p (k c) -> p k c", c=2), AF.Silu, [cond_sb.b], writes=[cond_bf.b])
    ps_w = PanelStream(KC, 512, 4)
    p0_specs = [(w_mod[l, :, pnl * 512:(pnl + 1) * 512], 512) for l in range(L) for pnl in range(24)]
    p0_iter = ps_w.stream(p0_specs)
    for l in range(L):
        pst = psum[l % 2]
        for pnl in range(24):
            _, pan = next(p0_iter)
            for c4 in range(4):
                j = pnl * 4 + c4
                parts = [(pan.t[:, kc, c4 * 128:(c4 + 1) * 128], cond_bf.t[:, kc, :], pst.t[:, 2 * j:2 * j + 2]) for kc in range(KC)]
                if j == 0:
                    mmgroup(pst, parts, [pan.b, cond_bf.b])
                else:
                    for i, (l_, r_, o_) in enumerate(parts):
                        mm(o_, l_, r_, i == 0, i == KC - 1, [pan.b, cond_bf.b], pwrites=[pst.b])
        tt("dve", modT.t[:, l * 192:(l + 1) * 192].rearrange("p (j c) -> p j c", c=2),
           pst.t[:, 0:192].rearrange("p (j c) -> p j c", c=2),
           V(l, "bmod", 0, 96).unsqueeze(2).to_broadcast([128, 96, 2]), ALU.add,
           [pst.b, vecs.b], pwrites=[modT.b])
    for l in range(L):
        for ci in range(2):
            def modcol(k6):
                return modT.t[:, l * 192:(l + 1) * 192].rearrange("p (j c) -> p j c", c=2)[:, k6 * 16:(k6 + 1) * 16, ci]

            def dvc(which):
                o = ((l * 2 + ci) * 6 + which) * 16
                return dv.t[:, o:o + 16]
            stt("dve", dvc(0), modcol(1), 1.0, V(l, "gpm", 0, 16), ALU.add, ALU.mult, [modT.b, vecs.b], pwrites=[dv.b])
            cp("dve", dvc(1), modcol(0), [modT.b], pwrites=[dv.b])
            tt("dve", dvc(2), modcol(2), V(l, "gpom", 0, 16), ALU.mult, [modT.b, vecs.b], pwrites=[dv.b])
            stt("dve", dvc(3), modcol(4), 1.0, V(l, "gpf", 0, 16), ALU.add, ALU.mult, [modT.b, vecs.b], pwrites=[dv.b])
            cp("dve", dvc(4), modcol(3), [modT.b], pwrites=[dv.b])
            tt("dve", dvc(5), modcol(5), V(l, "gpof", 0, 16), ALU.mult, [modT.b, vecs.b], pwrites=[dv.b])
    for l in range(L):
        e_ = clam2.t[:, l * 16:(l + 1) * 16]
        c_ = clam.t[:, l * 16:(l + 1) * 16]
        act(e_, V(l, "llam", 0, 16), AF.Exp, [vecs.b], pwrites=[clam2.b], scale=-1.0)
        ts("dve", c_, e_, -0.25, ALU.mult, [clam2.b], pwrites=[clam.b], s2=1.0 / 3.0, op1=ALU.add)
        tt("dve", c_, c_, e_, ALU.mult, [clam.b, clam2.b], writes=[clam.b])
        ts("dve", c_, c_, -0.5, ALU.add, [clam.b], writes=[clam.b])
        tt("dve", c_, c_, e_, ALU.mult, [clam.b, clam2.b], writes=[clam.b])
        ts("dve", c_, c_, 1.0, ALU.add, [clam.b], writes=[clam.b])
        tt("dve", c_, c_, e_, ALU.mult, [clam.b, clam2.b], writes=[clam.b])
        ts("dve", c_, c_, -8.0, ALU.mult, [clam.b], writes=[clam.b])
        ts("dve", e_, c_, 2.0, ALU.mult, [clam.b], writes=[clam2.b])
    release(m0)

    class Group:
        pass

    def mkgroup(name, ci, T, nseq, Sq, cache, x_in, y_out):
        g = Group()
        g.name, g.ci, g.T, g.nseq, g.S, g.cache = name, ci, T, nseq, Sq, cache
        g.Tk = T + (cache if nseq == 1 else 0)
        g.x_in, g.y_out = x_in, y_out
        g.xT = dscr(name + "_xT", [D, T], F32)
        g.H = dscr(name + "_H", [D, T], BF16)
        g.QA = dscr(name + "_QA", [4, 128, T], BF16)
        g.KA = dscr(name + "_KA", [2, 128, g.Tk], BF16)
        g.VA = dscr(name + "_VA", [g.Tk, 256], BF16)
        g.QN = dscr(name + "_QN", [4, 128, T], BF16)
        g.QR = dscr(name + "_QR", [4, 64, T], BF16)
        g.KN = dscr(name + "_KN", [4, 128, g.Tk], BF16)
        g.KR = dscr(name + "_KR", [64, g.Tk], BF16)
        g.VM = dscr(name + "_VM", [g.Tk, 512], BF16)
        g.CKV = dscr(name + "_CKV", [512, T], F32)
        g.XR = dscr(name + "_XR", [1024, T], F32)
        g.GR = dscr(name + "_GR", [1024, T], F32)
        g.MIX = dscr(name + "_MIX", [D, T], BF16)
        g.M = dscr(name + "_M", [D, T], F32)
        g.ACT = dscr(name + "_ACT", [DFF, T], BF16)
        for nm in ("xT", "H", "QA", "KA", "VA", "QN", "QR", "KN", "KR", "VM", "CKV", "XR", "GR", "MIX", "M", "ACT"):
            setattr(g, "b_" + nm, Buf())
        g.blocks = [(s, min(2048, T - s)) for s in range(0, T, 2048)]
        return g

    GP = mkgroup("p", 0, SP_T, 4, 256, 0, xp_in, yp_out)
    GS = mkgroup("s", 1, SS_T, 1, SS_T, PAST, xs_in, ys_out)

    rr = {"acc": 0, "aux": 0}

    def acc_ps():
        rr["acc"] += 1
        return psum[rr["acc"] % 4]

    def aux_ps():
        rr["aux"] += 1
        return psum[4 + rr["aux"] % 4]

    def prenorm_tile(g, l, which_a, xt, c0, tmp, sq, rstd, hstage):
        act(sq.t[:, :, :], xt.t[:, :, :], AF.Square, [xt.b], writes=[sq.b])
        pz = aux_ps()
        mmgroup(pz, [(ones_bf.t[:, :], sq.t[:, kc, :], pz.t[:, :]) for kc in range(KC)], [ones_bf.b, sq.b])
        rsqrt_from(rstd.t[:, :], pz.t[:, :], [pz.b], D, rstd.b)
        for kc in range(KC):
            tm = tmp[kc % len(tmp)]
            stt("dve", tm.t[:, :], xt.t[:, kc, :], DV(l, g.ci, which_a, kc), rstd.t[:, :], ALU.mult, ALU.mult,
                [xt.b, rstd.b, dv.b], writes=[tm.b])
            if kc == 0:
                act(hstage.t[:, kc, :], tm.t[:, :], AF.Identity, [tm.b, dv.b], writes=[hstage.b],
                    bias=DV(l, g.ci, which_a + 1, kc))
            else:
                act(hstage.t[:, kc, :], tm.t[:, :], AF.Identity, [tm.b, dv.b], pwrites=[hstage.b],
                    bias=DV(l, g.ci, which_a + 1, kc))
        S.dma("act", g.H[:, c0:c0 + 512].rearrange("(kc p) t -> p kc t", p=128), hstage.t[:, :, :],
              reads=[hstage.b], pwrites=[g.b_H])

    def phase_P1(g, l):
        m = mark()
        xts = [alloc([128, KC, 512], F32) for _ in range(2)]
        tmp = [alloc([128, 512], F32) for _ in range(3)]
        sq = alloc([128, KC, 512], BF16)
        rstd = alloc([128, 512], F32)
        hst = [alloc([128, KC, 512], BF16) for _ in range(2)]
        for ti in range(g.T // 512):
            xt = xts[ti % 2]
            S.dma("sp", xt.t[:, :, :], g.x_in[:, ti * 512:(ti + 1) * 512].rearrange("(kc p) t -> p kc t", p=128), writes=[xt.b])
            S.dma("act", g.xT[:, ti * 512:(ti + 1) * 512].rearrange("(kc p) t -> p kc t", p=128), xt.t[:, :, :],
                  reads=[xt.b], pwrites=[g.b_xT])
            prenorm_tile(g, l, 0, xt, ti * 512, tmp, sq, rstd, hst[ti % 2])
        release(m)

    P2_PANELS = [
        (0, 512, [(0, 128, "qa", 0), (128, 128, "qa", 1), (256, 128, "qa", 2), (384, 128, "qa", 3)]),
        (512, 512, [(0, 128, "ka", 0), (128, 128, "ka", 1), (256, 256, "va", 0)]),
        (1024, 512, [(i * 128, 128, "xr", i) for i in range(4)]),
        (1536, 512, [(i * 128, 128, "xr", 4 + i) for i in range(4)]),
        (2048, 512, [(i * 128, 128, "gr", i) for i in range(4)]),
        (2560, 512, [(i * 128, 128, "gr", 4 + i) for i in range(4)]),
        (3072, 512, [(0, 128, "qn", 0), (128, 64, "qr", 0), (192, 128, "qn", 1), (320, 64, "qr", 1), (384, 128, "qn", 2)]),
        (3584, 512, [(0, 64, "qr", 2), (64, 128, "qn", 3), (192, 64, "qr", 3), (256, 128, "ckv", 0), (384, 128, "ckv", 1)]),
        (4096, 320, [(0, 128, "ckv", 2), (128, 128, "ckv", 3), (256, 64, "kr", 0)]),
    ]

    def phase_P2(g, l, blk):
        b0, Tb = blk
        m = mark()
        hT = alloc([128, KC, Tb], BF16)
        ps_w = PanelStream(KC, 512, 4)
        hTb = [Buf() for _ in range(Tb // 512)]
        ev32 = [alloc([128, 512], F32) for _ in range(3)]
        evbf = [alloc([128, 512], BF16) for _ in range(3)]
        sqb = [alloc([128, 512], BF16) for _ in range(2)]
        rstd = [alloc([128, 512], F32) for _ in range(2)]
        rp = [alloc([128, 2, 512], F32) for _ in range(2)]
        t1 = [alloc([128, 512], F32) for _ in range(2)]
        cnt = {"e": 0, "s": 0, "r": 0}
        sample = g.nseq == 1
        koff = g.cache if sample else 0

        def rope(src32, width, c0, outbf):
            r_ = rp[cnt["r"] % 2]
            t_ = t1[cnt["r"] % 2]
            cnt["r"] += 1
            tab = rope128_in if width == 128 else rope64_in
            S.dma("sp", r_.t[0:width, :, :], tab[:, :, c0:c0 + 512], writes=[r_.b])
            pr = aux_ps()
            rm = rmat.t[:, 0:128] if width == 128 else rmat.t[0:64, 128:192]
            mmgroup(pr, [(rm, src32.t[0:width, :], pr.t[0:width, :])], [rmat.b, src32.b])
            tt("dve", t_.t[0:width, :], pr.t[0:width, :], r_.t[0:width, 1, :], ALU.mult, [pr.b, r_.b], writes=[t_.b])
            tt("pool", src32.t[0:width, :], src32.t[0:width, :], r_.t[0:width, 0, :], ALU.mult, [src32.b, r_.b], writes=[src32.b])
            tt("dve", outbf.t[0:width, :], src32.t[0:width, :], t_.t[0:width, :], ALU.add, [src32.b, t_.b], writes=[outbf.b])

        pend = {"f": None}

        def evac(ps, e32, eb, kind, idx, c0):
            if kind in ("xr", "gr", "ckv"):
                act(e32.t[:, :], ps.t[:, :], AF.Copy, [ps.b], writes=[e32.b])
                dst, bb = {"xr": (g.XR, g.b_XR), "gr": (g.GR, g.b_GR), "ckv": (g.CKV, g.b_CKV)}[kind]
                S.dma("act", dst[idx * 128:(idx + 1) * 128, c0:c0 + 512], e32.t[:, :], reads=[e32.b], pwrites=[bb])
            elif kind in ("qa", "ka"):
                sq_ = sqb[cnt["s"] % 2]
                rs_ = rstd[cnt["s"] % 2]
                cnt["s"] += 1
                act(sq_.t[:, :], ps.t[:, :], AF.Square, [ps.b], writes=[sq_.b])
                pz = aux_ps()
                mmgroup(pz, [(ones_bf.t[:, :], sq_.t[:, :], pz.t[:, :])], [ones_bf.b, sq_.b])
                rsqrt_from(rs_.t[:, :], pz.t[:, :], [pz.b], 128, rs_.b)
                gname = "gq" if kind == "qa" else "gk"
                stt("dve", e32.t[:, :], ps.t[:, :], V(l, gname), rs_.t[:, :], ALU.mult, ALU.mult,
                    [ps.b, rs_.b, vecs.b], writes=[e32.b])
                if kind == "ka" and not sample:
                    S.dma("act", nk_out[l, idx, :, c0:c0 + 512], e32.t[:, :], reads=[e32.b])
                if sample:
                    rope(e32, 128, c0, eb)
                else:
                    cp("pool", eb.t[:, :], e32.t[:, :], [e32.b], writes=[eb.b])
                if kind == "qa":
                    S.dma("act", g.QA[idx, :, c0:c0 + 512], eb.t[:, :], reads=[eb.b], pwrites=[g.b_QA])
                else:
                    S.dma("act", g.KA[idx, :, koff + c0:koff + c0 + 512], eb.t[:, :], reads=[eb.b], pwrites=[g.b_KA])
            elif kind == "qn":
                act(eb.t[:, :], ps.t[:, :], AF.Copy, [ps.b], writes=[eb.b])
                S.dma("act", g.QN[idx, :, c0:c0 + 512], eb.t[:, :], reads=[eb.b], pwrites=[g.b_QN])
            elif kind in ("qr", "kr"):
                if sample:
                    act(e32.t[0:64, :], ps.t[0:64, :], AF.Copy, [ps.b], writes=[e32.b])
                    rope(e32, 64, c0, eb)
                else:
                    if kind == "kr":
                        act(e32.t[0:64, :], ps.t[0:64, :], AF.Copy, [ps.b], writes=[e32.b])
                        S.dma("act", nkr_out[l, :, c0:c0 + 512], e32.t[0:64, :], reads=[e32.b])
                        cp("dve", eb.t[0:64, :], e32.t[0:64, :], [e32.b], writes=[eb.b])
                    else:
                        act(eb.t[0:64, :], ps.t[0:64, :], AF.Copy, [ps.b], writes=[eb.b])
                if kind == "qr":
                    S.dma("act", g.QR[idx, :, c0:c0 + 512], eb.t[0:64, :], reads=[eb.b], pwrites=[g.b_QR])
                else:
                    S.dma("act", g.KR[:, koff + c0:koff + c0 + 512], eb.t[0:64, :], reads=[eb.b], pwrites=[g.b_KR])


        for pi_, pan in ps_w.stream([(w_in[l, :, c0_:c0_ + nc_], nc_) for (c0_, nc_, _) in P2_PANELS]):
            (col0, ncols, chunks) = P2_PANELS[pi_]
            if pi_ == 0:
                for ti in range(Tb // 512):
                    S.dma("sp", hT.t[:, :, ti * 512:(ti + 1) * 512],
                          g.H[:, b0 + ti * 512:b0 + (ti + 1) * 512].rearrange("(kc p) t -> p kc t", p=128),
                          reads=[g.b_H], writes=[hTb[ti]])
            for ti in range(Tb // 512):
                c0 = b0 + ti * 512
                for (off, width, kind, idx) in chunks:
                    if kind == "va":
                        if pend["f"] is not None:
                            pend["f"]()
                            pend["f"] = None
                        for j in range(4):
                            ps = acc_ps()
                            mmgroup(ps, [(hT.t[:, kc, ti * 512 + j * 128: ti * 512 + (j + 1) * 128], pan.t[:, kc, off:off + 256], ps.t[:, 0:256])
                                         for kc in range(KC)], [hTb[ti], pan.b])
                            eb = evbf[cnt["e"] % 3]
                            e32 = ev32[cnt["e"] % 3]
                            cnt["e"] += 1
                            r0 = c0 + j * 128
                            if not sample:
                                act(e32.t[:, 0:256], ps.t[:, 0:256], AF.Copy, [ps.b], writes=[e32.b])
                                S.dma("act", nv_out[l, r0:r0 + 128, :], e32.t[:, 0:256], reads=[e32.b])
                                cp("dve", eb.t[:, 0:256], e32.t[:, 0:256], [e32.b], writes=[eb.b])
                            else:
                                act(eb.t[:, 0:256], ps.t[:, 0:256], AF.Copy, [ps.b], writes=[eb.b])
                            S.dma("act", g.VA[koff + r0:koff + r0 + 128, :], eb.t[:, 0:256], reads=[eb.b], pwrites=[g.b_VA])
                        continue
                    ps = acc_ps()
                    mmgroup(ps, [(pan.t[:, kc, off:off + width], hT.t[:, kc, ti * 512:(ti + 1) * 512], ps.t[0:width, :])
                                 for kc in range(KC)], [hTb[ti], pan.b])
                    e32 = ev32[cnt["e"] % 3]
                    eb = evbf[cnt["e"] % 3]
                    cnt["e"] += 1
                    fn_ = (lambda ps=ps, e32=e32, eb=eb, kind=kind, idx=idx, c0=c0: evac(ps, e32, eb, kind, idx, c0))
                    if pend["f"] is not None:
                        pend["f"]()
                    pend["f"] = fn_
        if pend["f"] is not None:
            pend["f"]()
            pend["f"] = None
        release(m)


    def phase_P2C(g, l):
        m = mark()
        a32 = alloc([128, 2, 512], F32)
        abf = alloc([128, 2, 512], BF16)
        S.dma("sp", a32.t[:, :, :], ck_in[l].rearrange("h d t -> d h t"), writes=[a32.b])
        cp("dve", abf.t[:, :, :], a32.t[:, :, :], [a32.b], writes=[abf.b])
        S.dma("act", g.KA[:, :, 0:PAST].rearrange("h d t -> d h t"), abf.t[:, :, :], reads=[abf.b], pwrites=[g.b_KA])
        v32 = alloc([128, 4, 256], F32)
        vbf = alloc([128, 4, 256], BF16)
        S.dma("sp", v32.t[:, :, :], cv_in[l].rearrange("(kb p) d -> p kb d", p=128), writes=[v32.b])
        cp("dve", vbf.t[:, :, :], v32.t[:, :, :], [v32.b], writes=[vbf.b])
        S.dma("act", g.VA[0:PAST, :].rearrange("(kb p) d -> p kb d", p=128), vbf.t[:, :, :], reads=[vbf.b], pwrites=[g.b_VA])
        r32 = alloc([64, 512], F32)
        rbf = alloc([64, 512], BF16)
        S.dma("sp", r32.t[:, :], ckr_in[l], writes=[r32.b])
        cp("dve", rbf.t[:, :], r32.t[:, :], [r32.b], writes=[rbf.b])
        S.dma("act", g.KR[:, 0:PAST], rbf.t[:, :], reads=[rbf.b], pwrites=[g.b_KR])
        release(m)

    def phase_P2M(g, l):
        m = mark()
        sample = g.nseq == 1
        koff = g.cache if sample else 0
        w32 = alloc([128, 4, 512], F32)
        wuk = alloc([128, 4, 512], BF16)
        wuv = alloc([128, 4, 512], BF16)
        S.dma("sp", w32.t[:, :, :], w_uk[l].rearrange("(rc p) n -> p rc n", p=128), writes=[w32.b])
        cp("pool", wuk.t[:, :, :], w32.t[:, :, :], [w32.b], writes=[wuk.b])
        S.dma("sp", w32.t[:, :, :], w_uv[l].rearrange("(rc p) n -> p rc n", p=128), writes=[w32.b])
        cp("pool", wuv.t[:, :, :], w32.t[:, :, :], [w32.b], writes=[wuv.b])
        c32 = [alloc([128, 4, 512], F32) for _ in range(2)]
        cbf = [alloc([128, 4, 512], BF16) for _ in range(2)]
        sq = alloc([128, 4, 512], BF16)
        rstd = alloc([128, 512], F32)
        evb = [alloc([128, 512], BF16) for _ in range(3)]
        ne = 0
        tiles = []
        if sample:
            tiles.append(("cache", 0))
        tiles += [("new", ti) for ti in range(g.T // 512)]
        for n, (kind, ti) in enumerate(tiles):
            c_ = c32[n % 2]
            b_ = cbf[n % 2]
            if kind == "cache":
                S.dma("sp", c_.t[:, :, :], cckv_in[l].rearrange("(rc p) t -> p rc t", p=128), writes=[c_.b])
                cp("dve", b_.t[:, :, :], c_.t[:, :, :], [c_.b], writes=[b_.b])
                kc0 = 0
            else:
                c0 = ti * 512
                kc0 = koff + c0
                S.dma("sp", c_.t[:, :, :], g.CKV[:, c0:c0 + 512].rearrange("(rc p) t -> p rc t", p=128), reads=[g.b_CKV], writes=[c_.b])
                act(sq.t[:, :, :], c_.t[:, :, :], AF.Square, [c_.b], writes=[sq.b])
                pz = aux_ps()
                mmgroup(pz, [(ones_bf.t[:, :], sq.t[:, rc, :], pz.t[:, :]) for rc in range(4)], [ones_bf.b, sq.b])
                rsqrt_from(rstd.t[:, :], pz.t[:, :], [pz.b], 512, rstd.b)
                for rc in range(4):
                    stt("dve", c_.t[:, rc, :], c_.t[:, rc, :], V(l, "gkv", rc), rstd.t[:, :], ALU.mult, ALU.mult,
                        [c_.b, rstd.b, vecs.b], writes=[c_.b])
                if not sample:
                    S.dma("act", nckv_out[l, :, c0:c0 + 512].rearrange("(rc p) t -> p rc t", p=128), c_.t[:, :, :], reads=[c_.b])
                cp("pool", b_.t[:, :, :], c_.t[:, :, :], [c_.b], writes=[b_.b])
            for h in range(4):
                ps = acc_ps()
                mmgroup(ps, [(wuk.t[:, rc, h * 128:(h + 1) * 128], b_.t[:, rc, :], ps.t[:, :]) for rc in range(4)], [wuk.b, b_.b])
                e_ = evb[ne % 3]
                ne += 1
                act(e_.t[:, :], ps.t[:, :], AF.Copy, [ps.b], writes=[e_.b])
                S.dma("act", g.KN[h, :, kc0:kc0 + 512], e_.t[:, :], reads=[e_.b], pwrites=[g.b_KN])
            for j in range(4):
                ps = acc_ps()
                mmgroup(ps, [(b_.t[:, rc, j * 128:(j + 1) * 128], wuv.t[:, rc, :], ps.t[:, :]) for rc in range(4)], [wuv.b, b_.b])
                e_ = evb[ne % 3]
                ne += 1
                act(e_.t[:, :], ps.t[:, :], AF.Copy, [ps.b], writes=[e_.b])
                S.dma("act", g.VM[kc0 + j * 128:kc0 + (j + 1) * 128, :], e_.t[:, :], reads=[e_.b], pwrites=[g.b_VM])
        release(m)

    def phase_P2L(g, l):
        m = mark()
        sample = g.nseq == 1
        T, ns, Sq = g.T, g.nseq, g.S
        W = Sq + 3
        g32 = alloc([128, 32, 128], F32)
        gw = alloc([128, 32, 128], BF16)
        S.dma("sp", g32.t[:, 0:16, :], lru_w_a[l].rearrange("d n k j -> k (d n) j"), writes=[g32.b])
        S.dma("sp", g32.t[:, 16:32, :], lru_w_x[l].rearrange("d n k j -> k (d n) j"), pwrites=[g32.b])
        cp("pool", gw.t[:, :, :], g32.t[:, :, :], [g32.b], writes=[gw.b])
        xpad = alloc([128, ns, W], F32)
        xc = alloc([128, ns, Sq], F32)
        xcb = alloc([128, ns, Sq], BF16)
        A = alloc([128, ns, Sq], F32)
        U = alloc([128, ns, Sq], F32)
        Hf = alloc([128, ns, Sq], F32)
        Hb = alloc([128, ns, Sq], F32)
        G = alloc([128, ns, Sq], F32)
        yb = alloc([128, ns, Sq], BF16)
        gt = [alloc([128, 512], F32) for _ in range(2)]
        S.op("pool", lambda e: e.memset(xpad.t[:, :, :], 0.0), writes=[xpad.b])
        S.barrier()
        flat = lambda t_: t_.t[:, :, :].rearrange("p s t -> p (s t)")
        for ch in range(8):
            S.dma("sp", xpad.t[:, :, 1:1 + Sq], g.XR[ch * 128:(ch + 1) * 128, :].rearrange("p (s t) -> p s t", s=ns),
                  reads=[g.b_XR, xpad.b], writes=[xpad.b])
            S.dma("sp", flat(G), g.GR[ch * 128:(ch + 1) * 128, :], reads=[g.b_GR], writes=[G.b])
            ts("dve", xc.t[:, :, :], xpad.t[:, :, 0:Sq], V(l, "lcw", 0 * 8 + ch), ALU.mult, [xpad.b, vecs.b], writes=[xc.b],
               s2=V(l, "lcb", ch), op1=ALU.add)
            for j in (1, 2, 3):
                stt("dve", xc.t[:, :, :], xpad.t[:, :, j:j + Sq], V(l, "lcw", j * 8 + ch), xc.t[:, :, :], ALU.mult, ALU.add,
                    [xpad.b, xc.b, vecs.b], writes=[xc.b])
            act(xcb.t[:, :, :], xc.t[:, :, :], AF.Copy, [xc.b], writes=[xcb.b])
            act(flat(G), flat(G), AF.Gelu, [G.b], writes=[G.b])
            for d in range(2):
                H = Hf if d == 0 else Hb
                lo = l * 16 + d * 8 + ch
                for ti in range(T // 512):
                    sl = slice(ti * 512, (ti + 1) * 512)
                    pa = acc_ps()
                    mmgroup(pa, [(gw.t[:, d * 8 + ch, :], flat(xcb)[:, sl], pa.t[:, :])], [gw.b, xcb.b])
                    pi = acc_ps()
                    mmgroup(pi, [(gw.t[:, 16 + d * 8 + ch, :], flat(xcb)[:, sl], pi.t[:, :])], [gw.b, xcb.b])
                    act(flat(A)[:, sl], pa.t[:, :], AF.Sigmoid, [pa.b, vecs.b], pwrites=[A.b], bias=V(l, "lba", d * 8 + ch))
                    act(flat(U)[:, sl], pi.t[:, :], AF.Sigmoid, [pi.b, vecs.b], pwrites=[U.b], bias=V(l, "lbx", d * 8 + ch))
                act(flat(H), flat(A), AF.Exp, [A.b, clam2.b], writes=[H.b], scale=clam2.t[:, lo:lo + 1])
                act(flat(A), flat(A), AF.Exp, [A.b, clam.b], writes=[A.b], scale=clam.t[:, lo:lo + 1])
                act(flat(H), flat(H), AF.Identity, [H.b], writes=[H.b], bias=1.0, scale=-1.0)
                act(flat(H), flat(H), AF.Sqrt, [H.b], writes=[H.b])
                tt("dve", flat(U), flat(U), flat(xc), ALU.mult, [U.b, xc.b], writes=[U.b])
                tt("dve", flat(U), flat(U), flat(H), ALU.mult, [U.b, H.b], writes=[U.b])
                for s_ in range(ns):
                    if sample:
                        so = l * 16 + d * 8 + ch
                        init = st_sb.t[:, so:so + 1]
                        rd = [A.b, U.b, st_sb.b]
                    else:
                        init = 0.0
                        rd = [A.b, U.b]
                    if d == 0:
                        S.op("dve", lambda e, s_=s_, init=init, H=H: e.tensor_tensor_scan(
                            out=H.t[:, s_, :], data0=A.t[:, s_, :], data1=U.t[:, s_, :], initial=init, op0=ALU.mult, op1=ALU.add),
                            reads=rd, **({"writes": [H.b]} if s_ == 0 else {"pwrites": [H.b]}))
                    else:
                        rv = lambda t_, s_=s_: bass.AP(t_.t, s_ * Sq + Sq - 1, [[ns * Sq, 128], [-1, Sq]])
                        S.op("dve", lambda e, rv=rv, init=init, H=H: e.tensor_tensor_scan(
                            out=rv(H), data0=rv(A), data1=rv(U), initial=init, op0=ALU.mult, op1=ALU.add),
                            reads=rd, **({"writes": [H.b]} if s_ == 0 else {"pwrites": [H.b]}))
                if not sample:
                    o = ((l * 2 + d) * 8 + ch) * 4
                    col = Sq - 1 if d == 0 else 0
                    cp("pool", nst_sb.t[:, o:o + 4], H.t[:, :, col], [H.b], pwrites=[nst_sb.b])
            tt("dve", flat(Hf), flat(Hf), flat(Hb), ALU.add, [Hf.b, Hb.b], writes=[Hf.b])
            tt("dve", flat(yb), flat(Hf), flat(G), ALU.mult, [Hf.b, G.b], writes=[yb.b])
            S.dma("act", g.MIX[512 + ch * 128:512 + (ch + 1) * 128, :], flat(yb), reads=[yb.b], pwrites=[g.b_MIX])
        release(m)

    def phase_P3(g, l):
        m = mark()
        sample = g.nseq == 1
        Tk = g.Tk if sample else g.S
        nkb = Tk // 128
        QW = 512 if sample else 256
        nqg = g.S // QW
        kT = [alloc([128, Tk], BF16) for _ in range(2)]
        krT = alloc([64, g.Tk], BF16)
        vv = [alloc([128, nkb, 128], BF16) for _ in range(2)]
        qT = [alloc([128, QW], BF16) for _ in range(2)]
        qrT = [alloc([64, QW], BF16) for _ in range(2)]
        pT = [alloc([128, QW], BF16) for _ in range(4)]
        rz = alloc([128, QW], F32)
        ob = [alloc([128, QW], BF16) for _ in range(2)]
        S.dma("sp", krT.t[:, :], g.KR[:, :], reads=[g.b_KR], writes=[krT.b])
        n = 0
        nq = 0
        npt = 0
        for s_ in range(g.nseq):
            k0 = 0 if sample else s_ * g.S
            for h in range(8):
                mla = h >= 4
                hh = h - 4 if mla else h
                kt_, v_ = kT[n % 2], vv[n % 2]
                n += 1
                if mla:
                    S.dma("sp", kt_.t[:, :], g.KN[hh, :, k0:k0 + Tk], reads=[g.b_KN], writes=[kt_.b])
                    S.dma("sp", v_.t[:, :, :], g.VM[k0:k0 + Tk, hh * 128:(hh + 1) * 128].rearrange("(kb p) d -> p kb d", p=128),
                          reads=[g.b_VM], writes=[v_.b])
                    scale = 192 ** -0.5
                else:
                    S.dma("sp", kt_.t[:, :], g.KA[hh // 2, :, k0:k0 + Tk], reads=[g.b_KA], writes=[kt_.b])
                    S.dma("sp", v_.t[:, :, :], g.VA[k0:k0 + Tk, (hh // 2) * 128:(hh // 2 + 1) * 128].rearrange("(kb p) d -> p kb d", p=128),
                          reads=[g.b_VA], writes=[v_.b])
                    scale = 128 ** -0.5
                for qg in range(nqg):
                    q0 = s_ * g.S + qg * QW
                    q_, qr_ = qT[nq % 2], qrT[nq % 2]
                    o_ = ob[nq % 2]
                    nq += 1
                    if mla:
                        S.dma("sp", q_.t[:, :], g.QN[hh, :, q0:q0 + QW], reads=[g.b_QN], writes=[q_.b])
                        S.dma("sp", qr_.t[:, :], g.QR[hh, :, q0:q0 + QW], reads=[g.b_QR], writes=[qr_.b])
                    else:
                        S.dma("sp", q_.t[:, :], g.QA[hh, :, q0:q0 + QW], reads=[g.b_QA], writes=[q_.b])
                    pO = psum[2 + (nq % 2) * 1]
                    pZ = psum[4 + (nq % 2) * 1]

                    def s_mm(kb):
                        ps = psum[kb % 2]
                        ksl = slice(kb * 128, (kb + 1) * 128)
                        if mla:
                            kr0 = k0 + kb * 128
                            mmgroup(ps, [(kt_.t[:, ksl], q_.t[:, :], ps.t[:, 0:QW]),
                                         (krT.t[:, kr0:kr0 + 128], qr_.t[:, :], ps.t[:, 0:QW])], [kt_.b, q_.b, krT.b, qr_.b])
                        else:
                            mmgroup(ps, [(kt_.t[:, ksl], q_.t[:, :], ps.t[:, 0:QW])], [kt_.b, q_.b])
                        return ps
                    pend = s_mm(0)
                    for kb in range(nkb):
                        ps = pend
                        if kb + 1 < nkb:
                            pend = s_mm(kb + 1)
                        p_ = pT[npt % 4]
                        npt += 1
                        act(p_.t[:, :], ps.t[:, 0:QW], AF.Exp, [ps.b], writes=[p_.b], scale=scale)
                        if kb == 0:
                            mm(pO.t[:, 0:QW], v_.t[:, kb, :], p_.t[:, :], True, nkb == 1, [v_.b, p_.b], writes=[pO.b])
                            mm(pZ.t[:, 0:QW], ones_bf.t[:, :], p_.t[:, :], True, nkb == 1, [ones_bf.b, p_.b], writes=[pZ.b])
                        else:
                            mm(pO.t[:, 0:QW], v_.t[:, kb, :], p_.t[:, :], False, kb == nkb - 1, [v_.b, p_.b], pwrites=[pO.b])
                            mm(pZ.t[:, 0:QW], ones_bf.t[:, :], p_.t[:, :], False, kb == nkb - 1, [ones_bf.b, p_.b], pwrites=[pZ.b])
                    S.op("dve", lambda e, pZ=pZ: e.reciprocal(out=rz.t[:, :], in_=pZ.t[:, 0:QW]), reads=[pZ.b], writes=[rz.b])
                    tt("dve", o_.t[:, :], pO.t[:, 0:QW], rz.t[:, :], ALU.mult, [pO.b, rz.b], writes=[o_.b])
                    row = (1536 + hh * 128) if mla else hh * 128
                    S.dma("act", g.MIX[row:row + 128, q0:q0 + QW], o_.t[:, :], reads=[o_.b], pwrites=[g.b_MIX])
        release(m)

    def proj_to_M(g, l, blk, src_scr, b_src, nk, wsrc, kstage, pcols):
        b0, Tb = blk
        m = mark()
        aT = alloc([128, nk, Tb], BF16)
        aTb = [Buf() for _ in range(Tb // 512)]
        ps_w = PanelStream(nk, pcols, kstage)
        ev = [alloc([128, 512], F32) for _ in range(3)]
        ne = 0
        for pnl, pan in ps_w.stream([(wsrc[:, q_ * pcols:(q_ + 1) * pcols], pcols) for q_ in range(D // pcols)]):
            if pnl == 0:
                for ti in range(Tb // 512):
                    for k0_ in range(0, nk, 16):
                        kn_ = min(16, nk - k0_)
                        S.dma("sp", aT.t[:, k0_:k0_ + kn_, ti * 512:(ti + 1) * 512],
                              src_scr[k0_ * 128:(k0_ + kn_) * 128, b0 + ti * 512:b0 + (ti + 1) * 512].rearrange("(kc p) t -> p kc t", p=128),
                              reads=[b_src], **({"writes": [aTb[ti]]} if k0_ == 0 else {"pwrites": [aTb[ti]]}))
            for ti in range(Tb // 512):
                c0 = b0 + ti * 512
                for c in range(pcols // 128):
                    mc = pnl * (pcols // 128) + c
                    ps = acc_ps()
                    mmgroup(ps, [(pan.t[:, kc, c * 128:(c + 1) * 128], aT.t[:, kc, ti * 512:(ti + 1) * 512], ps.t[:, :])
                                 for kc in range(nk)], [aTb[ti], pan.b])
                    e_ = ev[ne % 3]
                    ne += 1
                    act(e_.t[:, :], ps.t[:, :], AF.Copy, [ps.b], writes=[e_.b])
                    S.dma("act", g.M[mc * 128:(mc + 1) * 128, c0:c0 + 512], e_.t[:, :], reads=[e_.b], pwrites=[g.b_M])
        release(m)

    def phase_update(g, l, which_g, next_norm):
        m = mark()
        xts = [alloc([128, KC, 512], F32) for _ in range(2)]
        mts = [alloc([128, KC, 512], F32) for _ in range(2)]
        tmp = [alloc([128, 512], F32) for _ in range(3)]
        sq = alloc([128, KC, 512], BF16)
        rstd = alloc([128, 512], F32)
        rstd2 = alloc([128, 512], F32)
        hst = [alloc([128, KC, 512], BF16) for _ in range(1)]
        for ti in range(g.T // 512):
            xt, mt = xts[ti % 2], mts[ti % 2]
            sl = slice(ti * 512, (ti + 1) * 512)
            S.dma("sp", xt.t[:, :, :], g.xT[:, sl].rearrange("(kc p) t -> p kc t", p=128), reads=[g.b_xT], writes=[xt.b])
            S.dma("sp", mt.t[:, :, :], g.M[:, sl].rearrange("(kc p) t -> p kc t", p=128), reads=[g.b_M], writes=[mt.b])
            act(sq.t[:, :, :], mt.t[:, :, :], AF.Square, [mt.b], writes=[sq.b])
            pz = aux_ps()
            mmgroup(pz, [(ones_bf.t[:, :], sq.t[:, kc, :], pz.t[:, :]) for kc in range(KC)], [ones_bf.b, sq.b])
            rsqrt_from(rstd.t[:, :], pz.t[:, :], [pz.b], D, rstd.b)
            for kc in range(KC):
                tm = tmp[kc % 3]
                stt("dve", tm.t[:, :], mt.t[:, kc, :], DV(l, g.ci, which_g, kc), rstd.t[:, :], ALU.mult, ALU.mult,
                    [mt.b, rstd.b, dv.b], writes=[tm.b])
                tt("pool" if kc % 4 == 3 else "dve", xt.t[:, kc, :], xt.t[:, kc, :], tm.t[:, :], ALU.add, [xt.b, tm.b], writes=[xt.b])
            if next_norm is None:
                S.dma("act", g.y_out[:, sl].rearrange("(kc p) t -> p kc t", p=128), xt.t[:, :, :], reads=[xt.b])
            else:
                S.dma("act", g.xT[:, sl].rearrange("(kc p) t -> p kc t", p=128), xt.t[:, :, :], reads=[xt.b], pwrites=[g.b_xT])
                prenorm_tile(g, next_norm[0], next_norm[1], xt, ti * 512, tmp, sq, rstd2, hst[0])
        release(m)

    def phase_P6(g, l, blk):
        b0, Tb = blk
        m = mark()
        sample = g.nseq == 1
        ns = 1 if sample else g.nseq
        Sq = Tb if sample else g.S
        W = Sq + 2
        hT = alloc([128, KC, Tb], BF16)
        halo = alloc([128, KC, 2], BF16)
        hTb = [Buf() for _ in range(Tb // 512)]
        S.op("pool", lambda e: e.memset(halo.t[:, :, :], 0.0), writes=[halo.b])
        has_halo = sample
        if sample:
            Hv = g.H.rearrange("(kc p) t -> p kc t", p=128)
            if b0 > 0:
                S.dma("sp", halo.t[:, :, 0:1], Hv[:, :, b0 - 1:b0], reads=[g.b_H, halo.b], writes=[halo.b], slow=True)
            if b0 + Tb < g.T:
                S.dma("sp", halo.t[:, :, 1:2], Hv[:, :, b0 + Tb:b0 + Tb + 1], reads=[g.b_H, halo.b], writes=[halo.b], slow=True)
        ps_w = PanelStream(KC, 512, 4)
        upad = [alloc([128, ns, W], F32) for _ in range(2)]
        cc = [alloc([128, ns, Sq], F32) for _ in range(2)]
        Gs = alloc([128, 4, Tb], BF16)
        ab = [alloc([128, Tb], BF16) for _ in range(2)]
        for u_ in upad:
            S.op("pool", lambda e, u_=u_: e.memset(u_.t[:, :, :], 0.0), writes=[u_.b])
        S.barrier()
        nu = 0
        def colbase_of(pp_):
            return (DFF if pp_ % 2 == 1 else 0) + (pp_ // 2) * 512
        for pp, pan in ps_w.stream([(w_up[l, :, colbase_of(q_):colbase_of(q_) + 512], 512) for q_ in range(22)]):
            if pp == 0:
                for ti in range(Tb // 512):
                    S.dma("sp", hT.t[:, :, ti * 512:(ti + 1) * 512],
                          g.H[:, b0 + ti * 512:b0 + (ti + 1) * 512].rearrange("(kc p) t -> p kc t", p=128),
                          reads=[g.b_H], writes=[hTb[ti]])
            isv = pp % 2 == 1
            cp4 = pp // 2
            for c in range(4):
                chn = cp4 * 4 + c + (NFC if isv else 0)
                u_ = upad[nu % 2]
                c_ = cc[nu % 2]
                a_ = ab[nu % 2]
                nu += 1
                for ti in range(Tb // 512):
                    ps = acc_ps()
                    mmgroup(ps, [(pan.t[:, kc, c * 128:(c + 1) * 128], hT.t[:, kc, ti * 512:(ti + 1) * 512], ps.t[:, :])
                                 for kc in range(KC)], [hTb[ti], pan.b])
                    if sample:
                        dstv = u_.t[:, 0, 1 + ti * 512:1 + (ti + 1) * 512]
                        srcv = ps.t[:, :]
                    else:
                        dstv = u_.t[:, 2 * ti:2 * ti + 2, 1:1 + Sq]
                        srcv = ps.t[:, :].rearrange("p (s t) -> p s t", s=2)
                    act(dstv, srcv, AF.Copy, [ps.b], pwrites=[u_.b])
                if has_halo:
                    ph = aux_ps()
                    mmgroup(ph, [(pan.t[:, kc, c * 128:(c + 1) * 128], halo.t[:, kc, :], ph.t[:, 0:2]) for kc in range(KC)],
                            [halo.b, pan.b])
                    act(u_.t[:, 0, 0:1], ph.t[:, 0:1], AF.Copy, [ph.b], pwrites=[u_.b])
                    act(u_.t[:, 0, W - 1:W], ph.t[:, 1:2], AF.Copy, [ph.b], pwrites=[u_.b])
                ts("dve", c_.t[:, :, :], u_.t[:, :, 0:Sq], V(l, "fcw", 0 * 88 + chn), ALU.mult, [u_.b, vecs.b], writes=[c_.b])
                stt("dve", c_.t[:, :, :], u_.t[:, :, 1:1 + Sq], V(l, "fcw", 1 * 88 + chn), c_.t[:, :, :], ALU.mult, ALU.add,
                    [u_.b, c_.b, vecs.b], writes=[c_.b])
                stt("dve", c_.t[:, :, :], u_.t[:, :, 2:2 + Sq], V(l, "fcw", 2 * 88 + chn), c_.t[:, :, :], ALU.mult, ALU.add,
                    [u_.b, c_.b, vecs.b], writes=[c_.b])
                cflat = c_.t[:, :, :].rearrange("p s t -> p (s t)")
                if not isv:
                    act(Gs.t[:, c, :], cflat, AF.Silu, [c_.b, vecs.b], bias=V(l, "fcb", chn),
                        **({"writes": [Gs.b]} if c == 0 else {"pwrites": [Gs.b]}))
                else:
                    act(cflat, cflat, AF.Identity, [c_.b, vecs.b], writes=[c_.b], bias=V(l, "fcb", chn))
                    tt("pool", a_.t[:, :], cflat, Gs.t[:, c, :], ALU.mult, [c_.b, Gs.b], writes=[a_.b])
                    r0 = (cp4 * 4 + c) * 128
                    S.dma("act", g.ACT[r0:r0 + 128, b0:b0 + Tb], a_.t[:, :], reads=[a_.b], pwrites=[g.b_ACT])
        release(m)

    import os
    PH = {"n": 0, "max": int(os.environ.get("MK_STOP", "100000"))}

    def ph(fn, *a):
        if PH["n"] < PH["max"]:
            fn(*a)
        PH["n"] += 1

    def run_group(g):
        ph(phase_P1, g, 0)
        for l in range(L):
            for blk in g.blocks:
                ph(phase_P2, g, l, blk)
            if g.nseq == 1:
                ph(phase_P2C, g, l)
            ph(phase_P2M, g, l)
            ph(phase_P2L, g, l)
            ph(phase_P3, g, l)
            for blk in g.blocks:
                ph(proj_to_M, g, l, blk, g.MIX, g.b_MIX, KC, w_out[l], 4, 512)
            ph(phase_update, g, l, 2, (l, 3))
            for blk in g.blocks:
                ph(phase_P6, g, l, blk)
            for s0 in range(0, g.T, 1024):
                ph(proj_to_M, g, l, (s0, 1024), g.ACT, g.b_ACT, NFC, w_down[l], 11, 256)
            ph(phase_update, g, l, 5, (l + 1, 0) if l + 1 < L else None)

    which = os.environ.get("MK_GROUPS", "ps")
    if "p" in which:
        run_group(GP)
        S.dma("act", nst_out, nst_sb.t[:, :], reads=[nst_sb.b])
    if "s" in which:
        run_group(GS)
    S.emit()
    return nc


def _fm(v):
    v = np.asarray(v, np.float32)
    lead = int(np.prod(v.shape[:-1])) if v.ndim > 1 else 1
    return np.ascontiguousarray(v.reshape(lead, -1, 128).transpose(2, 0, 1).reshape(128, -1))


def _rope_tables(n_tok, width):
    half = width // 2
    quarter = half // 2
    inv = (10000.0 ** (-np.arange(quarter, dtype=np.float32) / quarter)).astype(np.float32)
    t = np.arange(n_tok)
    row = (t // 64).astype(np.float32)
    col = (t % 64).astype(np.float32)
    tab = np.zeros((width, 2, n_tok), np.float32)
    for part, pos in ((0, row), (1, col)):
        ang = pos[None, :] * inv[:, None]
        c, s = np.cos(ang).astype(np.float32), np.sin(ang).astype(np.float32)
        base = part * half
        tab[base:base + quarter, 0] = c
        tab[base + quarter:base + half, 0] = c
        tab[base:base + quarter, 1] = -s
        tab[base + quarter:base + half, 1] = s
    return tab


_NC_CACHE = {}


def kernel(x_prompt, x_sample, cache_attn_k, cache_attn_v, cache_mla_ckv, cache_mla_krope, state_lru,
           c, c_ctx, w_mod, b_mod, g_pre_mix, g_post_mix, g_pre_ffn, g_post_ffn, w_in, g_q, g_k,
           lru_conv_w, lru_conv_b, lru_w_a, lru_b_a, lru_w_x, lru_b_x, lru_lambda, g_kv, w_uk, w_uv,
           w_out, ffn_w_up, ffn_conv_w, ffn_conv_b, ffn_w_down):
    f = lambda a: np.ascontiguousarray(np.asarray(a, dtype=np.float32))
    if "nc" not in _NC_CACHE:
        _NC_CACHE["nc"] = build_program()
    nc = _NC_CACHE["nc"]
    vecs = np.zeros((128, L * NVL), np.float32)
    for l in range(L):
        o = l * NVL
        def put(name, arr):
            a = _fm(arr)
            vecs[:, o + VO[name]:o + VO[name] + a.shape[1]] = a
        put("bmod", b_mod[l]); put("gpm", g_pre_mix[l]); put("gpom", g_post_mix[l]); put("gpf", g_pre_ffn[l])
        put("gpof", g_post_ffn[l]); put("gq", g_q[l]); put("gk", g_k[l]); put("lcw", lru_conv_w[l]); put("lcb", lru_conv_b[l])
        put("lba", lru_b_a[l]); put("lbx", lru_b_x[l]); put("llam", lru_lambda[l]); put("gkv", g_kv[l])
        put("fcw", ffn_conv_w[l]); put("fcb", ffn_conv_b[l])
    rmat = np.zeros((128, 256), np.float32)
    for d in range(128):
        rmat[d ^ 32, d] = 1.0
    for d in range(64):
        rmat[d ^ 16, 128 + d] = 1.0
    rope128 = _rope_tables(SS_T, 128)
    rope64 = _rope_tables(SS_T, 64)
    shared = dict(vecs=vecs, rmat=rmat, rope128=rope128, rope64=rope64,
                  w_mod=f(w_mod), w_in=f(w_in), lru_w_a=f(lru_w_a), lru_w_x=f(lru_w_x),
                  w_uk=f(w_uk).reshape(L, 512, 512), w_uv=f(w_uv).reshape(L, 512, 512), w_out=f(w_out),
                  ffn_w_up=f(ffn_w_up), ffn_w_down=f(ffn_w_down))
    x_prompt = np.asarray(x_prompt, np.float32)
    x_sample = np.asarray(x_sample, np.float32)
    in_maps = []
    for i in range(8):
        b = i // 2
        m = dict(shared)
        m["xp"] = np.ascontiguousarray(x_prompt[4 * i:4 * i + 4].reshape(SP_T, D).T)
        m["xs"] = np.ascontiguousarray(x_sample[b].T)
        m["ck"] = np.ascontiguousarray(np.asarray(cache_attn_k[b], np.float32).transpose(0, 2, 3, 1))
        m["cv"] = np.ascontiguousarray(np.asarray(cache_attn_v[b], np.float32).reshape(L, PAST, 256))
        m["cckv"] = np.ascontiguousarray(np.asarray(cache_mla_ckv[b], np.float32).transpose(0, 2, 1))
        m["ckr"] = np.ascontiguousarray(np.asarray(cache_mla_krope[b], np.float32).transpose(0, 2, 1))
        m["st"] = _fm(np.asarray(state_lru[b], np.float32).reshape(L * 2, 1024))
        cond = np.stack([np.asarray(c_ctx, np.float32), np.asarray(c[b], np.float32)], axis=-1)
        m["cond"] = np.ascontiguousarray(cond.reshape(KC, 128, 2).transpose(1, 0, 2).reshape(128, KC * 2))
        in_maps.append(m)
    import os
    ncores = int(os.environ.get("MK_NCORES", "8"))
    res = run_bass_kernel_spmd(nc, in_maps[:ncores], core_ids=list(range(ncores)))
    R = list(res.results)
    while len(R) < 8:
        R.append({k: np.zeros_like(v) for k, v in R[0].items()})
    y_prompt = np.empty((32, 256, D), np.float32)
    y_sample = np.empty((4, SS_T, D), np.float32)
    nk = np.empty((32, L, 256, 2, 128), np.float32)
    nv = np.empty((32, L, 256, 2, 128), np.float32)
    nckv = np.empty((32, L, 256, 512), np.float32)
    nkr = np.empty((32, L, 256, 64), np.float32)
    nst = np.empty((32, L, 2, 1024), np.float32)
    for i in range(8):
        r = R[i]
        y_prompt[4 * i:4 * i + 4] = r["yp"].T.reshape(4, 256, D)
        if i % 2 == 0:
            y_sample[i // 2] = r["ys"].T
        nk[4 * i:4 * i + 4] = r["nk"].reshape(L, 2, 128, 4, 256).transpose(3, 0, 4, 1, 2)
        nv[4 * i:4 * i + 4] = r["nv"].reshape(L, 4, 256, 2, 128).transpose(1, 0, 2, 3, 4)
        nckv[4 * i:4 * i + 4] = r["nckv"].reshape(L, 512, 4, 256).transpose(2, 0, 3, 1)
        nkr[4 * i:4 * i + 4] = r["nkr"].reshape(L, 64, 4, 256).transpose(2, 0, 3, 1)
        nst[4 * i:4 * i + 4] = r["nst"].reshape(128, L, 2, 8, 4).transpose(4, 1, 2, 3, 0).reshape(4, L, 2, 1024)
    return (y_prompt, y_sample, nk, nv, nckv, nkr, nst)
```
